# Optimizing a Trainium2 kernel written in Bass

```python
import math
import jax, jax.numpy as jnp
from jax import lax
import numpy as np

D_MODEL = 1024
BATCH = 8
SEQ = 4096
DEPTH = 4

N_EVEN = (DEPTH + 1) // 2
N_ODD = DEPTH // 2
CONV_WIDTH = 4
NORM_EPS = 1e-6
LRU_WIDTH = D_MODEL
LRU_BLOCKS = 16
LRU_BLOCK = LRU_WIDTH // LRU_BLOCKS
LRU_C = 8.0
SSD_WIDTH = D_MODEL
SSD_HEAD_DIM = 64
SSD_HEADS = SSD_WIDTH // SSD_HEAD_DIM
SSD_GROUPS = 2
SSD_STATE = 128
SSD_CHUNK = 128
SSD_CONV_CH = SSD_WIDTH + 2 * SSD_GROUPS * SSD_STATE
REC_IN = 2 * LRU_WIDTH + SSD_WIDTH + SSD_CONV_CH + SSD_HEADS
REC_OUT = LRU_WIDTH + SSD_WIDTH
ATT_HEADS = 16
ATT_HEAD_DIM = D_MODEL // ATT_HEADS
ATT_WIDTH = ATT_HEADS * ATT_HEAD_DIM
ROPE_DIM = ATT_HEAD_DIM // 4
ROPE_THETA = 500000.0
DILATED_PATTERNS = ((128, 1), (512, 4), (2048, 16))
FFN_HIDDEN = -(-8 * D_MODEL // (3 * 256)) * 256

kernel_name = 'hybrid_rglru_ssd_dilated_attn_trunk'


def rmsnorm(x, g):
    x32 = x.astype(jnp.float32)
    y = x32 * lax.rsqrt(jnp.mean(x32 * x32, axis=-1, keepdims=True) + NORM_EPS)
    return (y * g.astype(jnp.float32)).astype(x.dtype)


def causal_dwconv(x, w, b):
    k = w.shape[0]
    l = x.shape[1]
    xp = jnp.pad(x, ((0, 0), (k - 1, 0), (0, 0)))
    y = b
    for j in range(k):
        y = y + xp[:, j:j + l] * w[j]
    return y


def rg_lru(xc, w_r, b_r, w_i, b_i, lam):
    b, l, w = xc.shape
    xb = xc.reshape(b, l, LRU_BLOCKS, LRU_BLOCK)
    r = jax.nn.sigmoid(jnp.einsum('blhi,hij->blhj', xb, w_r).reshape(b, l, w) + b_r)
    i = jax.nn.sigmoid(jnp.einsum('blhi,hij->blhj', xb, w_i).reshape(b, l, w) + b_i)
    log_a = -LRU_C * r * jax.nn.softplus(-lam)
    a = jnp.exp(log_a)
    u = jnp.sqrt(-jnp.expm1(2.0 * log_a)) * (i * xc)

    def combine(e1, e2):
        a1, h1 = e1
        a2, h2 = e2
        return a1 * a2, a2 * h1 + h2

    _, h = lax.associative_scan(combine, (a, u), axis=1)
    return h


def ssd_chunked(xs, dt, a, bm, cm):
    b, l, nh, p = xs.shape
    g, n = bm.shape[2], bm.shape[3]
    r = nh // g
    c = l // SSD_CHUNK
    t = SSD_CHUNK
    xdt = (xs * dt[..., None]).reshape(b, c, t, g, r, p)
    adt = (dt * a).reshape(b, c, t, g, r)
    bc = bm.reshape(b, c, t, g, n)
    cc = cm.reshape(b, c, t, g, n)
    cs = jnp.cumsum(adt, axis=2)
    seg = cs[:, :, :, None] - cs[:, :, None, :]
    causal = (jnp.arange(t)[:, None] >= jnp.arange(t)[None, :])[:, :, None, None]
    decay = jnp.exp(jnp.where(causal, seg, -jnp.inf))
    cb = jnp.einsum('bclgn,bcsgn->bclsg', cc, bc)
    y_diag = jnp.einsum('bclsgr,bcsgrp->bclgrp', cb[..., None] * decay, xdt)
    decay_to_end = jnp.exp(cs[:, :, -1:] - cs)
    states = jnp.einsum('bcsgn,bcsgrp->bcgrpn', bc, xdt * decay_to_end[..., None])
    chunk_decay = jnp.exp(cs[:, :, -1])

    def step(carry, inp):
        s_c, d_c = inp
        return carry * d_c[..., None, None] + s_c, carry

    init = jnp.zeros((b, g, r, p, n), jnp.float32)
    _, prev = lax.scan(step, init, (jnp.moveaxis(states, 1, 0), jnp.moveaxis(chunk_decay, 1, 0)))
    prev = jnp.moveaxis(prev, 0, 1)
    y_off = jnp.einsum('bclgn,bcgrpn->bclgrp', cc, prev) * jnp.exp(cs)[..., None]
    return (y_diag + y_off).reshape(b, l, nh, p)


def recurrent_layer(x, norm_g, w_in, lru_conv_w, lru_conv_b, lru_w_r, lru_b_r, lru_w_i, lru_b_i,
                    lru_lambda, ssd_conv_w, ssd_conv_b, ssd_dt_bias, ssd_a_log, ssd_d, ssd_norm, w_out):
    f32 = jnp.float32
    b, l, _ = x.shape
    h = rmsnorm(x, norm_g)
    proj = h @ w_in
    s1 = LRU_WIDTH
    s2 = 2 * LRU_WIDTH
    s3 = s2 + SSD_WIDTH
    s4 = s3 + SSD_CONV_CH
    lru_x, lru_gate, z, xbc, dt_raw = jnp.split(proj, [s1, s2, s3, s4], axis=-1)
    xc = causal_dwconv(lru_x, lru_conv_w, lru_conv_b).astype(f32)
    h_lru = rg_lru(xc, lru_w_r.astype(f32), lru_b_r.astype(f32), lru_w_i.astype(f32),
                   lru_b_i.astype(f32), lru_lambda.astype(f32))
    out_a = h_lru * jax.nn.gelu(lru_gate.astype(f32))
    xbc = jax.nn.silu(causal_dwconv(xbc, ssd_conv_w, ssd_conv_b).astype(f32))
    xs, bm, cm = jnp.split(xbc, [SSD_WIDTH, SSD_WIDTH + SSD_GROUPS * SSD_STATE], axis=-1)
    xs = xs.reshape(b, l, SSD_HEADS, SSD_HEAD_DIM)
    bm = bm.reshape(b, l, SSD_GROUPS, SSD_STATE)
    cm = cm.reshape(b, l, SSD_GROUPS, SSD_STATE)
    dt = jax.nn.softplus(dt_raw.astype(f32) + ssd_dt_bias.astype(f32))
    a = -jnp.exp(ssd_a_log.astype(f32))
    y = ssd_chunked(xs, dt, a, bm, cm) + ssd_d.astype(f32)[:, None] * xs
    y = y.reshape(b, l, SSD_WIDTH) * jax.nn.silu(z.astype(f32))
    y = y.reshape(b, l, SSD_GROUPS, SSD_WIDTH // SSD_GROUPS)
    y = y * lax.rsqrt(jnp.mean(y * y, axis=-1, keepdims=True) + NORM_EPS)
    out_b = y.reshape(b, l, SSD_WIDTH) * ssd_norm.astype(f32)
    mixed = jnp.concatenate([out_a, out_b], axis=-1).astype(x.dtype)
    return x + mixed @ w_out


def partial_rope(x, pos):
    half = ROPE_DIM // 2
    inv = ROPE_THETA ** (-2.0 * jnp.arange(half, dtype=jnp.float32) / ROPE_DIM)
    ang = pos[:, None] * inv[None, :]
    cos = jnp.cos(ang)[None, :, None, :]
    sin = jnp.sin(ang)[None, :, None, :]
    x1 = x[..., :half].astype(jnp.float32)
    x2 = x[..., half:ROPE_DIM].astype(jnp.float32)
    rot = jnp.concatenate([x1 * cos - x2 * sin, x2 * cos + x1 * sin], axis=-1)
    return jnp.concatenate([rot.astype(x.dtype), x[..., ROPE_DIM:]], axis=-1)


def dilated_window_attention(q, k, v, window, dilation):
    b, l, h, e = q.shape
    span = window // dilation
    u = l // dilation
    nblk = -(-u // span)
    up = nblk * span

    def to_strided(t):
        t = t.reshape(b, u, dilation, h, e).transpose(0, 2, 1, 3, 4).reshape(b * dilation, u, h, e)
        t = jnp.pad(t, ((0, 0), (0, up - u), (0, 0), (0, 0)))
        return t.reshape(b * dilation, nblk, span, h, e)

    def with_prev(t):
        prev = jnp.pad(t[:, :-1], ((0, 0), (1, 0), (0, 0), (0, 0), (0, 0)))
        return jnp.concatenate([prev, t], axis=2)

    def from_strided(t):
        tail = t.shape[3:]
        t = t.reshape((b, dilation, up) + tail)[:, :, :u]
        return jnp.swapaxes(t, 1, 2).reshape((b, l) + tail)

    qb = to_strided(q)
    kc = with_prev(to_strided(k))
    vc = with_prev(to_strided(v))
    s = jnp.einsum('bnqhe,bnkhe->bnhqk', qb, kc).astype(jnp.float32)
    qi = jnp.arange(span)[:, None]
    kj = jnp.arange(2 * span)[None, :]
    band = (kj >= qi) & (kj <= qi + span)
    valid = band[None] & ((jnp.arange(nblk)[:, None, None] > 0) | (kj[None] >= span))
    s = jnp.where(valid[None, :, None], s, -jnp.inf)
    m = jnp.max(s, axis=-1)
    p = jnp.exp(s - m[..., None])
    den = jnp.sum(p, axis=-1)
    o = jnp.einsum('bnhqk,bnkhe->bnqhe', p, vc.astype(jnp.float32))
    den_q = jnp.swapaxes(den, 2, 3)
    o = o / den_q[..., None]
    return from_strided(o), from_strided(jnp.swapaxes(m, 2, 3)), from_strided(den_q)


def attention_layer(x, norm_g, w_qkv, q_norm, k_norm, w_out):
    b, l, _ = x.shape
    h = rmsnorm(x, norm_g)
    qkv = (h @ w_qkv).reshape(b, l, 3, ATT_HEADS, ATT_HEAD_DIM)
    pos = jnp.arange(l, dtype=jnp.float32)
    q = partial_rope(rmsnorm(qkv[:, :, 0], q_norm), pos) * (ATT_HEAD_DIM ** -0.5)
    k = partial_rope(rmsnorm(qkv[:, :, 1], k_norm), pos)
    v = qkv[:, :, 2]
    outs, maxes, dens = [], [], []
    for window, dilation in DILATED_PATTERNS:
        o_i, m_i, d_i = dilated_window_attention(q, k, v, window, dilation)
        outs.append(o_i)
        maxes.append(m_i)
        dens.append(d_i)
    m_all = jnp.stack(maxes)
    wts = jnp.stack(dens) * jnp.exp(m_all - jnp.max(m_all, axis=0, keepdims=True))
    o = jnp.sum(wts[..., None] * jnp.stack(outs), axis=0) / jnp.sum(wts, axis=0)[..., None]
    o = o.astype(x.dtype).reshape(b, l, ATT_WIDTH)
    return x + o @ w_out


def swiglu_layer(x, norm_g, w_gate_up, w_down):
    h = rmsnorm(x, norm_g)
    g, u = jnp.split(h @ w_gate_up, 2, axis=-1)
    return x + (jax.nn.silu(g) * u) @ w_down


def setup_inputs(seed: int = 0) -> dict:
    key = jax.random.key(seed)
    ks = iter(jax.random.split(key, 32))
    f32 = jnp.float32
    ne, no = N_EVEN, N_ODD

    def nrm(shape, scale):
        return jax.random.normal(next(ks), shape, f32) * scale

    def gain(shape):
        return 1.0 + 0.02 * jax.random.normal(next(ks), shape, f32)

    lam_base = jax.random.uniform(next(ks), (ne, LRU_WIDTH), f32, 0.9, 0.999)
    sig = lam_base ** (1.0 / LRU_C)
    lru_lambda = jnp.log(sig) - jnp.log1p(-sig)
    dt0 = jnp.exp(jax.random.uniform(next(ks), (ne, SSD_HEADS), f32, math.log(1e-3), math.log(1e-1)))
    ssd_dt_bias = dt0 + jnp.log(-jnp.expm1(-dt0))
    ssd_a_log = jnp.log(jax.random.uniform(next(ks), (ne, SSD_HEADS), f32, 1.0, 16.0))
    inputs = {}
    inputs['x'] = nrm((BATCH, SEQ, D_MODEL), 1.0)
    inputs['rec_norm'] = gain((ne, D_MODEL))
    inputs['rec_w_in'] = nrm((ne, D_MODEL, REC_IN), D_MODEL ** -0.5)
    inputs['lru_conv_w'] = nrm((ne, CONV_WIDTH, LRU_WIDTH), CONV_WIDTH ** -0.5)
    inputs['lru_conv_b'] = nrm((ne, LRU_WIDTH), 0.02)
    inputs['lru_w_r'] = nrm((ne, LRU_BLOCKS, LRU_BLOCK, LRU_BLOCK), LRU_BLOCK ** -0.5)
    inputs['lru_b_r'] = nrm((ne, LRU_WIDTH), 0.02)
    inputs['lru_w_i'] = nrm((ne, LRU_BLOCKS, LRU_BLOCK, LRU_BLOCK), LRU_BLOCK ** -0.5)
    inputs['lru_b_i'] = nrm((ne, LRU_WIDTH), 0.02)
    inputs['lru_lambda'] = lru_lambda
    inputs['ssd_conv_w'] = nrm((ne, CONV_WIDTH, SSD_CONV_CH), CONV_WIDTH ** -0.5)
    inputs['ssd_conv_b'] = nrm((ne, SSD_CONV_CH), 0.02)
    inputs['ssd_dt_bias'] = ssd_dt_bias
    inputs['ssd_a_log'] = ssd_a_log
    inputs['ssd_d'] = 1.0 + 0.1 * jax.random.normal(next(ks), (ne, SSD_HEADS), f32)
    inputs['ssd_norm'] = gain((ne, SSD_WIDTH))
    inputs['rec_w_out'] = nrm((ne, REC_OUT, D_MODEL), REC_OUT ** -0.5)
    inputs['att_norm'] = gain((no, D_MODEL))
    inputs['att_w_qkv'] = nrm((no, D_MODEL, 3 * ATT_WIDTH), D_MODEL ** -0.5)
    inputs['att_q_norm'] = gain((no, ATT_HEAD_DIM))
    inputs['att_k_norm'] = gain((no, ATT_HEAD_DIM))
    inputs['att_w_out'] = nrm((no, ATT_WIDTH, D_MODEL), ATT_WIDTH ** -0.5)
    inputs['ffn_norm'] = gain((DEPTH, D_MODEL))
    inputs['ffn_w_gate_up'] = nrm((DEPTH, D_MODEL, 2 * FFN_HIDDEN), D_MODEL ** -0.5)
    inputs['ffn_w_down'] = nrm((DEPTH, FFN_HIDDEN, D_MODEL), FFN_HIDDEN ** -0.5)
    return inputs


def reference(x, rec_norm, rec_w_in, lru_conv_w, lru_conv_b, lru_w_r, lru_b_r, lru_w_i, lru_b_i,
              lru_lambda, ssd_conv_w, ssd_conv_b, ssd_dt_bias, ssd_a_log, ssd_d, ssd_norm, rec_w_out,
              att_norm, att_w_qkv, att_q_norm, att_k_norm, att_w_out,
              ffn_norm, ffn_w_gate_up, ffn_w_down):
    for layer in range(DEPTH):
        i = layer // 2
        if layer % 2 == 0:
            x = recurrent_layer(x, rec_norm[i], rec_w_in[i], lru_conv_w[i], lru_conv_b[i],
                                lru_w_r[i], lru_b_r[i], lru_w_i[i], lru_b_i[i], lru_lambda[i],
                                ssd_conv_w[i], ssd_conv_b[i], ssd_dt_bias[i], ssd_a_log[i],
                                ssd_d[i], ssd_norm[i], rec_w_out[i])
        else:
            x = attention_layer(x, att_norm[i], att_w_qkv[i], att_q_norm[i], att_k_norm[i],
                                att_w_out[i])
        x = swiglu_layer(x, ffn_norm[layer], ffn_w_gate_up[layer], ffn_w_down[layer])
    return x
```

```python
import os
import numpy as np
import ml_dtypes
import concourse.bass as bass
import concourse.mybir as mybir
from concourse.bass_utils import run_bass_kernel_spmd

F32 = mybir.dt.float32
BF16 = mybir.dt.bfloat16
AF = mybir.ActivationFunctionType
ALU = mybir.AluOpType

ENG_ATTR = {'pe': 'tensor', 'act': 'scalar', 'dve': 'vector', 'pool': 'gpsimd', 'sp': 'sync'}
SEM_LIMIT = 30000
NPOOL = 12

D = 1024
L = 4096
NT = 8
TT = 512
FH = 2816
NJ = FH // 128
EPS = 1e-6


class Op:
    __slots__ = ('eng', 'fn', 'deps', 'idx', 'dma', 'sem', 'cnt', 'snap', 'sig', 'sigval',
                 'waits', 'key')


class Tile:
    __slots__ = ('ap', 'w', 'r', 'name', 'ro', 'span')

    def __init__(self, ap, name='', ro=False):
        self.ap = ap
        self.w = None
        self.r = []
        self.name = name
        self.ro = ro
        self.span = None


class KB:
    def __init__(self, nc):
        self.nc = nc
        self.ops = []
        self.nidx = {e: 0 for e in ENG_ATTR}
        self.arena = nc.alloc_sbuf_tensor("arena", [128, 212000 // 4], F32)
        self.sb_ptr = 0
        self.dbg_tiles = []
        self.regions = []
        self.psum = [Tile(nc.alloc_psum_tensor(f"psb{i}", [128, 512], F32), f"ps{i}")
                     for i in range(8)]

    def sb_set(self, ptr):
        self.sb_ptr = ptr

    def alloc(self, name, free_shape, dtype):
        esz = 2 if dtype == BF16 else 4
        n = 1
        for s in free_shape:
            n *= s
        nbytes = (n * esz + 31) // 32 * 32
        start = self.sb_ptr
        end = start + nbytes
        assert end <= 212000, f"SBUF overflow allocating {name}: {end}"
        self.sb_ptr = end
        ap = self.arena[:, start // 4:end // 4]
        if dtype == BF16:
            ap = ap.bitcast(BF16)
        ap = ap[:, 0:n]
        if len(free_shape) == 2:
            ap = ap.rearrange("p (a b) -> p a b", a=free_shape[0])
        elif len(free_shape) == 3:
            ap = ap.rearrange("p (a b c) -> p a b c", a=free_shape[0], b=free_shape[1])
        t = Tile(ap, name)
        t.span = (start, end)
        keep = []
        for (s, e, old) in self.regions:
            if s < end and start < e:
                if old.w is not None:
                    t.r.append(old.w)
                t.r.extend(old.r)
                if s < start:
                    keep.append((s, start, old))
                if end < e:
                    keep.append((end, e, old))
            else:
                keep.append((s, e, old))
        keep.append((start, end, t))
        self.regions = keep
        return t

    def alias(self, big, name):
        t = Tile(big.ap, name)
        t.span = big.span
        if big.w is not None:
            t.r.append(big.w)
        t.r.extend(big.r)
        self.regions.append((big.span[0], big.span[1], t))
        return t

    def op(self, eng, fn, reads=(), writes=(), dma=False):
        o = Op()
        o.eng = eng
        o.fn = fn
        o.dma = dma
        o.sig = False
        o.idx = 0
        deps = {}
        reads = [t for t in reads if not t.ro]
        for t in reads:
            if t.w is not None:
                deps[id(t.w)] = t.w
        for t in writes:
            if t.w is not None:
                deps[id(t.w)] = t.w
            for x in t.r:
                deps[id(x)] = x
        o.deps = list(deps.values())
        if not dma:
            self.nidx[eng] += 1
            o.idx = self.nidx[eng]
        wset = set(id(t) for t in writes)
        for t in writes:
            t.w = o
            t.r = []
        for t in reads:
            if id(t) in wset:
                continue
            if not dma:
                t.r = [x for x in t.r if x.dma or x.eng != eng]
            t.r.append(o)
        self.ops.append(o)
        return o

    def mm(self, ps, out_ap, lt, lhsT_ap, rt, rhs_ap, start, stop, extra=()):
        return self.op('pe', lambda e: e.matmul(out_ap, lhsT=lhsT_ap, rhs=rhs_ap, start=start, stop=stop),
                       reads=[lt, rt] + list(extra), writes=[ps])

    def transpose(self, ps, out_ap, it, in_ap, idt, ident_ap):
        return self.op('pe', lambda e: e.transpose(out_ap, in_ap, ident_ap), reads=[it, idt], writes=[ps])

    def act(self, ot, out_ap, it, in_ap, func, bias=None, scale=None, reads=(), accum=None, eng='act',
            extra_w=()):
        kw = {}
        if bias is not None:
            kw['bias'] = bias
        if scale is not None:
            kw['scale'] = scale
        if accum is not None:
            kw['accum_out'] = accum
        return self.op(eng, lambda e: e.activation(out_ap, in_ap, func, **kw),
                       reads=[it] + list(reads), writes=[ot] + list(extra_w))

    def tt(self, eng, ot, out_ap, at, a_ap, bt, b_ap, op):
        return self.op(eng, lambda e: e.tensor_tensor(out_ap, a_ap, b_ap, op), reads=[at, bt], writes=[ot])

    def ts(self, eng, ot, out_ap, at, a_ap, s1, s2, op0, op1=None, reads=()):
        if op1 is None:
            return self.op(eng, lambda e: e.tensor_scalar(out_ap, a_ap, s1, None, op0),
                           reads=[at] + list(reads), writes=[ot])
        return self.op(eng, lambda e: e.tensor_scalar(out_ap, a_ap, s1, s2, op0, op1),
                       reads=[at] + list(reads), writes=[ot])

    def stt(self, ot, out_ap, at, a_ap, scalar, bt, b_ap, op0, op1, reads=()):
        return self.op('dve', lambda e: e.scalar_tensor_tensor(out_ap, a_ap, scalar, b_ap, op0, op1),
                       reads=[at, bt] + list(reads), writes=[ot])

    def copy(self, eng, ot, out_ap, it, in_ap):
        if eng == 'act':
            return self.op(eng, lambda e: e.copy(out_ap, in_ap), reads=[it], writes=[ot])
        return self.op(eng, lambda e: e.tensor_copy(out_ap, in_ap), reads=[it], writes=[ot])

    def memset(self, eng, ot, out_ap, val):
        return self.op(eng, lambda e: e.memset(out_ap, val), reads=[], writes=[ot])

    def dma(self, eng, ot, out_ap, it, in_ap, **kw):
        return self.op(eng, lambda e: e.dma_start(out=out_ap, in_=in_ap, **kw), reads=[it], writes=[ot],
                       dma=True)

    def dbg(self, name, tile, ap, dtype=F32):
        if not os.environ.get('DBG_SCR'):
            return
        d = self.nc.dram_tensor("dbg_" + name, list(ap.shape), dtype, kind="ExternalOutput").ap()
        t = Tile(d, "dbg_" + name)
        self.dma('sp', t, d, tile, ap)
        self.dbg_tiles.append(t)

    def emit(self):
        nc = self.nc
        clock = {e: {} for e in ENG_ATTR}
        pools = {}
        rr = {}
        nid = [0]

        def newslot(e):
            nid[0] += 1
            return {'sem': nc.alloc_semaphore(f"d_{e}_{nid[0]}"), 'cnt': 0, 'last': None, 'id': nid[0]}

        for o in self.ops:
            e = o.eng
            deps = list(o.deps)
            if o.dma:
                pool = pools.setdefault(e, [])
                if len(pool) < NPOOL:
                    pool.append(newslot(e))
                    i = len(pool) - 1
                    rr[e] = 0
                else:
                    i = rr[e]
                    rr[e] = (i + 1) % NPOOL
                    if pool[i]['cnt'] + 16 > SEM_LIMIT:
                        last = pool[i]['last']
                        pool[i] = newslot(e)
                        pool[i]['carry'] = last
                slot = pool[i]
                if slot['last'] is not None:
                    deps.append(slot['last'])
                slot['cnt'] += 16
                slot['last'] = o
                o.sem = slot['sem']
                o.cnt = slot['cnt']
                o.key = ('d', slot['id'])
            ck = clock[e]
            waits = []
            deps.sort(key=lambda d: -(d.cnt if d.dma else d.idx))
            for d in deps:
                if d.dma:
                    key = d.key
                    val = d.cnt
                else:
                    if d.eng == 'pe' and e == 'pe' and not o.dma:
                        continue
                    key = d.eng
                    val = d.idx
                if ck.get(key, 0) >= val:
                    continue
                waits.append(d)
                d.sig = True
                nk = dict(ck)
                for k, v in d.snap.items():
                    if nk.get(k, 0) < v:
                        nk[k] = v
                if nk.get(key, 0) < val:
                    nk[key] = val
                ck = nk
            clock[e] = ck
            o.snap = ck
            o.waits = waits

        cnt = {e: 0 for e in ENG_ATTR}
        esems = {e: [] for e in ENG_ATTR}
        for o in self.ops:
            if o.dma or not o.sig:
                continue
            c = cnt[o.eng]
            cnt[o.eng] = c + 1
            ep = c // SEM_LIMIT
            if ep >= len(esems[o.eng]):
                esems[o.eng].append(nc.alloc_semaphore(f"c_{o.eng}_{ep}"))
            o.sem = esems[o.eng][ep]
            o.sigval = c - ep * SEM_LIMIT + 1

        streams = {e: [] for e in ENG_ATTR}
        for o in self.ops:
            streams[o.eng].append(o)
        self.stats = {e: (len(streams[e]), sum(len(o.waits) for o in streams[e])) for e in ENG_ATTR}
        with nc.Block() as block:
            for e, attr in ENG_ATTR.items():
                ops = streams[e]

                def body(eng, ops=ops):
                    for o in ops:
                        for d in o.waits:
                            eng.wait_ge(d.sem, d.cnt if d.dma else d.sigval)
                        if o.fn is None:
                            continue
                        ins = o.fn(eng)
                        if o.dma:
                            ins.then_inc(o.sem, 16)
                        elif o.sig:
                            ins.then_inc(o.sem, 1)
                getattr(block, attr)(body)


def phase_consts(K, C, names):
    for n in names:
        d_ap, shp, dt = C['_phase_d'][n]
        C[n] = K.alloc("pc_" + n, shp[1:], dt)
        K.dma('sp', C[n], C[n].ap, K.dram_in, d_ap)


def xt_tile_ap(XT_ap, i):
    return XT_ap.rearrange("(c p) n -> p c n", p=128)[:, :, i * TT:(i + 1) * TT]


def load_weight_cast(K, W_ap, kchunks, ncols, name, seg=2048):
    big = K.alloc(name, [kchunks, ncols], BF16)
    pieces = {}
    Wv = W_ap.rearrange("(c p) n -> p c n", p=128)
    for kc in range(kchunks):
        for s0 in range(0, ncols, seg):
            s1 = min(ncols, s0 + seg)
            t = K.alias(big, f"{name}_{kc}_{s0}")
            K.dma('pool', t, big.ap[:, kc, s0:s1], K.dram_in, Wv[:, kc, s0:s1])
            pieces[(kc, s0 // seg)] = t
    return big.ap, pieces, seg


def rms_stats(K, C, xt, ps_stats, sq, lnv, rstd):
    for c in range(8):
        s = sq[c % 2]
        K.act(s, s.ap[:, :], xt, xt.ap[:, c, :], AF.Square)
        K.mm(ps_stats, ps_stats.ap[:, :], C['ones_bf'], C['ones_bf'].ap[:, :], s, s.ap[:, :],
             start=(c == 0), stop=(c == 7))
    K.act(lnv, lnv.ap[:, :], ps_stats, ps_stats.ap[:, :], AF.Ln, bias=C['eps'].ap[:, 0:1], scale=1.0 / D,
          reads=[C['eps']])
    K.act(rstd, rstd.ap[:, :], lnv, lnv.ap[:, :], AF.Exp, scale=-0.5)


def ffn_phase(K, C, XT, layer, Wgu_ap, Wd_ap, gvec, gcol):
    K.sb_set(C['sb_base'])
    wgu, pgu, seg = load_weight_cast(K, Wgu_ap, 8, 2 * FH, f"wgu{layer}")
    wd, pd, segd = load_weight_cast(K, Wd_ap, NJ, D, f"wd{layer}", seg=1024)
    xts = [K.alloc(f"f_xt{b}", [8, TT], F32) for b in range(2)]
    hT = K.alloc("f_hT", [8, TT], BF16)
    sq = [K.alloc(f"f_sq{b}", [TT], BF16) for b in range(2)]
    lnv = K.alloc("f_lnv", [TT], F32)
    rstd = K.alloc("f_rstd", [TT], F32)
    sg = [K.alloc("f_sg", [TT], F32)] * 2
    aT = K.alloc("f_aT", [NJ, TT], BF16)
    psS = K.psum[0]
    psG = [K.psum[1], K.psum[2]]
    psU = [K.psum[3], K.psum[4]]
    psO = [K.psum[5], K.psum[6]]

    def load(i):
        K.dma('sp', xts[i % 2], xts[i % 2].ap, XT[i], xt_tile_ap(XT[i].ap, i))

    load(0)
    for i in range(NT):
        xt = xts[i % 2]
        if i + 1 < NT:
            load(i + 1)
        rstd_pow(K, C, xt, psS, sq, rstd, lnv)
        for c in range(8):
            K.stt(hT, hT.ap[:, c, :], xt, xt.ap[:, c, :], gvec.ap[:, gcol + c:gcol + c + 1], rstd, rstd.ap[:, :],
                  ALU.mult, ALU.mult, reads=[gvec])
        for j in range(NJ):
            pg = psG[j % 2]
            pu = psU[j % 2]
            for kc in range(8):
                c0 = j * 128
                K.mm(pg, pg.ap[:, :], pgu[(kc, c0 // seg)], wgu[:, kc, c0:c0 + 128], hT, hT.ap[:, kc, :],
                     start=(kc == 0), stop=(kc == 7))
            for kc in range(8):
                c0 = FH + j * 128
                K.mm(pu, pu.ap[:, :], pgu[(kc, c0 // seg)], wgu[:, kc, c0:c0 + 128], hT, hT.ap[:, kc, :],
                     start=(kc == 0), stop=(kc == 7))
            s = sg[j % 2]
            K.act(s, s.ap[:, :], pg, pg.ap[:, :], AF.Silu)
            K.tt('dve', aT, aT.ap[:, j, :], s, s.ap[:, :], pu, pu.ap[:, :], ALU.mult)
        for m in range(8):
            po = psO[m % 2]
            for j in range(NJ):
                K.mm(po, po.ap[:, :], pd[(j, 0)], wd[:, j, m * 128:(m + 1) * 128], aT, aT.ap[:, j, :],
                     start=(j == 0), stop=(j == NJ - 1))
            K.tt('dve', xt, xt.ap[:, m, :], po, po.ap[:, :], xt, xt.ap[:, m, :], ALU.add)
        K.dma('sp', XT[i], xt_tile_ap(XT[i].ap, i), xt, xt.ap)


PATS = (1, 4, 16)


def att_qkv_phase(K, C, XT, li, Wqkv_ap, S, gcol, qcol, kcol):
    K.sb_set(C['sb_base'])
    phase_consts(K, C, ['blk64', 'ropeP'])
    w, pw, seg = load_weight_cast(K, Wqkv_ap, 8, 3 * D, f"wqkv{li}", seg=1024)
    xts = [K.alloc(f"a_xt{b}", [8, TT], F32) for b in range(2)]
    hTs = K.alloc("a_hTs", [8, 2048], BF16)
    sq = [K.alloc(f"a_sq{b}", [TT], BF16) for b in range(2)]
    lnv = K.alloc("a_lnv", [TT], F32)
    rstd = K.alloc("a_rstd", [TT], F32)
    cs = [(K.alloc(f"a_cos{b}", [TT], F32), K.alloc(f"a_sin{b}", [TT], F32)) for b in range(2)]
    sets = []
    for b in range(2):
        sets.append(dict(qs=K.alloc(f"a_qs{b}", [TT], F32), hsq=K.alloc(f"a_hsq{b}", [TT], BF16),
                         lnh=K.alloc(f"a_lnh{b}", [TT], F32), rsh=K.alloc(f"a_rsh{b}", [TT], F32),
                         qn=K.alloc(f"a_qn{b}", [TT], F32), t1=K.alloc(f"a_t1{b}", [TT], F32),
                         t2=K.alloc(f"a_t2{b}", [TT], F32), qo=K.alloc(f"a_qo{b}", [TT], BF16)))
    vbuf = K.alloc("a_vbuf", [16, D], BF16)
    psS = K.psum[0]
    psQ = [K.psum[1], K.psum[2]]
    psH = [K.psum[3], K.psum[7]]
    psP = K.psum[4]
    psV = [K.psum[5], K.psum[6]]
    vecs = C['vecs']

    def load(i):
        K.dma('sp', xts[i % 2], xts[i % 2].ap, XT[i], xt_tile_ap(XT[i].ap, i))
        co, si = cs[i % 2]
        K.dma('sp', co, co.ap, K.dram_in, C['cos_d'][:, i * TT:(i + 1) * TT])
        K.dma('sp', si, si.ap, K.dram_in, C['sin_d'][:, i * TT:(i + 1) * TT])

    load(0)
    vcnt = 0
    for i in range(NT):
        xt = xts[i % 2]
        co, si = cs[i % 2]
        if i + 1 < NT:
            load(i + 1)
        rms_stats(K, C, xt, psS, sq, lnv, rstd)
        lc = (i % 4) * TT
        for c in range(8):
            K.stt(hTs, hTs.ap[:, c, lc:lc + TT], xt, xt.ap[:, c, :], vecs.ap[:, gcol + c:gcol + c + 1], rstd,
                  rstd.ap[:, :], ALU.mult, ALU.mult, reads=[vecs])
        for c in range(16):
            st = sets[c % 2]
            ps = psQ[c % 2]
            ph = psH[c % 2]
            for kc in range(8):
                c0 = c * 128
                K.mm(ps, ps.ap[:, :], pw[(kc, c0 // seg)], w[:, kc, c0:c0 + 128], hTs, hTs.ap[:, kc, lc:lc + TT],
                     start=(kc == 0), stop=(kc == 7))
            K.copy('act', st['qs'], st['qs'].ap[:, :], ps, ps.ap[:, :])
            K.act(st['hsq'], st['hsq'].ap[:, :], ps, ps.ap[:, :], AF.Square)
            K.mm(ph, ph.ap[:, :], C['blk64'], C['blk64'].ap[:, :], st['hsq'], st['hsq'].ap[:, :], True, True)
            K.act(st['lnh'], st['lnh'].ap[:, :], ph, ph.ap[:, :], AF.Ln, bias=C['eps'].ap[:, 0:1], scale=1.0 / 64,
                  reads=[C['eps']])
            K.act(st['rsh'], st['rsh'].ap[:, :], st['lnh'], st['lnh'].ap[:, :], AF.Exp, scale=-0.5)
            gc = (qcol if c < 8 else kcol)
            K.stt(st['qn'], st['qn'].ap[:, :], st['qs'], st['qs'].ap[:, :], vecs.ap[:, gc:gc + 1], st['rsh'],
                  st['rsh'].ap[:, :], ALU.mult, ALU.mult, reads=[vecs])
            K.mm(psP, psP.ap[:, :], C['ropeP'], C['ropeP'].ap[:, :], st['qn'], st['qn'].ap[:, :], True, True)
            K.tt('pool', st['t1'], st['t1'].ap[:, :], st['qn'], st['qn'].ap[:, :], co, co.ap[:, :], ALU.mult)
            K.tt('dve', st['t2'], st['t2'].ap[:, :], psP, psP.ap[:, :], si, si.ap[:, :], ALU.mult)
            K.tt('dve', st['qo'], st['qo'].ap[:, :], st['t1'], st['t1'].ap[:, :], st['t2'], st['t2'].ap[:, :], ALU.add)
            dst = S['QT'][c][i] if c < 8 else S['KT'][c - 8][i]
            K.dma('sp', dst, dst.ap[:, i * TT:(i + 1) * TT], st['qo'], st['qo'].ap[:, :])
        if i % 4 == 3:
            s = i // 4
            for pi, d in enumerate(PATS):
                for lb in range(16):
                    nl = lb // d
                    r = lb % d
                    start = nl * 128 * d + r
                    for half in range(2):
                        pv = psV[vcnt % 2]
                        vcnt += 1
                        for kc in range(8):
                            c0 = 2 * D + half * 512
                            K.mm(pv, pv.ap[:, :], hTs, hTs.ap[:, kc, start:start + 127 * d + 1:d],
                                 pw[(kc, c0 // seg)], w[:, kc, c0:c0 + 512], start=(kc == 0), stop=(kc == 7))
                        eng = 'act' if half == 0 else 'dve'
                        K.copy(eng, vbuf, vbuf.ap[:, lb, half * 512:(half + 1) * 512], pv, pv.ap[:, :])
                for hp in range(8):
                    dst = S['VP'][pi][hp][s]
                    K.dma('sp', dst, dst.ap[:, 16 * s:16 * s + 16, :], vbuf, vbuf.ap[:, :, hp * 128:(hp + 1) * 128])


def att_core_phase(K, C, li, S):
    K.sb_set(C['sb_base'])
    phase_consts(K, C, ['mask4'])
    qk = [(K.alloc(f"c_qt{b}", [L], BF16), K.alloc(f"c_kt{b}", [L], BF16)) for b in range(2)]
    vps = [[K.alloc(f"c_vp{b}_{p}", [32, 128], BF16) for p in range(3)] for b in range(2)]
    accN = K.alloc("c_accN", [L], F32)
    accD = K.alloc("c_accD", [L], F32)
    pts = [[K.alloc(f"c_pt{h}_{k}", [TT], BF16) for k in range(3)] for h in range(2)]
    ot = K.alloc("c_ot", [L], BF16)
    rec = [K.alloc(f"c_rec{b}", [TT], F32) for b in range(2)]
    psSb = [[K.psum[0], K.psum[1]], [K.psum[2], K.psum[3]]]
    psOb = [K.psum[4], K.psum[5]]
    psDb = [K.psum[6], K.psum[7]]
    mask = C['mask4']
    ones64 = C['ones_bf']

    def load(hp):
        qt, kt = qk[hp % 2]
        K.dma('sp', qt, qt.ap, S['QT'][hp][0], S['QT'][hp][0].ap)
        for t in S['QT'][hp][1:]:
            qt.r
        K.dma('sp', kt, kt.ap, S['KT'][hp][0], S['KT'][hp][0].ap)
        for p in range(3):
            v = vps[hp % 2][p]
            K.dma('sp', v, v.ap, S['VP'][p][hp][0], S['VP'][p][hp][0].ap)

    def load_multi(dst, dst_ap, tiles, src_ap):
        K.op('sp', lambda e: e.dma_start(out=dst_ap, in_=src_ap), reads=list(tiles), writes=[dst], dma=True)

    def load2(hp):
        qt, kt = qk[hp % 2]
        load_multi(qt, qt.ap, S['QT'][hp], S['QT'][hp][0].ap)
        load_multi(kt, kt.ap, S['KT'][hp], S['KT'][hp][0].ap)
        for p in range(3):
            v = vps[hp % 2][p]
            load_multi(v, v.ap, S['VP'][p][hp], S['VP'][p][hp][0].ap)

    load2(0)
    octr = 0
    for hp in range(8):
        if hp + 1 < 8:
            load2(hp + 1)
        qt, kt = qk[hp % 2]
        groups = []
        for pi, d in enumerate(PATS):
            nb = 32 // d
            for r in range(d):
                for g in range((nb + 1) // 2):
                    units = [m for m in (2 * g, 2 * g + 1) if m < nb]
                    groups.append((pi, d, r, nb, units))

        def emit_S(gi):
            pi, d, r, nb, units = groups[gi]
            qv = qt.ap.rearrange("p (m d) -> p d m", d=d)
            kv = kt.ap.rearrange("p (m d) -> p d m", d=d)
            for h in range(2):
                ps = psSb[h][gi % 2]
                rows = slice(64 * h, 64 * h + 64)
                for ui, m in enumerate(units):
                    ncols = 256 if m < nb - 1 else 128
                    K.mm(ps, ps.ap[:, ui * 256:ui * 256 + ncols], kt, kv[rows, r, 128 * m:128 * m + 128],
                         qt, qv[rows, r, 128 * m:128 * m + ncols], True, True)

        def emit_E(gi):
            pi, d, r, nb, units = groups[gi]
            valid = 0
            for ui, m in enumerate(units):
                valid = ui * 256 + (256 if m < nb - 1 else 128)
            for h in range(2):
                ps = psSb[h][gi % 2]
                pt = pts[h][gi % 3]
                K.act(pt, pt.ap[:, 0:valid], ps, ps.ap[:, 0:valid], AF.Exp, scale=0.125)
                K.tt('pool', pt, pt.ap[:, 0:valid], pt, pt.ap[:, 0:valid], mask, mask.ap[:, 0:valid], ALU.mult)

        state = {'ob': 0, 'obank': 0}

        def emit_PV(gi):
            nonlocal octr
            pi, d, r, nb, units = groups[gi]
            vp = vps[hp % 2][pi]
            for ui, m in enumerate(units):
                blk = m * d + r
                pso = psOb[octr % 2]
                psd = psDb[octr % 2]
                oc = state['ob'] * 128
                for h in range(2):
                    rows = slice(64 * h, 64 * h + 64)
                    pt = pts[h][gi % 3]
                    cur = pt.ap[:, ui * 256:ui * 256 + 128]
                    if m > 0:
                        if ui == 1:
                            ppt = pt
                            prev = pt.ap[:, 128:256]
                        else:
                            ppt = pts[h][(gi - 1) % 3]
                            prev = ppt.ap[:, 256 + 128:512]
                    for (pp, lcur, lprev) in ((pso, vp.ap[:, blk, 64 * h:64 * h + 64],
                                               vp.ap[:, blk - d, 64 * h:64 * h + 64] if m > 0 else None),
                                              (psd, ones64.ap[:, 0:64], ones64.ap[:, 0:64])):
                        lt = vp if pp is pso else ones64
                        K.mm(pp, pp.ap[rows, oc:oc + 128], lt, lcur, pt, cur, True, m == 0)
                        if m > 0:
                            K.mm(pp, pp.ap[rows, oc:oc + 128], lt, lprev, ppt, prev, False, True)
                state['ob'] += 1
                last_of_class = (m == nb - 1)
                flush = state['ob'] == 4 or (last_of_class and d != 16) or (last_of_class and d == 16 and r % 2 == 1)
                if flush:
                    nblk = state['ob']
                    accNv = accN.ap.rearrange("p (m d) -> p d m", d=d)
                    accDv = accD.ap.rearrange("p (m d) -> p d m", d=d)
                    if d == 16:
                        av = lambda a: a[:, r - 1:r + 1, :]
                        pv_ = lambda p: p.ap.rearrange("p (a b) -> p a b", a=2)
                    else:
                        n0 = m - nblk + 1
                        av = lambda a: a[:, r, 128 * n0:128 * (m + 1)]
                        pv_ = lambda p: p.ap[:, 0:128 * nblk]
                    if d == 1:
                        K.copy('dve', accN, av(accNv), pso, pv_(pso))
                        K.copy('dve', accD, av(accDv), psd, pv_(psd))
                    else:
                        K.tt('dve', accN, av(accNv), pso, pv_(pso), accN, av(accNv), ALU.add)
                        K.tt('dve', accD, av(accDv), psd, pv_(psd), accD, av(accDv), ALU.add)
                    state['ob'] = 0
                    octr += 1

        ng = len(groups)
        emit_S(0)
        for gi in range(ng):
            if gi + 1 < ng:
                emit_S(gi + 1)
            emit_E(gi)
            emit_PV(gi)
        for i in range(NT):
            rc = rec[i % 2]
            cols = slice(i * TT, (i + 1) * TT)
            K.op('dve', lambda e, rc=rc, cols=cols: e.reciprocal(rc.ap[:, :], accD.ap[:, cols]), reads=[accD],
                 writes=[rc])
            K.tt('pool', ot, ot.ap[:, cols], accN, accN.ap[:, cols], rc, rc.ap[:, :], ALU.mult)
        K.dma('sp', S['OT'][hp], S['OT'][hp].ap, ot, ot.ap)


def att_out_phase(K, C, XT, li, Wo_ap, S):
    K.sb_set(C['sb_base'])
    w, pw, seg = load_weight_cast(K, Wo_ap, 8, D, f"wo{li}", seg=1024)
    xts = [K.alloc(f"o_xt{b}", [8, TT], F32) for b in range(2)]
    ots = [K.alloc(f"o_ot{b}", [8, TT], BF16) for b in range(2)]
    psO = [K.psum[0], K.psum[1], K.psum[2], K.psum[3]]

    def load(i):
        K.dma('sp', xts[i % 2], xts[i % 2].ap, XT[i], xt_tile_ap(XT[i].ap, i))
        K.op('sp', lambda e, i=i: e.dma_start(out=ots[i % 2].ap,
                                             in_=S['OT_d'].rearrange("c p n -> p c n")[:, :, i * TT:(i + 1) * TT]),
             reads=S['OT'], writes=[ots[i % 2]], dma=True)

    load(0)
    for i in range(NT):
        if i + 1 < NT:
            load(i + 1)
        xt = xts[i % 2]
        o = ots[i % 2]
        for m in range(8):
            po = psO[m % 4]
            for kc in range(8):
                K.mm(po, po.ap[:, :], pw[(kc, 0)], w[:, kc, m * 128:(m + 1) * 128], o, o.ap[:, kc, :],
                     start=(kc == 0), stop=(kc == 7))
            K.tt('dve', xt, xt.ap[:, m, :], po, po.ap[:, :], xt, xt.ap[:, m, :], ALU.add)
        K.dma('sp', XT[i], xt_tile_ap(XT[i].ap, i), xt, xt.ap)


def rstd_pow(K, C, xt, ps_stats, sq, rstd, tmp):
    for c in range(8):
        s = sq[c % 2]
        K.act(s, s.ap[:, :], xt, xt.ap[:, c, :], AF.Square)
        K.mm(ps_stats, ps_stats.ap[:, :], C['ones_bf'], C['ones_bf'].ap[:, :], s, s.ap[:, :],
             start=(c == 0), stop=(c == 7))
    K.ts('dve', tmp, tmp.ap[:, :], ps_stats, ps_stats.ap[:, :], 1.0 / D, EPS, ALU.mult, ALU.add)
    K.tt('pool', rstd, rstd.ap[:, :], tmp, tmp.ap[:, :], C['neghalf'], C['neghalf'].ap[:, 0:1].to_broadcast([128, TT]), ALU.pow)


def rec_a_phase(K, C, XT, li, Win_ap, Wr_ap, Wi_ap, S, vo):
    K.sb_set(C['sb_base'])
    vecs = C['vecs']
    NW = 3584
    wbig = K.alloc(f"ra_w{li}", [8, NW], BF16)
    w = wbig.ap
    Wv = Win_ap.rearrange("(c p) n -> p c n", p=128)
    pw = {}
    for kc in range(8):
        for (d0, s0, n) in ((0, 0, 1024), (1024, 1024, 1024), (2048, 3072, 1536)):
            t = K.alias(wbig, f"ra_w{li}_{kc}_{d0}")
            K.dma('pool', t, w[:, kc, d0:d0 + n], K.dram_in, Wv[:, kc, s0:s0 + n])
            pw[(kc, d0)] = t

    def wpiece(kc, col):
        return pw[(kc, 0 if col < 1024 else (1024 if col < 2048 else 2048))]

    wr = K.alloc("ra_wr", [8, 128], BF16)
    wi = K.alloc("ra_wi", [8, 128], BF16)
    K.memset('pool', wr, wr.ap, 0.0)
    K.memset('pool', wi, wi.ap, 0.0)
    for (wt, src) in ((wr, Wr_ap), (wi, Wi_ap)):
        sv = src.rearrange("(c two) i j -> two i c j", two=2)
        for h in range(2):
            K.dma('pool', wt, wt.ap[64 * h:64 * h + 64, :, 64 * h:64 * h + 64], K.dram_in, sv[h])
    diag = K.alloc("ra_diag", [20 * 4, 128], BF16)
    for c in range(20):
        for j in range(4):
            col = (vo['lru_conv_w'] + j * 8 + c) if c < 8 else (vo['ssd_conv_w'] + j * 12 + (c - 8))
            K.ts('pool', diag, diag.ap[:, c * 4 + j, :], C['ident_bf'], C['ident_bf'].ap[:, :],
                 vecs.ap[:, col:col + 1], None, ALU.mult, reads=[vecs])
    dv = K.alloc("ra_dv", [32], F32)
    sp = [K.alloc(f"ra_sp{k}", [8], F32) for k in range(4)]
    K.ts('dve', dv, dv.ap[:, 0:8], vecs, vecs.ap[:, vo['lru_b_r']:vo['lru_b_r'] + 8], 0.5, None, ALU.mult)
    K.ts('dve', dv, dv.ap[:, 8:16], vecs, vecs.ap[:, vo['lru_b_i']:vo['lru_b_i'] + 8], 0.5, None, ALU.mult)
    lam = vecs.ap[:, vo['lru_lambda']:vo['lru_lambda'] + 8]
    K.act(sp[0], sp[0].ap[:, :], vecs, lam, AF.Exp, scale=-1.0)
    K.ts('dve', sp[1], sp[1].ap[:, :], sp[0], sp[0].ap[:, :], 1.0, None, ALU.add)
    K.ts('dve', sp[2], sp[2].ap[:, :], sp[1], sp[1].ap[:, :], -1.0, 1e-30, ALU.add, ALU.max)
    K.op('dve', lambda e: e.reciprocal(sp[2].ap[:, :], sp[2].ap[:, :]), reads=[sp[2]], writes=[sp[2]])
    K.tt('dve', sp[2], sp[2].ap[:, :], sp[2], sp[2].ap[:, :], sp[0], sp[0].ap[:, :], ALU.mult)
    K.act(sp[3], sp[3].ap[:, :], sp[1], sp[1].ap[:, :], AF.Ln)
    K.tt('dve', sp[3], sp[3].ap[:, :], sp[3], sp[3].ap[:, :], sp[2], sp[2].ap[:, :], ALU.mult)
    K.ts('dve', dv, dv.ap[:, 16:24], sp[3], sp[3].ap[:, :], -8.0, None, ALU.mult)
    K.ts('dve', dv, dv.ap[:, 24:32], sp[3], sp[3].ap[:, :], -4.0, None, ALU.mult)

    rawb = [K.alloc(f"ra_raw{c}", [516], BF16) for c in range(20)]
    for c in range(20):
        K.memset('pool', rawb[c], rawb[c].ap[:, 0:4], 0.0)
    hcar = K.alloc("ra_hcar", [8], F32)
    xt = K.alloc("ra_xt", [8, TT], F32)
    hT = K.alloc("ra_hT", [8, TT], BF16)
    sq = [K.alloc(f"ra_sq{b}", [TT], BF16) for b in range(2)]
    rtmp = K.alloc("ra_rtmp", [TT], F32)
    rstd = K.alloc("ra_rstd", [TT], F32)
    gl = K.alloc("ra_gl", [8, TT], BF16)
    xo = [K.alloc(f"ra_xo{b}", [TT], BF16) for b in range(2)]
    names = ['xc', 'thr', 'thi', 'a', 'a2', 'om', 'sr', 't2', 'u', 'hl']
    sets = [{n: K.alloc(f"ra_{n}{b}", [TT], F32) for n in names} for b in range(2)]
    for b in range(2):
        sets[b]['xcb'] = K.alloc(f"ra_xcb{b}", [TT], BF16)
        sets[b]['oa'] = K.alloc(f"ra_oa{b}", [TT], BF16)
    psS = K.psum[0]
    psP = [K.psum[1], K.psum[2]]
    psC = [K.psum[3], K.psum[4]]
    psR = K.psum[5]
    psI = K.psum[6]

    def proj(ps, col0, ncol=128):
        for kc in range(8):
            K.mm(ps, ps.ap[0:ncol, :], wpiece(kc, col0), w[:, kc, col0:col0 + ncol], hT, hT.ap[:, kc, :],
                 start=(kc == 0), stop=(kc == 7))

    def conv(pc, c, ps, first):
        rb = rawb[c]
        if not first:
            K.copy('pool', rb, rb.ap[:, 0:4], rb, rb.ap[:, 512:516])
        K.copy('act', rb, rb.ap[:, 4:516], ps, ps.ap[:, :])
        for j in range(4):
            K.mm(pc, pc.ap[:, :], diag, diag.ap[:, c * 4 + j, :], rb, rb.ap[:, 1 + j:1 + j + 512],
                 start=(j == 0), stop=(j == 3))

    for i in range(NT):
        K.dma('sp', xt, xt.ap, XT[i], xt_tile_ap(XT[i].ap, i))
        rstd_pow(K, C, xt, psS, sq, rstd, rtmp)
        for c in range(8):
            K.stt(hT, hT.ap[:, c, :], xt, xt.ap[:, c, :], vecs.ap[:, vo['rec_norm'] + c:vo['rec_norm'] + c + 1],
                  rstd, rstd.ap[:, :], ALU.mult, ALU.mult, reads=[vecs])
        K.dma('sp', S['HT'][i], S['HT_d'].rearrange("c p n -> p c n")[:, :, i * TT:(i + 1) * TT], hT, hT.ap)
        for c in range(8):
            ps = psP[c % 2]
            proj(ps, 1024 + c * 128)
            K.act(gl, gl.ap[:, c, :], ps, ps.ap[:, :], AF.Gelu_apprx_tanh)
        for cc in range(12):
            ps = psP[cc % 2]
            pc = psC[cc % 2]
            proj(ps, 2048 + cc * 128)
            conv(pc, 8 + cc, ps, i == 0)
            o = xo[cc % 2]
            bcol = vo['ssd_conv_b'] + cc
            K.act(o, o.ap[:, :], pc, pc.ap[:, :], AF.Silu, bias=vecs.ap[:, bcol:bcol + 1], reads=[vecs])
            K.dma('sp', S['XBC'][cc][i], S['XBC'][cc][i].ap[:, i * TT:(i + 1) * TT], o, o.ap[:, :])
        for c in range(8):
            st = sets[c % 2]
            ps = psP[c % 2]
            pc = psC[c % 2]
            proj(ps, c * 128)
            conv(pc, c, ps, i == 0)
            bcol = vo['lru_conv_b'] + c
            K.act(st['xc'], st['xc'].ap[:, :], pc, pc.ap[:, :], AF.Identity, bias=vecs.ap[:, bcol:bcol + 1],
                  reads=[vecs])
            K.copy('dve', st['xcb'], st['xcb'].ap[:, :], st['xc'], st['xc'].ap[:, :])
            K.mm(psR, psR.ap[:, :], wr, wr.ap[:, c, :], st['xcb'], st['xcb'].ap[:, :], True, True)
            K.mm(psI, psI.ap[:, :], wi, wi.ap[:, c, :], st['xcb'], st['xcb'].ap[:, :], True, True)
            K.act(st['thr'], st['thr'].ap[:, :], psR, psR.ap[:, :], AF.Tanh, bias=dv.ap[:, c:c + 1], scale=0.5,
                  reads=[dv])
            K.act(st['thi'], st['thi'].ap[:, :], psI, psI.ap[:, :], AF.Tanh, bias=dv.ap[:, 8 + c:9 + c], scale=0.5,
                  reads=[dv])
            K.act(st['a'], st['a'].ap[:, :], st['thr'], st['thr'].ap[:, :], AF.Exp, bias=dv.ap[:, 24 + c:25 + c],
                  scale=dv.ap[:, 24 + c:25 + c], reads=[dv])
            K.act(st['a2'], st['a2'].ap[:, :], st['thr'], st['thr'].ap[:, :], AF.Exp, bias=dv.ap[:, 16 + c:17 + c],
                  scale=dv.ap[:, 16 + c:17 + c], reads=[dv])
            K.ts('dve', st['om'], st['om'].ap[:, :], st['a2'], st['a2'].ap[:, :], -1.0, 1.0, ALU.mult, ALU.add)
            K.tt('pool', st['sr'], st['sr'].ap[:, :], st['om'], st['om'].ap[:, :], C['half'],
                 C['half'].ap[:, 0:1].to_broadcast([128, TT]), ALU.pow)
            K.stt(st['t2'], st['t2'].ap[:, :], st['thi'], st['thi'].ap[:, :], 1.0, st['xc'], st['xc'].ap[:, :],
                  ALU.add, ALU.mult)
            K.stt(st['u'], st['u'].ap[:, :], st['t2'], st['t2'].ap[:, :], 0.5, st['sr'], st['sr'].ap[:, :],
                  ALU.mult, ALU.mult)
            init = 0.0 if i == 0 else hcar.ap[:, c:c + 1]
            K.op('dve', lambda e, st=st, init=init: e.tensor_tensor_scan(st['hl'].ap[:, :], st['a'].ap[:, :],
                                                                       st['u'].ap[:, :], init, ALU.mult, ALU.add),
                 reads=[st['a'], st['u'], hcar], writes=[st['hl']])
            K.copy('pool', hcar, hcar.ap[:, c:c + 1], st['hl'], st['hl'].ap[:, 511:512])
            K.tt('dve', st['oa'], st['oa'].ap[:, :], st['hl'], st['hl'].ap[:, :], gl, gl.ap[:, c, :], ALU.mult)
            K.dma('sp', S['MA'][c][i], S['MA'][c][i].ap[:, i * TT:(i + 1) * TT], st['oa'], st['oa'].ap[:, :])


def rec_b_phase(K, C, XT, li, Win_ap, Wout_ap, S, vo, bv_d, bo):
    K.sb_set(C['sb_base'])
    phase_consts(K, C, ['U_f32', 'ones_f32', 'neg_bf'])
    wo, pwo, sego = load_weight_cast(K, Wout_ap, 16, D, f"rb_wo{li}", seg=1024)
    wzbig = K.alloc(f"rb_wz{li}", [8, 1040], BF16)
    wz = wzbig.ap
    Wv = Win_ap.rearrange("(c p) n -> p c n", p=128)
    pwz = {}
    for kc in range(8):
        t = K.alias(wzbig, f"rb_wz{li}_{kc}")
        K.dma('pool', t, wz[:, kc, 0:1024], K.dram_in, Wv[:, kc, 2048:3072])
        K.dma('pool', t, wz[:, kc, 1024:1040], K.dram_in, Wv[:, kc, 4608:4624])
        pwz[kc] = t
    bv = K.alloc("rb_bv", [1072], F32)
    K.dma('sp', bv, bv.ap, K.dram_in, bv_d[:, bo:bo + 1072])
    Aneg = K.alloc("rb_A", [16], F32)
    K.act(Aneg, Aneg.ap[:, :], bv, bv.ap[:, 16:32], AF.Exp)
    K.ts('dve', Aneg, Aneg.ap[:, :], Aneg, Aneg.ap[:, :], -1.0, None, ALU.mult)
    DI = K.alloc("rb_DI", [16, 128], BF16)
    for h in range(16):
        K.ts('pool', DI, DI.ap[:, h, :], C['ident_bf'], C['ident_bf'].ap[:, :], bv.ap[:, 32 + h:33 + h], None,
             ALU.mult, reads=[bv])
    xts = [K.alloc("rb_xt", [8, TT], F32)] * 2
    hTs = [K.alloc(f"rb_hT{b}", [8, TT], BF16) for b in range(2)]
    xbcs = [K.alloc(f"rb_xbc{b}", [12, TT], BF16) for b in range(2)]
    mAs = [K.alloc("rb_mA", [8, TT], BF16)] * 2
    mB = K.alloc("rb_mB", [8, TT], BF16)
    ytok = K.alloc("rb_ytok", [4, D], F32)
    Sst = K.alloc("rb_S", [D], F32)
    prevb = K.alloc("rb_prev", [D], BF16)
    rhsR = K.alloc("rb_rhsR", [16 * 128], F32)
    Eexp = K.alloc("rb_E", [16 * 128], F32)
    MT = K.alloc("rb_MT", [16, 128], BF16)
    xsb = K.alloc("rb_xsb", [D], BF16)
    xsw = K.alloc("rb_xsw", [D], BF16)
    Btok = K.alloc("rb_Btok", [256], BF16)
    tmpF = K.alloc("rb_tmpF", [512], F32)
    bcw = K.alloc("rb_bcw", [D], F32)
    bce = K.alloc("rb_bce", [D], F32)
    bcc = K.alloc("rb_bcc", [D], F32)
    small = {n: K.alloc(f"rb_{n}", [16], F32) for n in ['v', 'e', 'dt', 'lndt', 'adt', 'cs', 'bE', 'wx', 'w', 'cd',
                                                        'ecs']}
    sz = K.alloc("rb_sz", [512], F32)
    yz = K.alloc("rb_yz", [512], F32)
    ysq = K.alloc("rb_ysq", [512], BF16)
    ss = K.alloc("rb_ss", [4], F32)
    yb = K.alloc("rb_yb", [512], BF16)
    ps_small = K.psum[0]
    ps_R = K.psum[1]
    ps_X = K.psum[2]
    ps_G = K.psum[3]
    ps_Y = K.psum[4]
    ps_F = K.psum[5]
    ps_St = K.psum[6]
    ps_Z = K.psum[7]
    identb = C['ident_bf']

    def load(i):
        b = i % 2
        K.dma('sp', hTs[b], hTs[b].ap, S['HT'][i], S['HT_d'].rearrange("c p n -> p c n")[:, :, i * TT:(i + 1) * TT])
        K.op('sp', lambda e, b=b, i=i: e.dma_start(
            out=xbcs[b].ap, in_=S['XBC_d'].rearrange("c p n -> p c n")[:, :, i * TT:(i + 1) * TT]),
            reads=[S['XBC'][cc][i] for cc in range(12)], writes=[xbcs[b]], dma=True)

    def load_single(i):
        b = i % 2
        K.dma('sp', xts[b], xts[b].ap, XT[i], xt_tile_ap(XT[i].ap, i))
        K.op('sp', lambda e, b=b, i=i: e.dma_start(
            out=mAs[b].ap, in_=S['MA_d'].rearrange("c p n -> p c n")[:, :, i * TT:(i + 1) * TT]),
            reads=[S['MA'][c][i] for c in range(8)], writes=[mAs[b]], dma=True)

    load(0)
    for i in range(NT):
        b = i % 2
        xt, hT, xbc, mA = xts[b], hTs[b], xbcs[b], mAs[b]
        load_single(i)
        if i + 1 < NT:
            load(i + 1)
        RB = float(os.environ.get('RB_STOP', '99'))
        for q in range(4):
            cg = 4 * i + q
            tc = slice(128 * q, 128 * q + 128)
            sm = small
            if RB < 2:
                continue
            for kc in range(8):
                K.mm(ps_small, ps_small.ap[:, 0:16], hT, hT.ap[:, kc, tc], pwz[kc], wz[:, kc, 1024:1040],
                     start=(kc == 0), stop=(kc == 7))
            K.tt('dve', sm['v'], sm['v'].ap[:, :], ps_small, ps_small.ap[:, 0:16], bv, bv.ap[:, 0:16], ALU.add)
            K.act(sm['e'], sm['e'].ap[:, :], sm['v'], sm['v'].ap[:, :], AF.Exp)
            K.act(sm['dt'], sm['dt'].ap[:, :], sm['e'], sm['e'].ap[:, :], AF.Ln, bias=C['one'].ap[:, 0:1],
                  reads=[C['one']])
            K.act(sm['lndt'], sm['lndt'].ap[:, :], sm['dt'], sm['dt'].ap[:, :], AF.Ln)
            K.tt('dve', sm['adt'], sm['adt'].ap[:, :], sm['dt'], sm['dt'].ap[:, :], Aneg, Aneg.ap[:, :], ALU.mult)
            K.mm(ps_small, ps_small.ap[:, 16:32], C['U_f32'], C['U_f32'].ap[:, :], sm['adt'], sm['adt'].ap[:, :],
                 True, True)
            K.mm(ps_small, ps_small.ap[:, 32:48], C['ones_f32'], C['ones_f32'].ap[:, :], sm['adt'],
                 sm['adt'].ap[:, :], True, True)
            K.copy('dve', sm['cs'], sm['cs'].ap[:, :], ps_small, ps_small.ap[:, 16:32])
            K.tt('dve', sm['bE'], sm['bE'].ap[:, :], sm['lndt'], sm['lndt'].ap[:, :], sm['cs'], sm['cs'].ap[:, :],
                 ALU.subtract)
            K.tt('dve', sm['wx'], sm['wx'].ap[:, :], ps_small, ps_small.ap[:, 32:48], sm['bE'], sm['bE'].ap[:, :],
                 ALU.add)
            K.act(sm['w'], sm['w'].ap[:, :], sm['wx'], sm['wx'].ap[:, :], AF.Exp)
            K.act(sm['cd'], sm['cd'].ap[:, :], ps_small, ps_small.ap[:, 32:48], AF.Exp)
            K.act(sm['ecs'], sm['ecs'].ap[:, :], sm['cs'], sm['cs'].ap[:, :], AF.Exp)
            if RB < 3:
                continue
            if cg <= 1:
                K.dbg(f"dt{cg}", sm['dt'], sm['dt'].ap[:, :])
                K.dbg(f"cs{cg}", sm['cs'], sm['cs'].ap[:, :])
                K.dbg(f"w{cg}", sm['w'], sm['w'].ap[:, :])
                K.dbg(f"cd{cg}", sm['cd'], sm['cd'].ap[:, :])
            K.tt('pool', rhsR, rhsR.ap.rearrange("p (h l) -> p h l", h=16),
                 C['U_f32'], C['U_f32'].ap.unsqueeze(1).to_broadcast([128, 16, 128]),
                 sm['adt'], sm['adt'].ap.unsqueeze(2).to_broadcast([128, 16, 128]), ALU.mult)
            for bb in range(4):
                K.mm(ps_R, ps_R.ap[:, :], C['ones_f32'], C['ones_f32'].ap[:, :], rhsR,
                     rhsR.ap[:, 512 * bb:512 * bb + 512], True, False)
                K.mm(ps_R, ps_R.ap.rearrange("p (h l) -> p h l", h=4), identb, identb.ap[:, :], C['neg_bf'],
                     C['neg_bf'].ap.unsqueeze(1).to_broadcast([128, 4, 128]), False, True)
                for hh in range(4):
                    h = 4 * bb + hh
                    K.act(Eexp, Eexp.ap[:, 128 * h:128 * h + 128], ps_R, ps_R.ap[:, 128 * hh:128 * hh + 128], AF.Exp,
                          bias=sm['bE'].ap[:, h:h + 1], reads=[sm['bE']])
            if RB < 4:
                continue
            pxb = ps_X.ap.bitcast(BF16)
            for c in range(8):
                K.transpose(ps_X, pxb[:, 128 * c:128 * c + 128], xbc, xbc.ap[:, c, tc], identb, identb.ap[:, :])
            if RB < 4.2:
                continue
            K.copy('act', xsb, xsb.ap[:, :], ps_X, pxb[:, :])
            if RB < 4.4:
                continue
            K.copy('pool', bcw, bcw.ap.rearrange("p (h e) -> p h e", h=16), sm['w'],
                   sm['w'].ap.unsqueeze(2).to_broadcast([128, 16, 64]))
            if RB < 4.47:
                continue
            K.tt('dve', xsw, xsw.ap[:, :], xsb, xsb.ap[:, :], bcw, bcw.ap[:, :], ALU.mult)
            if RB < 4.6:
                continue
            pgb = ps_G.ap.bitcast(BF16)
            for g in range(2):
                K.transpose(ps_G, pgb[:, 512 + 128 * g:512 + 128 * g + 128], xbc, xbc.ap[:, 8 + g, tc], identb,
                            identb.ap[:, :])
            K.copy('act', Btok, Btok.ap[:, :], ps_G, pgb[:, 512:768])
            if RB < 5:
                continue
            for g in range(2):
                K.mm(ps_G, ps_G.ap[:, 128 * g:128 * g + 128], xbc, xbc.ap[:, 8 + g, tc], xbc, xbc.ap[:, 10 + g, tc],
                     True, True)
            for g in range(2):
                K.tt('dve', MT, MT.ap[:, 8 * g:8 * g + 8, :], Eexp,
                     Eexp.ap[:, 1024 * g:1024 * g + 1024].rearrange("p (h l) -> p h l", h=8), ps_G,
                     ps_G.ap[:, 128 * g:128 * g + 128].unsqueeze(1).to_broadcast([128, 8, 128]), ALU.mult)
            if RB < 6:
                continue
            if cg <= 1:
                K.dbg(f"E{cg}", Eexp, Eexp.ap[:, :])
                K.dbg(f"xsb{cg}", xsb, xsb.ap[:, :], BF16)
                K.dbg(f"xsw{cg}", xsw, xsw.ap[:, :], BF16)
                K.dbg(f"Btok{cg}", Btok, Btok.ap[:, :], BF16)
                K.dbg(f"MT{cg}", MT, MT.ap.rearrange("p h l -> p (h l)"), BF16)
            for g in range(2):
                for hh in range(8):
                    h = 8 * g + hh
                    K.mm(ps_Y, ps_Y.ap[:, 64 * hh:64 * hh + 64], MT, MT.ap[:, h, :], xsb, xsb.ap[:, 64 * h:64 * h + 64],
                         True, False)
                    K.mm(ps_Y, ps_Y.ap[:, 64 * hh:64 * hh + 64], DI, DI.ap[:, h, :], xsb, xsb.ap[:, 64 * h:64 * h + 64],
                         False, True)
                yslot = ytok.ap[:, q, 512 * g:512 * g + 512]
                if cg > 0:
                    K.mm(ps_F, ps_F.ap[:, :], xbc, xbc.ap[:, 10 + g, tc], prevb, prevb.ap[:, 512 * g:512 * g + 512],
                         True, True)
                    if g == 0:
                        K.copy('pool', bce, bce.ap.rearrange("p (h e) -> p h e", h=16), sm['ecs'],
                               sm['ecs'].ap.unsqueeze(2).to_broadcast([128, 16, 64]))
                    K.tt('dve', tmpF, tmpF.ap[:, :], ps_F, ps_F.ap[:, :], bce, bce.ap[:, 512 * g:512 * g + 512],
                         ALU.mult)
                    K.tt('dve', ytok, yslot, ps_Y, ps_Y.ap[:, :], tmpF, tmpF.ap[:, :], ALU.add)
                else:
                    K.copy('dve', ytok, yslot, ps_Y, ps_Y.ap[:, :])
            if RB < 7:
                continue
            if cg < 31:
                for g in range(2):
                    K.mm(ps_St, ps_St.ap[:, :], Btok, Btok.ap[:, 128 * g:128 * g + 128], xsw,
                         xsw.ap[:, 512 * g:512 * g + 512], True, True)
                    sv = Sst.ap[:, 512 * g:512 * g + 512]
                    if cg > 0:
                        if g == 0:
                            K.copy('pool', bcc, bcc.ap.rearrange("p (h e) -> p h e", h=16), sm['cd'],
                                   sm['cd'].ap.unsqueeze(2).to_broadcast([128, 16, 64]))
                        K.tt('pool', Sst, sv, Sst, sv, bcc, bcc.ap[:, 512 * g:512 * g + 512], ALU.mult)
                        K.tt('dve', Sst, sv, ps_St, ps_St.ap[:, :], Sst, sv, ALU.add)
                    else:
                        K.copy('dve', Sst, sv, ps_St, ps_St.ap[:, :])
                K.copy('pool', prevb, prevb.ap[:, :], Sst, Sst.ap[:, :])
                if cg <= 1:
                    K.dbg(f"S{cg}", Sst, Sst.ap[:, :])
        for q in range(4 if RB >= 8 else 0):
            tc = slice(128 * q, 128 * q + 128)
            for g in range(2):
                for kc in range(8):
                    K.mm(ps_Z, ps_Z.ap[:, :], hT, hT.ap[:, kc, tc], pwz[kc], wz[:, kc, 512 * g:512 * g + 512],
                         start=(kc == 0), stop=(kc == 7))
                K.act(sz, sz.ap[:, :], ps_Z, ps_Z.ap[:, :], AF.Silu)
                K.tt('dve', yz, yz.ap[:, :], ytok, ytok.ap[:, q, 512 * g:512 * g + 512], sz, sz.ap[:, :], ALU.mult)
                K.act(ysq, ysq.ap[:, :], yz, yz.ap[:, :], AF.Square, accum=ss.ap[:, 0:1], extra_w=[ss])
                K.ts('dve', ss, ss.ap[:, 1:2], ss, ss.ap[:, 0:1], 1.0 / 512, EPS, ALU.mult, ALU.add)
                K.tt('pool', ss, ss.ap[:, 2:3], ss, ss.ap[:, 1:2], C['neghalf'], C['neghalf'].ap[:, 0:1], ALU.pow)
                K.stt(yb, yb.ap[:, :], yz, yz.ap[:, :], ss.ap[:, 2:3], bv, bv.ap[:, 48 + 512 * g:48 + 512 * g + 512],
                      ALU.mult, ALU.mult, reads=[ss])
                ptr = ps_X.ap.bitcast(BF16)
                for k4 in range(4):
                    K.transpose(ps_X, ptr[:, 128 * k4:128 * k4 + 128], yb, yb.ap[:, 128 * k4:128 * k4 + 128], identb,
                                identb.ap[:, :])
                K.copy('act', mB, mB.ap[:, 4 * g:4 * g + 4, tc], ps_X,
                       ptr[:, 0:512].rearrange("p (k t) -> p k t", k=4))
        if i == 0:
            K.dbg("ytok", ytok, ytok.ap.rearrange("p q d -> p (q d)"))
            K.dbg("mB", mB, mB.ap.rearrange("p c t -> p (c t)"), BF16)
        for m in range(8):
            po = [ps_Y, ps_F, ps_St, ps_Z][m % 4]
            for k in range(16):
                src, sap = (mA, mA.ap[:, k, :]) if k < 8 else (mB, mB.ap[:, k - 8, :])
                K.mm(po, po.ap[:, :], pwo[(k, 0)], wo[:, k, m * 128:(m + 1) * 128], src, sap,
                     start=(k == 0), stop=(k == 15))
            K.tt('dve', xt, xt.ap[:, m, :], po, po.ap[:, :], xt, xt.ap[:, m, :], ALU.add)
        K.dma('sp', XT[i], xt_tile_ap(XT[i].ap, i), xt, xt.ap)


WEIGHT_SHAPES = {
    'rec_w_in': [2, D, 4624], 'rec_w_out': [2, 2048, D],
    'lru_w_r': [2, 16, 64, 64], 'lru_w_i': [2, 16, 64, 64],
    'att_w_qkv': [2, D, 3 * D], 'att_w_out': [2, D, D],
    'ffn_w_gate_up': [4, D, 2 * FH], 'ffn_w_down': [4, FH, D],
}


def rope_tables():
    half = 8
    inv = (np.float32(500000.0) ** (-2.0 * np.arange(half, dtype=np.float32) / np.float32(16))).astype(np.float32)
    pos = np.arange(L, dtype=np.float32)
    ang = (pos[:, None] * inv[None, :]).astype(np.float32)
    cos = np.cos(ang).astype(np.float32).T
    sin = np.sin(ang).astype(np.float32).T
    COS = np.ones((128, L), np.float32)
    SIN = np.zeros((128, L), np.float32)
    for h in range(2):
        COS[64 * h:64 * h + 8] = cos
        COS[64 * h + 8:64 * h + 16] = cos
        SIN[64 * h:64 * h + 8] = sin
        SIN[64 * h + 8:64 * h + 16] = sin
    return COS, SIN


def build_consts_host():
    c = {}
    c['ones_bf'] = np.ones((128, 128), dtype=ml_dtypes.bfloat16)
    c['eps'] = np.full((128, 1), EPS, dtype=np.float32)
    blk = np.zeros((128, 128), np.float32)
    blk[:64, :64] = 1
    blk[64:, 64:] = 1
    c['blk64'] = blk.astype(ml_dtypes.bfloat16)
    P = np.zeros((128, 128), np.float32)
    for h in range(2):
        for e in range(8):
            P[64 * h + e + 8, 64 * h + e] = -1.0
            P[64 * h + e, 64 * h + e + 8] = 1.0
    c['ropeP'] = P
    cos, sin = rope_tables()
    c['cos_d'] = cos
    c['sin_d'] = sin
    k = np.arange(128)[:, None]
    q = np.arange(128)[None, :]
    m = np.concatenate([(k <= q), (k >= q)], axis=1).astype(np.float32)
    c['mask4'] = np.concatenate([m, m], axis=1).astype(ml_dtypes.bfloat16)
    c['ident_bf'] = np.eye(128, dtype=np.float32).astype(ml_dtypes.bfloat16)
    c['U_f32'] = (k <= q).astype(np.float32)
    c['ones_f32'] = np.ones((128, 128), np.float32)
    c['neg_bf'] = np.where(q < k, -32768.0, 0.0).astype(np.float32).astype(ml_dtypes.bfloat16)
    c['half'] = np.full((128, 1), 0.5, np.float32)
    c['neghalf'] = np.full((128, 1), -0.5, np.float32)
    c['one'] = np.ones((128, 1), np.float32)
    return c


CONST_SPECS = [('ones_bf', [128, 128], BF16), ('eps', [128, 1], F32), ('ident_bf', [128, 128], BF16),
               ('half', [128, 1], F32), ('neghalf', [128, 1], F32), ('one', [128, 1], F32)]
CONST_PHASE = [('blk64', [128, 128], BF16), ('ropeP', [128, 128], F32), ('mask4', [128, 512], BF16),
               ('U_f32', [128, 128], F32), ('ones_f32', [128, 128], F32), ('neg_bf', [128, 128], BF16)]
CONST_DRAM_ONLY = [('cos_d', [128, L], F32), ('sin_d', [128, L], F32)]


def build_program(phases, nvec):
    nc = bass.Bass("TRN2", target_bir_lowering=False)
    K = KB(nc)
    K.dram_in = Tile(None, "dram_in", ro=True)
    xin = nc.dram_tensor("xT", [D, L], F32, kind="ExternalInput").ap()
    yout = nc.dram_tensor("yT", [D, L], F32, kind="ExternalOutput").ap()
    vecs_d = nc.dram_tensor("vecs", [128, nvec], F32, kind="ExternalInput").ap()
    bv_d = nc.dram_tensor("bvecs", [128, 2 * 1072], F32, kind="ExternalInput").ap()
    W = {}
    for name, shp in WEIGHT_SHAPES.items():
        W[name] = nc.dram_tensor(name, shp, F32, kind="ExternalInput").ap()

    C = {}
    for name, shp, dt in CONST_SPECS:
        d_ap = nc.dram_tensor(name, shp, dt, kind="ExternalInput").ap()
        C[name] = K.alloc(name, shp[1:], dt)
        K.dma('sp', C[name], C[name].ap, K.dram_in, d_ap)
    for name, shp, dt in CONST_DRAM_ONLY:
        C[name] = nc.dram_tensor(name, shp, dt, kind="ExternalInput").ap()
    C['_phase_d'] = {name: (nc.dram_tensor(name, shp, dt, kind="ExternalInput").ap(), shp, dt)
                     for name, shp, dt in CONST_PHASE}
    C['vecs'] = K.alloc("vecs", [nvec], F32)
    K.dma('sp', C['vecs'], C['vecs'].ap, K.dram_in, vecs_d)
    C['sb_base'] = K.sb_ptr

    S = {}
    qt_d = nc.dram_tensor("QT_s", [8, 128, L], BF16, kind="Internal").ap()
    kt_d = nc.dram_tensor("KT_s", [8, 128, L], BF16, kind="Internal").ap()
    vp_d = nc.dram_tensor("VP_s", [3, 8, 128, 32, 128], BF16, kind="Internal").ap()
    ot_d = nc.dram_tensor("OT_s", [8, 128, L], BF16, kind="Internal").ap()
    S['QT'] = [[Tile(qt_d[c], f"QT{c}_{i}") for i in range(NT)] for c in range(8)]
    S['KT'] = [[Tile(kt_d[c], f"KT{c}_{i}") for i in range(NT)] for c in range(8)]
    S['VP'] = [[[Tile(vp_d[p, hp], f"VP{p}_{hp}_{s}") for s in range(2)] for hp in range(8)] for p in range(3)]
    S['OT'] = [Tile(ot_d[hp], f"OT{hp}") for hp in range(8)]
    S['OT_d'] = ot_d
    ks = "ExternalOutput" if os.environ.get('DBG_SCR') else "Internal"
    S['HT_d'] = nc.dram_tensor("HT_s", [8, 128, L], BF16, kind=ks).ap()
    S['XBC_d'] = nc.dram_tensor("XBC_s", [12, 128, L], BF16, kind=ks).ap()
    S['MA_d'] = nc.dram_tensor("MA_s", [8, 128, L], BF16, kind=ks).ap()
    S['HT'] = [Tile(S['HT_d'], f"HT{i}") for i in range(NT)]
    S['XBC'] = [[Tile(S['XBC_d'][c], f"XBC{c}_{i}") for i in range(NT)] for c in range(12)]
    S['MA'] = [[Tile(S['MA_d'][c], f"MA{c}_{i}") for i in range(NT)] for c in range(8)]

    XT = [Tile(yout, f"XT{i}") for i in range(NT)]
    for i in range(NT):
        K.dma('sp', XT[i], yout[:, i * TT:(i + 1) * TT], K.dram_in, xin[:, i * TT:(i + 1) * TT])

    for ph in phases:
        if ph[0] == 'ffn':
            layer = ph[1]
            ffn_phase(K, C, XT, layer, W['ffn_w_gate_up'][layer], W['ffn_w_down'][layer], C['vecs'],
                      VEC_OFF['ffn_norm'] + 8 * layer)
        elif ph[0] == 'att':
            li = ph[1]
            att_qkv_phase(K, C, XT, li, W['att_w_qkv'][li], S, VEC_OFF['att_norm'] + 8 * li,
                          VEC_OFF['att_q_norm'] + li, VEC_OFF['att_k_norm'] + li)
            att_core_phase(K, C, li, S)
            att_out_phase(K, C, XT, li, W['att_w_out'][li], S)
        elif ph[0] in ('rec', 'rec_a', 'rec_b'):
            li = ph[1]
            vo = rec_vo(li)
            if ph[0] != 'rec_b':
                rec_a_phase(K, C, XT, li, W['rec_w_in'][li], W['lru_w_r'][li], W['lru_w_i'][li], S, vo)
            if ph[0] != 'rec_a':
                rec_b_phase(K, C, XT, li, W['rec_w_in'][li], W['rec_w_out'][li], S, vo, bv_d, 1072 * li)
    K.op('sp', None, reads=XT + K.dbg_tiles, writes=[])
    K.emit()
    return nc, K


VEC_OFF = {'ffn_norm': 0, 'att_norm': 32, 'att_q_norm': 48, 'att_k_norm': 50}
REC_BASE = 64
REC_STRIDE = 160
REC_FIELDS = {'rec_norm': 0, 'lru_conv_w': 8, 'lru_conv_b': 40, 'lru_b_r': 48, 'lru_b_i': 56, 'lru_lambda': 64,
              'ssd_conv_w': 72, 'ssd_conv_b': 120}
NVEC = REC_BASE + 2 * REC_STRIDE


def rec_vo(li):
    return {k: REC_BASE + REC_STRIDE * li + v for k, v in REC_FIELDS.items()}


def pack_vecs(inputs):
    v = np.zeros((128, NVEC), dtype=np.float32)
    f = lambda k: np.asarray(inputs[k], dtype=np.float32)
    fn = f('ffn_norm')
    for l in range(4):
        v[:, VEC_OFF['ffn_norm'] + 8 * l: VEC_OFF['ffn_norm'] + 8 * l + 8] = fn[l].reshape(8, 128).T
    an = f('att_norm')
    for l in range(2):
        v[:, VEC_OFF['att_norm'] + 8 * l: VEC_OFF['att_norm'] + 8 * l + 8] = an[l].reshape(8, 128).T
        v[:, VEC_OFF['att_q_norm'] + l] = np.tile(f('att_q_norm')[l], 2)
        v[:, VEC_OFF['att_k_norm'] + l] = np.tile(f('att_k_norm')[l], 2)
    for l in range(2):
        vo = rec_vo(l)
        v[:, vo['rec_norm']:vo['rec_norm'] + 8] = f('rec_norm')[l].reshape(8, 128).T
        for j in range(4):
            v[:, vo['lru_conv_w'] + 8 * j:vo['lru_conv_w'] + 8 * j + 8] = f('lru_conv_w')[l, j].reshape(8, 128).T
            v[:, vo['ssd_conv_w'] + 12 * j:vo['ssd_conv_w'] + 12 * j + 12] = f('ssd_conv_w')[l, j].reshape(12, 128).T
        for k in ('lru_conv_b', 'lru_b_r', 'lru_b_i', 'lru_lambda'):
            v[:, vo[k]:vo[k] + 8] = f(k)[l].reshape(8, 128).T
        v[:, vo['ssd_conv_b']:vo['ssd_conv_b'] + 12] = f('ssd_conv_b')[l].reshape(12, 128).T
    return v


def pack_bvecs(inputs):
    f = lambda k: np.asarray(inputs[k], dtype=np.float32)
    b = np.zeros((128, 2 * 1072), np.float32)
    for l in range(2):
        row = np.concatenate([f('ssd_dt_bias')[l], f('ssd_a_log')[l], f('ssd_d')[l], f('ssd_norm')[l]])
        b[:, 1072 * l:1072 * (l + 1)] = row[None, :]
    return b


def run(inputs, phases, n_cores=8, trace=False):
    x = np.asarray(inputs['x'], dtype=np.float32)
    nc, K = build_program(phases, NVEC)
    consts = build_consts_host()
    vecs = pack_vecs(inputs)
    shared = {"vecs": vecs, "bvecs": pack_bvecs(inputs)}
    shared.update(consts)
    for name in WEIGHT_SHAPES:
        shared[name] = np.ascontiguousarray(np.asarray(inputs[name], dtype=np.float32))
    in_maps = []
    for c in range(n_cores):
        m = dict(shared)
        m["xT"] = np.ascontiguousarray(x[c].T)
        in_maps.append(m)
    res = run_bass_kernel_spmd(nc, in_maps, core_ids=list(range(n_cores)), trace=trace)
    out = np.stack([np.ascontiguousarray(res.results[c]["yT"].T) for c in range(n_cores)], axis=0)
    if trace:
        return out, res, K
    if os.environ.get('DBG_SCR'):
        return out, res
    return out


def kernel(**inputs):
    phases = [('rec', 0), ('ffn', 0), ('att', 0), ('ffn', 1), ('rec', 1), ('ffn', 2), ('att', 1), ('ffn', 3)]
    return run(inputs, phases)
```

```python
import os
import numpy as np
import ml_dtypes
import concourse.bass as bass
import concourse.mybir as mybir
from concourse.bass_utils import run_bass_kernel_spmd

F32 = mybir.dt.float32
BF16 = mybir.dt.bfloat16
AF = mybir.ActivationFunctionType
ALU = mybir.AluOpType

ENG_ATTR = {'pe': 'tensor', 'act': 'scalar', 'dve': 'vector', 'pool': 'gpsimd', 'sp': 'sync'}
SEM_LIMIT = 30000
NPOOL = 12

D = 1024
L = 4096
NT = 8
TT = 512
FH = 2816
NJ = FH // 128
EPS = 1e-6


class Op:
    __slots__ = ('eng', 'fn', 'deps', 'idx', 'dma', 'sem', 'cnt', 'snap', 'sig', 'sigval',
                 'waits', 'key')


class Tile:
    __slots__ = ('ap', 'w', 'r', 'name', 'ro', 'span')

    def __init__(self, ap, name='', ro=False):
        self.ap = ap
        self.w = None
        self.r = []
        self.name = name
        self.ro = ro
        self.span = None


class KB:
    def __init__(self, nc):
        self.nc = nc
        self.ops = []
        self.nidx = {e: 0 for e in ENG_ATTR}
        self.arena = nc.alloc_sbuf_tensor("arena", [128, 212000 // 4], F32)
        self.sb_ptr = 0
        self.dbg_tiles = []
        self.regions = []
        self.psum = [Tile(nc.alloc_psum_tensor(f"psb{i}", [128, 512], F32), f"ps{i}")
                     for i in range(8)]

    def sb_set(self, ptr):
        self.sb_ptr = ptr

    def alloc(self, name, free_shape, dtype):
        esz = 2 if dtype == BF16 else 4
        n = 1
        for s in free_shape:
            n *= s
        nbytes = (n * esz + 31) // 32 * 32
        start = self.sb_ptr
        end = start + nbytes
        assert end <= 212000, f"SBUF overflow allocating {name}: {end}"
        self.sb_ptr = end
        ap = self.arena[:, start // 4:end // 4]
        if dtype == BF16:
            ap = ap.bitcast(BF16)
        ap = ap[:, 0:n]
        if len(free_shape) == 2:
            ap = ap.rearrange("p (a b) -> p a b", a=free_shape[0])
        elif len(free_shape) == 3:
            ap = ap.rearrange("p (a b c) -> p a b c", a=free_shape[0], b=free_shape[1])
        t = Tile(ap, name)
        t.span = (start, end)
        keep = []
        for (s, e, old) in self.regions:
            if s < end and start < e:
                if old.w is not None:
                    t.r.append(old.w)
                t.r.extend(old.r)
                if s < start:
                    keep.append((s, start, old))
                if end < e:
                    keep.append((end, e, old))
            else:
                keep.append((s, e, old))
        keep.append((start, end, t))
        self.regions = keep
        return t

    def alias(self, big, name):
        t = Tile(big.ap, name)
        t.span = big.span
        if big.w is not None:
            t.r.append(big.w)
        t.r.extend(big.r)
        self.regions.append((big.span[0], big.span[1], t))
        return t

    def op(self, eng, fn, reads=(), writes=(), dma=False):
        o = Op()
        o.eng = eng
        o.fn = fn
        o.dma = dma
        o.sig = False
        o.idx = 0
        deps = {}
        reads = [t for t in reads if not t.ro]
        for t in reads:
            if t.w is not None:
                deps[id(t.w)] = t.w
        for t in writes:
            if t.w is not None:
                deps[id(t.w)] = t.w
            for x in t.r:
                deps[id(x)] = x
        o.deps = list(deps.values())
        if not dma:
            self.nidx[eng] += 1
            o.idx = self.nidx[eng]
        wset = set(id(t) for t in writes)
        for t in writes:
            t.w = o
            t.r = []
        for t in reads:
            if id(t) in wset:
                continue
            if not dma:
                t.r = [x for x in t.r if x.dma or x.eng != eng]
            t.r.append(o)
        self.ops.append(o)
        return o

    def mm(self, ps, out_ap, lt, lhsT_ap, rt, rhs_ap, start, stop, extra=()):
        return self.op('pe', lambda e: e.matmul(out_ap, lhsT=lhsT_ap, rhs=rhs_ap, start=start, stop=stop),
                       reads=[lt, rt] + list(extra), writes=[ps])

    def transpose(self, ps, out_ap, it, in_ap, idt, ident_ap):
        return self.op('pe', lambda e: e.transpose(out_ap, in_ap, ident_ap), reads=[it, idt], writes=[ps])

    def act(self, ot, out_ap, it, in_ap, func, bias=None, scale=None, reads=(), accum=None, eng='act',
            extra_w=()):
        kw = {}
        if bias is not None:
            kw['bias'] = bias
        if scale is not None:
            kw['scale'] = scale
        if accum is not None:
            kw['accum_out'] = accum
        return self.op(eng, lambda e: e.activation(out_ap, in_ap, func, **kw),
                       reads=[it] + list(reads), writes=[ot] + list(extra_w))

    def tt(self, eng, ot, out_ap, at, a_ap, bt, b_ap, op):
        return self.op(eng, lambda e: e.tensor_tensor(out_ap, a_ap, b_ap, op), reads=[at, bt], writes=[ot])

    def ts(self, eng, ot, out_ap, at, a_ap, s1, s2, op0, op1=None, reads=()):
        if op1 is None:
            return self.op(eng, lambda e: e.tensor_scalar(out_ap, a_ap, s1, None, op0),
                           reads=[at] + list(reads), writes=[ot])
        return self.op(eng, lambda e: e.tensor_scalar(out_ap, a_ap, s1, s2, op0, op1),
                       reads=[at] + list(reads), writes=[ot])

    def stt(self, ot, out_ap, at, a_ap, scalar, bt, b_ap, op0, op1, reads=()):
        return self.op('dve', lambda e: e.scalar_tensor_tensor(out_ap, a_ap, scalar, b_ap, op0, op1),
                       reads=[at, bt] + list(reads), writes=[ot])

    def copy(self, eng, ot, out_ap, it, in_ap):
        if eng == 'act':
            return self.op(eng, lambda e: e.copy(out_ap, in_ap), reads=[it], writes=[ot])
        return self.op(eng, lambda e: e.tensor_copy(out_ap, in_ap), reads=[it], writes=[ot])

    def memset(self, eng, ot, out_ap, val):
        return self.op(eng, lambda e: e.memset(out_ap, val), reads=[], writes=[ot])

    def dma(self, eng, ot, out_ap, it, in_ap, **kw):
        return self.op(eng, lambda e: e.dma_start(out=out_ap, in_=in_ap, **kw), reads=[it], writes=[ot],
                       dma=True)

    def dbg(self, name, tile, ap, dtype=F32):
        if not os.environ.get('DBG_SCR'):
            return
        d = self.nc.dram_tensor("dbg_" + name, list(ap.shape), dtype, kind="ExternalOutput").ap()
        t = Tile(d, "dbg_" + name)
        self.dma('sp', t, d, tile, ap)
        self.dbg_tiles.append(t)

    def emit(self):
        nc = self.nc
        clock = {e: {} for e in ENG_ATTR}
        pools = {}
        rr = {}
        nid = [0]

        def newslot(e):
            nid[0] += 1
            return {'sem': nc.alloc_semaphore(f"d_{e}_{nid[0]}"), 'cnt': 0, 'last': None, 'id': nid[0]}

        for o in self.ops:
            e = o.eng
            deps = list(o.deps)
            if o.dma:
                pool = pools.setdefault(e, [])
                if len(pool) < NPOOL:
                    pool.append(newslot(e))
                    i = len(pool) - 1
                    rr[e] = 0
                else:
                    i = rr[e]
                    rr[e] = (i + 1) % NPOOL
                    if pool[i]['cnt'] + 16 > SEM_LIMIT:
                        last = pool[i]['last']
                        pool[i] = newslot(e)
                        pool[i]['carry'] = last
                slot = pool[i]
                if slot['last'] is not None:
                    deps.append(slot['last'])
                slot['cnt'] += 16
                slot['last'] = o
                o.sem = slot['sem']
                o.cnt = slot['cnt']
                o.key = ('d', slot['id'])
            ck = clock[e]
            waits = []
            deps.sort(key=lambda d: -(d.cnt if d.dma else d.idx))
            for d in deps:
                if d.dma:
                    key = d.key
                    val = d.cnt
                else:
                    if d.eng == 'pe' and e == 'pe' and not o.dma:
                        continue
                    key = d.eng
                    val = d.idx
                if ck.get(key, 0) >= val:
                    continue
                waits.append(d)
                d.sig = True
                nk = dict(ck)
                for k, v in d.snap.items():
                    if nk.get(k, 0) < v:
                        nk[k] = v
                if nk.get(key, 0) < val:
                    nk[key] = val
                ck = nk
            clock[e] = ck
            o.snap = ck
            o.waits = waits

        cnt = {e: 0 for e in ENG_ATTR}
        esems = {e: [] for e in ENG_ATTR}
        for o in self.ops:
            if o.dma or not o.sig:
                continue
            c = cnt[o.eng]
            cnt[o.eng] = c + 1
            ep = c // SEM_LIMIT
            if ep >= len(esems[o.eng]):
                esems[o.eng].append(nc.alloc_semaphore(f"c_{o.eng}_{ep}"))
            o.sem = esems[o.eng][ep]
            o.sigval = c - ep * SEM_LIMIT + 1

        streams = {e: [] for e in ENG_ATTR}
        for o in self.ops:
            streams[o.eng].append(o)
        self.stats = {e: (len(streams[e]), sum(len(o.waits) for o in streams[e])) for e in ENG_ATTR}
        with nc.Block() as block:
            for e, attr in ENG_ATTR.items():
                ops = streams[e]

                def body(eng, ops=ops):
                    for o in ops:
                        for d in o.waits:
                            eng.wait_ge(d.sem, d.cnt if d.dma else d.sigval)
                        if o.fn is None:
                            continue
                        ins = o.fn(eng)
                        if o.dma:
                            ins.then_inc(o.sem, 16)
                        elif o.sig:
                            ins.then_inc(o.sem, 1)
                getattr(block, attr)(body)


def phase_consts(K, C, names):
    for n in names:
        d_ap, shp, dt = C['_phase_d'][n]
        C[n] = K.alloc("pc_" + n, shp[1:], dt)
        K.dma('sp', C[n], C[n].ap, K.dram_in, d_ap)


def xt_tile_ap(XT_ap, i):
    return XT_ap.rearrange("(c p) n -> p c n", p=128)[:, :, i * TT:(i + 1) * TT]


def load_weight_cast(K, W_ap, kchunks, ncols, name, seg=2048):
    big = K.alloc(name, [kchunks, ncols], BF16)
    pieces = {}
    Wv = W_ap.rearrange("(c p) n -> p c n", p=128)
    for kc in range(kchunks):
        for s0 in range(0, ncols, seg):
            s1 = min(ncols, s0 + seg)
            t = K.alias(big, f"{name}_{kc}_{s0}")
            K.dma('pool', t, big.ap[:, kc, s0:s1], K.dram_in, Wv[:, kc, s0:s1])
            pieces[(kc, s0 // seg)] = t
    return big.ap, pieces, seg


def rms_stats(K, C, xt, ps_stats, sq, lnv, rstd):
    for c in range(8):
        s = sq[c % 2]
        K.act(s, s.ap[:, :], xt, xt.ap[:, c, :], AF.Square)
        K.mm(ps_stats, ps_stats.ap[:, :], C['ones_bf'], C['ones_bf'].ap[:, :], s, s.ap[:, :],
             start=(c == 0), stop=(c == 7))
    K.act(lnv, lnv.ap[:, :], ps_stats, ps_stats.ap[:, :], AF.Ln, bias=C['eps'].ap[:, 0:1], scale=1.0 / D,
          reads=[C['eps']])
    K.act(rstd, rstd.ap[:, :], lnv, lnv.ap[:, :], AF.Exp, scale=-0.5)


def ffn_phase(K, C, XT, layer, Wgu_ap, Wd_ap, gvec, gcol):
    K.sb_set(C['sb_base'])
    wgu, pgu, seg = load_weight_cast(K, Wgu_ap, 8, 2 * FH, f"wgu{layer}")
    wd, pd, segd = load_weight_cast(K, Wd_ap, NJ, D, f"wd{layer}", seg=1024)
    xts = [K.alloc(f"f_xt{b}", [8, TT], F32) for b in range(2)]
    hT = K.alloc("f_hT", [8, TT], BF16)
    sq = [K.alloc(f"f_sq{b}", [TT], BF16) for b in range(2)]
    lnv = K.alloc("f_lnv", [TT], F32)
    rstd = K.alloc("f_rstd", [TT], F32)
    sg = [K.alloc("f_sg", [TT], F32)] * 2
    aT = K.alloc("f_aT", [NJ, TT], BF16)
    psS = K.psum[0]
    psG = [K.psum[1], K.psum[2]]
    psU = [K.psum[3], K.psum[4]]
    psO = [K.psum[5], K.psum[6]]

    def load(i):
        K.dma('sp', xts[i % 2], xts[i % 2].ap, XT[i], xt_tile_ap(XT[i].ap, i))

    load(0)
    for i in range(NT):
        xt = xts[i % 2]
        if i + 1 < NT:
            load(i + 1)
        rstd_pow(K, C, xt, psS, sq, rstd, lnv)
        for c in range(8):
            K.stt(hT, hT.ap[:, c, :], xt, xt.ap[:, c, :], gvec.ap[:, gcol + c:gcol + c + 1], rstd, rstd.ap[:, :],
                  ALU.mult, ALU.mult, reads=[gvec])
        for j in range(NJ):
            pg = psG[j % 2]
            pu = psU[j % 2]
            for kc in range(8):
                c0 = j * 128
                K.mm(pg, pg.ap[:, :], pgu[(kc, c0 // seg)], wgu[:, kc, c0:c0 + 128], hT, hT.ap[:, kc, :],
                     start=(kc == 0), stop=(kc == 7))
            for kc in range(8):
                c0 = FH + j * 128
                K.mm(pu, pu.ap[:, :], pgu[(kc, c0 // seg)], wgu[:, kc, c0:c0 + 128], hT, hT.ap[:, kc, :],
                     start=(kc == 0), stop=(kc == 7))
            s = sg[j % 2]
            K.act(s, s.ap[:, :], pg, pg.ap[:, :], AF.Silu)
            K.tt('dve', aT, aT.ap[:, j, :], s, s.ap[:, :], pu, pu.ap[:, :], ALU.mult)
        for m in range(8):
            po = psO[m % 2]
            for j in range(NJ):
                K.mm(po, po.ap[:, :], pd[(j, 0)], wd[:, j, m * 128:(m + 1) * 128], aT, aT.ap[:, j, :],
                     start=(j == 0), stop=(j == NJ - 1))
            K.tt('dve', xt, xt.ap[:, m, :], po, po.ap[:, :], xt, xt.ap[:, m, :], ALU.add)
        K.dma('sp', XT[i], xt_tile_ap(XT[i].ap, i), xt, xt.ap)


PATS = (1, 4, 16)


def att_qkv_phase(K, C, XT, li, Wqkv_ap, S, gcol, qcol, kcol):
    K.sb_set(C['sb_base'])
    phase_consts(K, C, ['blk64', 'ropeP'])
    w, pw, seg = load_weight_cast(K, Wqkv_ap, 8, 3 * D, f"wqkv{li}", seg=1024)
    xts = [K.alloc(f"a_xt{b}", [8, TT], F32) for b in range(2)]
    hTs = K.alloc("a_hTs", [8, 2048], BF16)
    sq = [K.alloc(f"a_sq{b}", [TT], BF16) for b in range(2)]
    lnv = K.alloc("a_lnv", [TT], F32)
    rstd = K.alloc("a_rstd", [TT], F32)
    cs = [(K.alloc(f"a_cos{b}", [TT], F32), K.alloc(f"a_sin{b}", [TT], F32)) for b in range(2)]
    sets = []
    for b in range(2):
        sets.append(dict(qs=K.alloc(f"a_qs{b}", [TT], F32), hsq=K.alloc(f"a_hsq{b}", [TT], BF16),
                         lnh=K.alloc(f"a_lnh{b}", [TT], F32), rsh=K.alloc(f"a_rsh{b}", [TT], F32),
                         qn=K.alloc(f"a_qn{b}", [TT], F32), t1=K.alloc(f"a_t1{b}", [TT], F32),
                         t2=K.alloc(f"a_t2{b}", [TT], F32), qo=K.alloc(f"a_qo{b}", [TT], BF16)))
    vbuf = K.alloc("a_vbuf", [16, D], BF16)
    psS = K.psum[0]
    psQ = [K.psum[1], K.psum[2]]
    psH = [K.psum[3], K.psum[7]]
    psP = K.psum[4]
    psV = [K.psum[5], K.psum[6]]
    vecs = C['vecs']

    def load(i):
        K.dma('sp', xts[i % 2], xts[i % 2].ap, XT[i], xt_tile_ap(XT[i].ap, i))
        co, si = cs[i % 2]
        K.dma('sp', co, co.ap, K.dram_in, C['cos_d'][:, i * TT:(i + 1) * TT])
        K.dma('sp', si, si.ap, K.dram_in, C['sin_d'][:, i * TT:(i + 1) * TT])

    load(0)
    vcnt = 0
    for i in range(NT):
        xt = xts[i % 2]
        co, si = cs[i % 2]
        if i + 1 < NT:
            load(i + 1)
        rms_stats(K, C, xt, psS, sq, lnv, rstd)
        lc = (i % 4) * TT
        for c in range(8):
            K.stt(hTs, hTs.ap[:, c, lc:lc + TT], xt, xt.ap[:, c, :], vecs.ap[:, gcol + c:gcol + c + 1], rstd,
                  rstd.ap[:, :], ALU.mult, ALU.mult, reads=[vecs])
        for c in range(16):
            st = sets[c % 2]
            ps = psQ[c % 2]
            ph = psH[c % 2]
            for kc in range(8):
                c0 = c * 128
                K.mm(ps, ps.ap[:, :], pw[(kc, c0 // seg)], w[:, kc, c0:c0 + 128], hTs, hTs.ap[:, kc, lc:lc + TT],
                     start=(kc == 0), stop=(kc == 7))
            K.copy('act', st['qs'], st['qs'].ap[:, :], ps, ps.ap[:, :])
            K.act(st['hsq'], st['hsq'].ap[:, :], ps, ps.ap[:, :], AF.Square)
            K.mm(ph, ph.ap[:, :], C['blk64'], C['blk64'].ap[:, :], st['hsq'], st['hsq'].ap[:, :], True, True)
            K.act(st['lnh'], st['lnh'].ap[:, :], ph, ph.ap[:, :], AF.Ln, bias=C['eps'].ap[:, 0:1], scale=1.0 / 64,
                  reads=[C['eps']])
            K.act(st['rsh'], st['rsh'].ap[:, :], st['lnh'], st['lnh'].ap[:, :], AF.Exp, scale=-0.5)
            gc = (qcol if c < 8 else kcol)
            K.stt(st['qn'], st['qn'].ap[:, :], st['qs'], st['qs'].ap[:, :], vecs.ap[:, gc:gc + 1], st['rsh'],
                  st['rsh'].ap[:, :], ALU.mult, ALU.mult, reads=[vecs])
            K.mm(psP, psP.ap[:, :], C['ropeP'], C['ropeP'].ap[:, :], st['qn'], st['qn'].ap[:, :], True, True)
            K.tt('pool', st['t1'], st['t1'].ap[:, :], st['qn'], st['qn'].ap[:, :], co, co.ap[:, :], ALU.mult)
            K.tt('dve', st['t2'], st['t2'].ap[:, :], psP, psP.ap[:, :], si, si.ap[:, :], ALU.mult)
            K.tt('dve', st['qo'], st['qo'].ap[:, :], st['t1'], st['t1'].ap[:, :], st['t2'], st['t2'].ap[:, :], ALU.add)
            dst = S['QT'][c][i] if c < 8 else S['KT'][c - 8][i]
            K.dma('sp', dst, dst.ap[:, i * TT:(i + 1) * TT], st['qo'], st['qo'].ap[:, :])
        if i % 4 == 3:
            s = i // 4
            for pi, d in enumerate(PATS):
                for lb in range(16):
                    nl = lb // d
                    r = lb % d
                    start = nl * 128 * d + r
                    for half in range(2):
                        pv = psV[vcnt % 2]
                        vcnt += 1
                        for kc in range(8):
                            c0 = 2 * D + half * 512
                            K.mm(pv, pv.ap[:, :], hTs, hTs.ap[:, kc, start:start + 127 * d + 1:d],
                                 pw[(kc, c0 // seg)], w[:, kc, c0:c0 + 512], start=(kc == 0), stop=(kc == 7))
                        eng = 'act' if half == 0 else 'dve'
                        K.copy(eng, vbuf, vbuf.ap[:, lb, half * 512:(half + 1) * 512], pv, pv.ap[:, :])
                for hp in range(8):
                    dst = S['VP'][pi][hp][s]
                    K.dma('sp', dst, dst.ap[:, 16 * s:16 * s + 16, :], vbuf, vbuf.ap[:, :, hp * 128:(hp + 1) * 128])


def att_core_phase(K, C, li, S):
    K.sb_set(C['sb_base'])
    phase_consts(K, C, ['mask4'])
    qk = [(K.alloc(f"c_qt{b}", [L], BF16), K.alloc(f"c_kt{b}", [L], BF16)) for b in range(2)]
    vps = [[K.alloc(f"c_vp{b}_{p}", [32, 128], BF16) for p in range(3)] for b in range(2)]
    accN = K.alloc("c_accN", [L], F32)
    accD = K.alloc("c_accD", [L], F32)
    pts = [[K.alloc(f"c_pt{h}_{k}", [TT], BF16) for k in range(3)] for h in range(2)]
    ot = K.alloc("c_ot", [L], BF16)
    rec = [K.alloc(f"c_rec{b}", [TT], F32) for b in range(2)]
    psSb = [[K.psum[0], K.psum[1]], [K.psum[2], K.psum[3]]]
    psOb = [K.psum[4], K.psum[5]]
    psDb = [K.psum[6], K.psum[7]]
    mask = C['mask4']
    ones64 = C['ones_bf']

    def load(hp):
        qt, kt = qk[hp % 2]
        K.dma('sp', qt, qt.ap, S['QT'][hp][0], S['QT'][hp][0].ap)
        for t in S['QT'][hp][1:]:
            qt.r
        K.dma('sp', kt, kt.ap, S['KT'][hp][0], S['KT'][hp][0].ap)
        for p in range(3):
            v = vps[hp % 2][p]
            K.dma('sp', v, v.ap, S['VP'][p][hp][0], S['VP'][p][hp][0].ap)

    def load_multi(dst, dst_ap, tiles, src_ap):
        K.op('sp', lambda e: e.dma_start(out=dst_ap, in_=src_ap), reads=list(tiles), writes=[dst], dma=True)

    def load2(hp):
        qt, kt = qk[hp % 2]
        load_multi(qt, qt.ap, S['QT'][hp], S['QT'][hp][0].ap)
        load_multi(kt, kt.ap, S['KT'][hp], S['KT'][hp][0].ap)
        for p in range(3):
            v = vps[hp % 2][p]
            load_multi(v, v.ap, S['VP'][p][hp], S['VP'][p][hp][0].ap)

    load2(0)
    octr = 0
    for hp in range(8):
        if hp + 1 < 8:
            load2(hp + 1)
        qt, kt = qk[hp % 2]
        groups = []
        for pi, d in enumerate(PATS):
            nb = 32 // d
            for r in range(d):
                for g in range((nb + 1) // 2):
                    units = [m for m in (2 * g, 2 * g + 1) if m < nb]
                    groups.append((pi, d, r, nb, units))

        def emit_S(gi):
            pi, d, r, nb, units = groups[gi]
            qv = qt.ap.rearrange("p (m d) -> p d m", d=d)
            kv = kt.ap.rearrange("p (m d) -> p d m", d=d)
            for h in range(2):
                ps = psSb[h][gi % 2]
                rows = slice(64 * h, 64 * h + 64)
                for ui, m in enumerate(units):
                    ncols = 256 if m < nb - 1 else 128
                    K.mm(ps, ps.ap[:, ui * 256:ui * 256 + ncols], kt, kv[rows, r, 128 * m:128 * m + 128],
                         qt, qv[rows, r, 128 * m:128 * m + ncols], True, True)

        def emit_E(gi):
            pi, d, r, nb, units = groups[gi]
            valid = 0
            for ui, m in enumerate(units):
                valid = ui * 256 + (256 if m < nb - 1 else 128)
            for h in range(2):
                ps = psSb[h][gi % 2]
                pt = pts[h][gi % 3]
                K.act(pt, pt.ap[:, 0:valid], ps, ps.ap[:, 0:valid], AF.Exp, scale=0.125)
                K.tt('pool', pt, pt.ap[:, 0:valid], pt, pt.ap[:, 0:valid], mask, mask.ap[:, 0:valid], ALU.mult)

        state = {'ob': 0, 'obank': 0}

        def emit_PV(gi):
            nonlocal octr
            pi, d, r, nb, units = groups[gi]
            vp = vps[hp % 2][pi]
            for ui, m in enumerate(units):
                blk = m * d + r
                pso = psOb[octr % 2]
                psd = psDb[octr % 2]
                oc = state['ob'] * 128
                for h in range(2):
                    rows = slice(64 * h, 64 * h + 64)
                    pt = pts[h][gi % 3]
                    cur = pt.ap[:, ui * 256:ui * 256 + 128]
                    if m > 0:
                        if ui == 1:
                            ppt = pt
                            prev = pt.ap[:, 128:256]
                        else:
                            ppt = pts[h][(gi - 1) % 3]
                            prev = ppt.ap[:, 256 + 128:512]
                    for (pp, lcur, lprev) in ((pso, vp.ap[:, blk, 64 * h:64 * h + 64],
                                               vp.ap[:, blk - d, 64 * h:64 * h + 64] if m > 0 else None),
                                              (psd, ones64.ap[:, 0:64], ones64.ap[:, 0:64])):
                        lt = vp if pp is pso else ones64
                        K.mm(pp, pp.ap[rows, oc:oc + 128], lt, lcur, pt, cur, True, m == 0)
                        if m > 0:
                            K.mm(pp, pp.ap[rows, oc:oc + 128], lt, lprev, ppt, prev, False, True)
                state['ob'] += 1
                last_of_class = (m == nb - 1)
                flush = state['ob'] == 4 or (last_of_class and d != 16) or (last_of_class and d == 16 and r % 2 == 1)
                if flush:
                    nblk = state['ob']
                    accNv = accN.ap.rearrange("p (m d) -> p d m", d=d)
                    accDv = accD.ap.rearrange("p (m d) -> p d m", d=d)
                    if d == 16:
                        av = lambda a: a[:, r - 1:r + 1, :]
                        pv_ = lambda p: p.ap.rearrange("p (a b) -> p a b", a=2)
                    else:
                        n0 = m - nblk + 1
                        av = lambda a: a[:, r, 128 * n0:128 * (m + 1)]
                        pv_ = lambda p: p.ap[:, 0:128 * nblk]
                    if d == 1:
                        K.copy('dve', accN, av(accNv), pso, pv_(pso))
                        K.copy('dve', accD, av(accDv), psd, pv_(psd))
                    else:
                        K.tt('dve', accN, av(accNv), pso, pv_(pso), accN, av(accNv), ALU.add)
                        K.tt('dve', accD, av(accDv), psd, pv_(psd), accD, av(accDv), ALU.add)
                    state['ob'] = 0
                    octr += 1

        ng = len(groups)
        emit_S(0)
        for gi in range(ng):
            if gi + 1 < ng:
                emit_S(gi + 1)
            emit_E(gi)
            emit_PV(gi)
        for i in range(NT):
            rc = rec[i % 2]
            cols = slice(i * TT, (i + 1) * TT)
            K.op('dve', lambda e, rc=rc, cols=cols: e.reciprocal(rc.ap[:, :], accD.ap[:, cols]), reads=[accD],
                 writes=[rc])
            K.tt('pool', ot, ot.ap[:, cols], accN, accN.ap[:, cols], rc, rc.ap[:, :], ALU.mult)
        K.dma('sp', S['OT'][hp], S['OT'][hp].ap, ot, ot.ap)


def att_out_phase(K, C, XT, li, Wo_ap, S):
    K.sb_set(C['sb_base'])
    w, pw, seg = load_weight_cast(K, Wo_ap, 8, D, f"wo{li}", seg=1024)
    xts = [K.alloc(f"o_xt{b}", [8, TT], F32) for b in range(2)]
    ots = [K.alloc(f"o_ot{b}", [8, TT], BF16) for b in range(2)]
    psO = [K.psum[0], K.psum[1], K.psum[2], K.psum[3]]

    def load(i):
        K.dma('sp', xts[i % 2], xts[i % 2].ap, XT[i], xt_tile_ap(XT[i].ap, i))
        K.op('sp', lambda e, i=i: e.dma_start(out=ots[i % 2].ap,
                                             in_=S['OT_d'].rearrange("c p n -> p c n")[:, :, i * TT:(i + 1) * TT]),
             reads=S['OT'], writes=[ots[i % 2]], dma=True)

    load(0)
    for i in range(NT):
        if i + 1 < NT:
            load(i + 1)
        xt = xts[i % 2]
        o = ots[i % 2]
        for m in range(8):
            po = psO[m % 4]
            for kc in range(8):
                K.mm(po, po.ap[:, :], pw[(kc, 0)], w[:, kc, m * 128:(m + 1) * 128], o, o.ap[:, kc, :],
                     start=(kc == 0), stop=(kc == 7))
            K.tt('dve', xt, xt.ap[:, m, :], po, po.ap[:, :], xt, xt.ap[:, m, :], ALU.add)
        K.dma('sp', XT[i], xt_tile_ap(XT[i].ap, i), xt, xt.ap)


def rstd_pow(K, C, xt, ps_stats, sq, rstd, tmp):
    for c in range(8):
        s = sq[c % 2]
        K.act(s, s.ap[:, :], xt, xt.ap[:, c, :], AF.Square)
        K.mm(ps_stats, ps_stats.ap[:, :], C['ones_bf'], C['ones_bf'].ap[:, :], s, s.ap[:, :],
             start=(c == 0), stop=(c == 7))
    K.act(tmp, tmp.ap[:, :], ps_stats, ps_stats.ap[:, :], AF.Ln, bias=C['eps'].ap[:, 0:1], scale=1.0 / D,
          reads=[C['eps']])
    K.act(rstd, rstd.ap[:, :], tmp, tmp.ap[:, :], AF.Exp, scale=-0.5)


def rec_a_phase(K, C, XT, li, Win_ap, Wr_ap, Wi_ap, S, vo):
    K.sb_set(C['sb_base'])
    vecs = C['vecs']
    NW = 3584
    wbig = K.alloc(f"ra_w{li}", [8, NW], BF16)
    w = wbig.ap
    Wv = Win_ap.rearrange("(c p) n -> p c n", p=128)
    pw = {}
    for kc in range(8):
        for (d0, s0, n) in ((0, 0, 1024), (1024, 1024, 1024), (2048, 3072, 1536)):
            t = K.alias(wbig, f"ra_w{li}_{kc}_{d0}")
            K.dma('pool', t, w[:, kc, d0:d0 + n], K.dram_in, Wv[:, kc, s0:s0 + n])
            pw[(kc, d0)] = t

    def wpiece(kc, col):
        return pw[(kc, 0 if col < 1024 else (1024 if col < 2048 else 2048))]

    wr = K.alloc("ra_wr", [8, 128], BF16)
    wi = K.alloc("ra_wi", [8, 128], BF16)
    K.memset('pool', wr, wr.ap, 0.0)
    K.memset('pool', wi, wi.ap, 0.0)
    for (wt, src) in ((wr, Wr_ap), (wi, Wi_ap)):
        sv = src.rearrange("(c two) i j -> two i c j", two=2)
        for h in range(2):
            K.dma('pool', wt, wt.ap[64 * h:64 * h + 64, :, 64 * h:64 * h + 64], K.dram_in, sv[h])
    diag = K.alloc("ra_diag", [20 * 4, 128], BF16)
    for c in range(20):
        for j in range(4):
            col = (vo['lru_conv_w'] + j * 8 + c) if c < 8 else (vo['ssd_conv_w'] + j * 12 + (c - 8))
            K.ts('pool', diag, diag.ap[:, c * 4 + j, :], C['ident_bf'], C['ident_bf'].ap[:, :],
                 vecs.ap[:, col:col + 1], None, ALU.mult, reads=[vecs])
    dv = K.alloc("ra_dv", [32], F32)
    sp = [K.alloc(f"ra_sp{k}", [8], F32) for k in range(4)]
    K.ts('dve', dv, dv.ap[:, 0:8], vecs, vecs.ap[:, vo['lru_b_r']:vo['lru_b_r'] + 8], 0.5, None, ALU.mult)
    K.ts('dve', dv, dv.ap[:, 8:16], vecs, vecs.ap[:, vo['lru_b_i']:vo['lru_b_i'] + 8], 0.5, None, ALU.mult)
    lam = vecs.ap[:, vo['lru_lambda']:vo['lru_lambda'] + 8]
    K.act(sp[0], sp[0].ap[:, :], vecs, lam, AF.Exp, scale=-1.0)
    K.ts('dve', sp[1], sp[1].ap[:, :], sp[0], sp[0].ap[:, :], 1.0, None, ALU.add)
    K.ts('dve', sp[2], sp[2].ap[:, :], sp[1], sp[1].ap[:, :], -1.0, 1e-30, ALU.add, ALU.max)
    K.op('dve', lambda e: e.reciprocal(sp[2].ap[:, :], sp[2].ap[:, :]), reads=[sp[2]], writes=[sp[2]])
    K.tt('dve', sp[2], sp[2].ap[:, :], sp[2], sp[2].ap[:, :], sp[0], sp[0].ap[:, :], ALU.mult)
    K.act(sp[3], sp[3].ap[:, :], sp[1], sp[1].ap[:, :], AF.Ln)
    K.tt('dve', sp[3], sp[3].ap[:, :], sp[3], sp[3].ap[:, :], sp[2], sp[2].ap[:, :], ALU.mult)
    K.ts('dve', dv, dv.ap[:, 16:24], sp[3], sp[3].ap[:, :], -8.0, None, ALU.mult)
    K.ts('dve', dv, dv.ap[:, 24:32], sp[3], sp[3].ap[:, :], -4.0, None, ALU.mult)

    rawb = [K.alloc(f"ra_raw{c}", [516], BF16) for c in range(20)]
    for c in range(20):
        K.memset('pool', rawb[c], rawb[c].ap[:, 0:4], 0.0)
    hcar = K.alloc("ra_hcar", [8], F32)
    xt = K.alloc("ra_xt", [8, TT], F32)
    hT = K.alloc("ra_hT", [8, TT], BF16)
    sq = [K.alloc(f"ra_sq{b}", [TT], BF16) for b in range(2)]
    rtmp = K.alloc("ra_rtmp", [TT], F32)
    rstd = K.alloc("ra_rstd", [TT], F32)
    gl = K.alloc("ra_gl", [8, TT], BF16)
    xo = [K.alloc(f"ra_xo{b}", [TT], BF16) for b in range(2)]
    names = ['xc', 'thr', 'thi', 'a', 'a2', 'om', 'sr', 't2', 'u', 'hl']
    sets = [{n: K.alloc(f"ra_{n}{b}", [TT], F32) for n in names} for b in range(2)]
    for b in range(2):
        sets[b]['xcb'] = K.alloc(f"ra_xcb{b}", [TT], BF16)
        sets[b]['oa'] = K.alloc(f"ra_oa{b}", [TT], BF16)
    psS = K.psum[0]
    psP = [K.psum[1], K.psum[2]]
    psC = [K.psum[3], K.psum[4]]
    psR = K.psum[5]
    psI = K.psum[6]

    def proj(ps, col0, ncol=128):
        for kc in range(8):
            K.mm(ps, ps.ap[0:ncol, :], wpiece(kc, col0), w[:, kc, col0:col0 + ncol], hT, hT.ap[:, kc, :],
                 start=(kc == 0), stop=(kc == 7))

    def conv(pc, c, ps, first):
        rb = rawb[c]
        if not first:
            K.copy('pool', rb, rb.ap[:, 0:4], rb, rb.ap[:, 512:516])
        K.copy('act', rb, rb.ap[:, 4:516], ps, ps.ap[:, :])
        for j in range(4):
            K.mm(pc, pc.ap[:, :], diag, diag.ap[:, c * 4 + j, :], rb, rb.ap[:, 1 + j:1 + j + 512],
                 start=(j == 0), stop=(j == 3))

    for i in range(NT):
        K.dma('sp', xt, xt.ap, XT[i], xt_tile_ap(XT[i].ap, i))
        rstd_pow(K, C, xt, psS, sq, rstd, rtmp)
        for c in range(8):
            K.stt(hT, hT.ap[:, c, :], xt, xt.ap[:, c, :], vecs.ap[:, vo['rec_norm'] + c:vo['rec_norm'] + c + 1],
                  rstd, rstd.ap[:, :], ALU.mult, ALU.mult, reads=[vecs])
        K.dma('sp', S['HT'][i], S['HT_d'].rearrange("c p n -> p c n")[:, :, i * TT:(i + 1) * TT], hT, hT.ap)
        for c in range(8):
            ps = psP[c % 2]
            proj(ps, 1024 + c * 128)
            K.act(gl, gl.ap[:, c, :], ps, ps.ap[:, :], AF.Gelu_apprx_tanh)
        for cc in range(12):
            ps = psP[cc % 2]
            pc = psC[cc % 2]
            proj(ps, 2048 + cc * 128)
            conv(pc, 8 + cc, ps, i == 0)
            o = xo[cc % 2]
            bcol = vo['ssd_conv_b'] + cc
            K.act(o, o.ap[:, :], pc, pc.ap[:, :], AF.Silu, bias=vecs.ap[:, bcol:bcol + 1], reads=[vecs])
            K.dma('sp', S['XBC'][cc][i], S['XBC'][cc][i].ap[:, i * TT:(i + 1) * TT], o, o.ap[:, :])
        for c in range(8):
            st = sets[c % 2]
            ps = psP[c % 2]
            pc = psC[c % 2]
            proj(ps, c * 128)
            conv(pc, c, ps, i == 0)
            bcol = vo['lru_conv_b'] + c
            K.act(st['xc'], st['xc'].ap[:, :], pc, pc.ap[:, :], AF.Identity, bias=vecs.ap[:, bcol:bcol + 1],
                  reads=[vecs])
            K.copy('dve', st['xcb'], st['xcb'].ap[:, :], st['xc'], st['xc'].ap[:, :])
            K.mm(psR, psR.ap[:, :], wr, wr.ap[:, c, :], st['xcb'], st['xcb'].ap[:, :], True, True)
            K.mm(psI, psI.ap[:, :], wi, wi.ap[:, c, :], st['xcb'], st['xcb'].ap[:, :], True, True)
            K.act(st['thr'], st['thr'].ap[:, :], psR, psR.ap[:, :], AF.Tanh, bias=dv.ap[:, c:c + 1], scale=0.5,
                  reads=[dv])
            K.act(st['thi'], st['thi'].ap[:, :], psI, psI.ap[:, :], AF.Tanh, bias=dv.ap[:, 8 + c:9 + c], scale=0.5,
                  reads=[dv])
            K.act(st['a'], st['a'].ap[:, :], st['thr'], st['thr'].ap[:, :], AF.Exp, bias=dv.ap[:, 24 + c:25 + c],
                  scale=dv.ap[:, 24 + c:25 + c], reads=[dv])
            K.act(st['a2'], st['a2'].ap[:, :], st['thr'], st['thr'].ap[:, :], AF.Exp, bias=dv.ap[:, 16 + c:17 + c],
                  scale=dv.ap[:, 16 + c:17 + c], reads=[dv])
            K.ts('dve', st['om'], st['om'].ap[:, :], st['a2'], st['a2'].ap[:, :], -1.0, 1.0, ALU.mult, ALU.add)
            K.act(st['sr'], st['sr'].ap[:, :], st['om'], st['om'].ap[:, :], AF.Ln)
            K.act(st['sr'], st['sr'].ap[:, :], st['sr'], st['sr'].ap[:, :], AF.Exp, scale=0.5)
            K.stt(st['t2'], st['t2'].ap[:, :], st['thi'], st['thi'].ap[:, :], 1.0, st['xc'], st['xc'].ap[:, :],
                  ALU.add, ALU.mult)
            K.stt(st['u'], st['u'].ap[:, :], st['t2'], st['t2'].ap[:, :], 0.5, st['sr'], st['sr'].ap[:, :],
                  ALU.mult, ALU.mult)
            init = 0.0 if i == 0 else hcar.ap[:, c:c + 1]
            K.op('dve', lambda e, st=st, init=init: e.tensor_tensor_scan(st['hl'].ap[:, :], st['a'].ap[:, :],
                                                                       st['u'].ap[:, :], init, ALU.mult, ALU.add),
                 reads=[st['a'], st['u'], hcar], writes=[st['hl']])
            K.copy('pool', hcar, hcar.ap[:, c:c + 1], st['hl'], st['hl'].ap[:, 511:512])
            K.tt('dve', st['oa'], st['oa'].ap[:, :], st['hl'], st['hl'].ap[:, :], gl, gl.ap[:, c, :], ALU.mult)
            K.dma('sp', S['MA'][c][i], S['MA'][c][i].ap[:, i * TT:(i + 1) * TT], st['oa'], st['oa'].ap[:, :])


def rec_b_phase(K, C, XT, li, Win_ap, Wout_ap, S, vo, bv_d, bo):
    K.sb_set(C['sb_base'])
    phase_consts(K, C, ['U_f32', 'ones_f32', 'neg_bf'])
    wo, pwo, sego = load_weight_cast(K, Wout_ap, 16, D, f"rb_wo{li}", seg=1024)
    wzbig = K.alloc(f"rb_wz{li}", [8, 1040], BF16)
    wz = wzbig.ap
    Wv = Win_ap.rearrange("(c p) n -> p c n", p=128)
    pwz = {}
    for kc in range(8):
        t = K.alias(wzbig, f"rb_wz{li}_{kc}")
        K.dma('pool', t, wz[:, kc, 0:1024], K.dram_in, Wv[:, kc, 2048:3072])
        K.dma('pool', t, wz[:, kc, 1024:1040], K.dram_in, Wv[:, kc, 4608:4624])
        pwz[kc] = t
    bv = K.alloc("rb_bv", [1072], F32)
    K.dma('sp', bv, bv.ap, K.dram_in, bv_d[:, bo:bo + 1072])
    Aneg = K.alloc("rb_A", [16], F32)
    K.act(Aneg, Aneg.ap[:, :], bv, bv.ap[:, 16:32], AF.Exp)
    K.ts('dve', Aneg, Aneg.ap[:, :], Aneg, Aneg.ap[:, :], -1.0, None, ALU.mult)
    DI = K.alloc("rb_DI", [16, 128], BF16)
    for h in range(16):
        K.ts('pool', DI, DI.ap[:, h, :], C['ident_bf'], C['ident_bf'].ap[:, :], bv.ap[:, 32 + h:33 + h], None,
             ALU.mult, reads=[bv])
    xts = [K.alloc("rb_xt", [8, TT], F32)] * 2
    hTs = [K.alloc(f"rb_hT{b}", [8, TT], BF16) for b in range(2)]
    xbcs = [K.alloc(f"rb_xbc{b}", [12, TT], BF16) for b in range(2)]
    mAs = [K.alloc("rb_mA", [8, TT], BF16)] * 2
    mB = K.alloc("rb_mB", [8, TT], BF16)
    ytok = K.alloc("rb_ytok", [4, D], F32)
    Sst = K.alloc("rb_S", [D], F32)
    prevb = K.alloc("rb_prev", [D], BF16)
    rhsR = K.alloc("rb_rhsR", [16 * 128], F32)
    Eexp = K.alloc("rb_E", [16 * 128], F32)
    MT = K.alloc("rb_MT", [16, 128], BF16)
    xsb = K.alloc("rb_xsb", [D], BF16)
    xsw = K.alloc("rb_xsw", [D], BF16)
    Btok = K.alloc("rb_Btok", [256], BF16)
    tmpF = K.alloc("rb_tmpF", [512], F32)
    bcw = K.alloc("rb_bcw", [D], F32)
    bce = K.alloc("rb_bce", [D], F32)
    bcc = K.alloc("rb_bcc", [D], F32)
    small = {n: K.alloc(f"rb_{n}", [16], F32) for n in ['v', 'e', 'dt', 'lndt', 'adt', 'cs', 'bE', 'wx', 'w', 'cd',
                                                        'ecs']}
    sz = K.alloc("rb_sz", [512], F32)
    yz = K.alloc("rb_yz", [512], F32)
    ysq = K.alloc("rb_ysq", [512], BF16)
    ss = K.alloc("rb_ss", [8], F32)
    ss2 = K.alloc("rb_ss2", [8], F32)
    yb = K.alloc("rb_yb", [512], BF16)
    ps_small = K.psum[0]
    ps_R = K.psum[1]
    ps_X = K.psum[2]
    ps_G = K.psum[3]
    ps_Y = K.psum[4]
    ps_F = K.psum[5]
    ps_St = K.psum[6]
    ps_Z = K.psum[7]
    identb = C['ident_bf']

    def load(i):
        b = i % 2
        K.dma('sp', hTs[b], hTs[b].ap, S['HT'][i], S['HT_d'].rearrange("c p n -> p c n")[:, :, i * TT:(i + 1) * TT])
        K.op('sp', lambda e, b=b, i=i: e.dma_start(
            out=xbcs[b].ap, in_=S['XBC_d'].rearrange("c p n -> p c n")[:, :, i * TT:(i + 1) * TT]),
            reads=[S['XBC'][cc][i] for cc in range(12)], writes=[xbcs[b]], dma=True)

    def load_single(i):
        b = i % 2
        K.dma('sp', xts[b], xts[b].ap, XT[i], xt_tile_ap(XT[i].ap, i))
        K.op('sp', lambda e, b=b, i=i: e.dma_start(
            out=mAs[b].ap, in_=S['MA_d'].rearrange("c p n -> p c n")[:, :, i * TT:(i + 1) * TT]),
            reads=[S['MA'][c][i] for c in range(8)], writes=[mAs[b]], dma=True)

    load(0)
    for i in range(NT):
        b = i % 2
        xt, hT, xbc, mA = xts[b], hTs[b], xbcs[b], mAs[b]
        load_single(i)
        if i + 1 < NT:
            load(i + 1)
        RB = float(os.environ.get('RB_STOP', '99'))
        for q in range(4):
            cg = 4 * i + q
            tc = slice(128 * q, 128 * q + 128)
            sm = small
            if RB < 2:
                continue
            for kc in range(8):
                K.mm(ps_small, ps_small.ap[:, 0:16], hT, hT.ap[:, kc, tc], pwz[kc], wz[:, kc, 1024:1040],
                     start=(kc == 0), stop=(kc == 7))
            K.tt('dve', sm['v'], sm['v'].ap[:, :], ps_small, ps_small.ap[:, 0:16], bv, bv.ap[:, 0:16], ALU.add)
            K.act(sm['e'], sm['e'].ap[:, :], sm['v'], sm['v'].ap[:, :], AF.Exp)
            K.act(sm['dt'], sm['dt'].ap[:, :], sm['e'], sm['e'].ap[:, :], AF.Ln, bias=C['one'].ap[:, 0:1],
                  reads=[C['one']])
            K.act(sm['lndt'], sm['lndt'].ap[:, :], sm['dt'], sm['dt'].ap[:, :], AF.Ln)
            K.tt('dve', sm['adt'], sm['adt'].ap[:, :], sm['dt'], sm['dt'].ap[:, :], Aneg, Aneg.ap[:, :], ALU.mult)
            K.mm(ps_small, ps_small.ap[:, 16:32], C['U_f32'], C['U_f32'].ap[:, :], sm['adt'], sm['adt'].ap[:, :],
                 True, True)
            K.mm(ps_small, ps_small.ap[:, 32:48], C['ones_f32'], C['ones_f32'].ap[:, :], sm['adt'],
                 sm['adt'].ap[:, :], True, True)
            K.copy('dve', sm['cs'], sm['cs'].ap[:, :], ps_small, ps_small.ap[:, 16:32])
            K.tt('dve', sm['bE'], sm['bE'].ap[:, :], sm['lndt'], sm['lndt'].ap[:, :], sm['cs'], sm['cs'].ap[:, :],
                 ALU.subtract)
            K.tt('dve', sm['wx'], sm['wx'].ap[:, :], ps_small, ps_small.ap[:, 32:48], sm['bE'], sm['bE'].ap[:, :],
                 ALU.add)
            K.act(sm['w'], sm['w'].ap[:, :], sm['wx'], sm['wx'].ap[:, :], AF.Exp)
            K.act(sm['cd'], sm['cd'].ap[:, :], ps_small, ps_small.ap[:, 32:48], AF.Exp)
            K.act(sm['ecs'], sm['ecs'].ap[:, :], sm['cs'], sm['cs'].ap[:, :], AF.Exp)
            if RB < 3:
                continue
            if cg <= 1:
                K.dbg(f"dt{cg}", sm['dt'], sm['dt'].ap[:, :])
                K.dbg(f"cs{cg}", sm['cs'], sm['cs'].ap[:, :])
                K.dbg(f"w{cg}", sm['w'], sm['w'].ap[:, :])
                K.dbg(f"cd{cg}", sm['cd'], sm['cd'].ap[:, :])
            K.tt('pool', rhsR, rhsR.ap.rearrange("p (h l) -> p h l", h=16),
                 C['U_f32'], C['U_f32'].ap.unsqueeze(1).to_broadcast([128, 16, 128]),
                 sm['adt'], sm['adt'].ap.unsqueeze(2).to_broadcast([128, 16, 128]), ALU.mult)
            for bb in range(4):
                K.mm(ps_R, ps_R.ap[:, :], C['ones_f32'], C['ones_f32'].ap[:, :], rhsR,
                     rhsR.ap[:, 512 * bb:512 * bb + 512], True, False)
                K.mm(ps_R, ps_R.ap.rearrange("p (h l) -> p h l", h=4), identb, identb.ap[:, :], C['neg_bf'],
                     C['neg_bf'].ap.unsqueeze(1).to_broadcast([128, 4, 128]), False, True)
                for hh in range(4):
                    h = 4 * bb + hh
                    K.act(Eexp, Eexp.ap[:, 128 * h:128 * h + 128], ps_R, ps_R.ap[:, 128 * hh:128 * hh + 128], AF.Exp,
                          bias=sm['bE'].ap[:, h:h + 1], reads=[sm['bE']])
            if RB < 4:
                continue
            pxb = ps_X.ap.bitcast(BF16)
            for c in range(8):
                K.transpose(ps_X, pxb[:, 128 * c:128 * c + 128], xbc, xbc.ap[:, c, tc], identb, identb.ap[:, :])
            if RB < 4.2:
                continue
            K.copy('act', xsb, xsb.ap[:, :], ps_X, pxb[:, :])
            if RB < 4.4:
                continue
            K.copy('pool', bcw, bcw.ap.rearrange("p (h e) -> p h e", h=16), sm['w'],
                   sm['w'].ap.unsqueeze(2).to_broadcast([128, 16, 64]))
            if RB < 4.47:
                continue
            K.tt('dve', xsw, xsw.ap[:, :], xsb, xsb.ap[:, :], bcw, bcw.ap[:, :], ALU.mult)
            if RB < 4.6:
                continue
            pgb = ps_G.ap.bitcast(BF16)
            for g in range(2):
                K.transpose(ps_G, pgb[:, 512 + 128 * g:512 + 128 * g + 128], xbc, xbc.ap[:, 8 + g, tc], identb,
                            identb.ap[:, :])
            K.copy('act', Btok, Btok.ap[:, :], ps_G, pgb[:, 512:768])
            if RB < 5:
                continue
            for g in range(2):
                K.mm(ps_G, ps_G.ap[:, 128 * g:128 * g + 128], xbc, xbc.ap[:, 8 + g, tc], xbc, xbc.ap[:, 10 + g, tc],
                     True, True)
            for g in range(2):
                K.tt('dve', MT, MT.ap[:, 8 * g:8 * g + 8, :], Eexp,
                     Eexp.ap[:, 1024 * g:1024 * g + 1024].rearrange("p (h l) -> p h l", h=8), ps_G,
                     ps_G.ap[:, 128 * g:128 * g + 128].unsqueeze(1).to_broadcast([128, 8, 128]), ALU.mult)
            if RB < 6:
                continue
            if cg <= 1:
                K.dbg(f"E{cg}", Eexp, Eexp.ap[:, :])
                K.dbg(f"xsb{cg}", xsb, xsb.ap[:, :], BF16)
                K.dbg(f"xsw{cg}", xsw, xsw.ap[:, :], BF16)
                K.dbg(f"Btok{cg}", Btok, Btok.ap[:, :], BF16)
                K.dbg(f"MT{cg}", MT, MT.ap.rearrange("p h l -> p (h l)"), BF16)
            for g in range(2):
                for hh in range(8):
                    h = 8 * g + hh
                    K.mm(ps_Y, ps_Y.ap[:, 64 * hh:64 * hh + 64], MT, MT.ap[:, h, :], xsb, xsb.ap[:, 64 * h:64 * h + 64],
                         True, False)
                    K.mm(ps_Y, ps_Y.ap[:, 64 * hh:64 * hh + 64], DI, DI.ap[:, h, :], xsb, xsb.ap[:, 64 * h:64 * h + 64],
                         False, True)
                yslot = ytok.ap[:, q, 512 * g:512 * g + 512]
                if cg > 0:
                    K.mm(ps_F, ps_F.ap[:, :], xbc, xbc.ap[:, 10 + g, tc], prevb, prevb.ap[:, 512 * g:512 * g + 512],
                         True, True)
                    if g == 0:
                        K.copy('pool', bce, bce.ap.rearrange("p (h e) -> p h e", h=16), sm['ecs'],
                               sm['ecs'].ap.unsqueeze(2).to_broadcast([128, 16, 64]))
                    K.tt('dve', tmpF, tmpF.ap[:, :], ps_F, ps_F.ap[:, :], bce, bce.ap[:, 512 * g:512 * g + 512],
                         ALU.mult)
                    K.tt('dve', ytok, yslot, ps_Y, ps_Y.ap[:, :], tmpF, tmpF.ap[:, :], ALU.add)
                else:
                    K.copy('dve', ytok, yslot, ps_Y, ps_Y.ap[:, :])
            if RB < 7:
                continue
            if cg < 31:
                for g in range(2):
                    K.mm(ps_St, ps_St.ap[:, :], Btok, Btok.ap[:, 128 * g:128 * g + 128], xsw,
                         xsw.ap[:, 512 * g:512 * g + 512], True, True)
                    sv = Sst.ap[:, 512 * g:512 * g + 512]
                    if cg > 0:
                        if g == 0:
                            K.copy('pool', bcc, bcc.ap.rearrange("p (h e) -> p h e", h=16), sm['cd'],
                                   sm['cd'].ap.unsqueeze(2).to_broadcast([128, 16, 64]))
                        K.tt('pool', Sst, sv, Sst, sv, bcc, bcc.ap[:, 512 * g:512 * g + 512], ALU.mult)
                        K.tt('dve', Sst, sv, ps_St, ps_St.ap[:, :], Sst, sv, ALU.add)
                    else:
                        K.copy('dve', Sst, sv, ps_St, ps_St.ap[:, :])
                K.copy('pool', prevb, prevb.ap[:, :], Sst, Sst.ap[:, :])
                if cg <= 1:
                    K.dbg(f"S{cg}", Sst, Sst.ap[:, :])
        if RB >= 8:
            for q in range(4):
                tc = slice(128 * q, 128 * q + 128)
                for g in range(2):
                    for kc in range(8):
                        K.mm(ps_Z, ps_Z.ap[:, :], hT, hT.ap[:, kc, tc], pwz[kc], wz[:, kc, 512 * g:512 * g + 512],
                             start=(kc == 0), stop=(kc == 7))
                    K.act(sz, sz.ap[:, :], ps_Z, ps_Z.ap[:, :], AF.Silu)
                    ysl = ytok.ap[:, q, 512 * g:512 * g + 512]
                    K.tt('dve', ytok, ysl, ytok, ysl, sz, sz.ap[:, :], ALU.mult)
                    K.act(ysq, ysq.ap[:, :], ytok, ysl, AF.Square, accum=ss.ap[:, 2 * q + g:2 * q + g + 1],
                          extra_w=[ss])
            K.act(ss2, ss2.ap[:, :], ss, ss.ap[:, :], AF.Ln, bias=C['eps'].ap[:, 0:1], scale=1.0 / 512,
                  reads=[C['eps']])
            K.act(ss2, ss2.ap[:, :], ss2, ss2.ap[:, :], AF.Exp, scale=-0.5)
            for q in range(4):
                tc = slice(128 * q, 128 * q + 128)
                for g in range(2):
                    ysl = ytok.ap[:, q, 512 * g:512 * g + 512]
                    K.stt(yb, yb.ap[:, :], ytok, ysl, ss2.ap[:, 2 * q + g:2 * q + g + 1], bv,
                          bv.ap[:, 48 + 512 * g:48 + 512 * g + 512], ALU.mult, ALU.mult, reads=[ss2])
                    ptr = ps_X.ap.bitcast(BF16)
                    for k4 in range(4):
                        K.transpose(ps_X, ptr[:, 128 * k4:128 * k4 + 128], yb, yb.ap[:, 128 * k4:128 * k4 + 128],
                                    identb, identb.ap[:, :])
                    K.copy('act', mB, mB.ap[:, 4 * g:4 * g + 4, tc], ps_X,
                           ptr[:, 0:512].rearrange("p (k t) -> p k t", k=4))
        if i == 0:
            K.dbg("ytok", ytok, ytok.ap.rearrange("p q d -> p (q d)"))
            K.dbg("mB", mB, mB.ap.rearrange("p c t -> p (c t)"), BF16)
        for m in range(8):
            po = [ps_Y, ps_F, ps_St, ps_Z][m % 4]
            for k in range(16):
                src, sap = (mA, mA.ap[:, k, :]) if k < 8 else (mB, mB.ap[:, k - 8, :])
                K.mm(po, po.ap[:, :], pwo[(k, 0)], wo[:, k, m * 128:(m + 1) * 128], src, sap,
                     start=(k == 0), stop=(k == 15))
            K.tt('dve', xt, xt.ap[:, m, :], po, po.ap[:, :], xt, xt.ap[:, m, :], ALU.add)
        K.dma('sp', XT[i], xt_tile_ap(XT[i].ap, i), xt, xt.ap)


WEIGHT_SHAPES = {
    'rec_w_in': [2, D, 4624], 'rec_w_out': [2, 2048, D],
    'lru_w_r': [2, 16, 64, 64], 'lru_w_i': [2, 16, 64, 64],
    'att_w_qkv': [2, D, 3 * D], 'att_w_out': [2, D, D],
    'ffn_w_gate_up': [4, D, 2 * FH], 'ffn_w_down': [4, FH, D],
}


def rope_tables():
    half = 8
    inv = (np.float32(500000.0) ** (-2.0 * np.arange(half, dtype=np.float32) / np.float32(16))).astype(np.float32)
    pos = np.arange(L, dtype=np.float32)
    ang = (pos[:, None] * inv[None, :]).astype(np.float32)
    cos = np.cos(ang).astype(np.float32).T
    sin = np.sin(ang).astype(np.float32).T
    COS = np.ones((128, L), np.float32)
    SIN = np.zeros((128, L), np.float32)
    for h in range(2):
        COS[64 * h:64 * h + 8] = cos
        COS[64 * h + 8:64 * h + 16] = cos
        SIN[64 * h:64 * h + 8] = sin
        SIN[64 * h + 8:64 * h + 16] = sin
    return COS, SIN


def build_consts_host():
    c = {}
    c['ones_bf'] = np.ones((128, 128), dtype=ml_dtypes.bfloat16)
    c['eps'] = np.full((128, 1), EPS, dtype=np.float32)
    blk = np.zeros((128, 128), np.float32)
    blk[:64, :64] = 1
    blk[64:, 64:] = 1
    c['blk64'] = blk.astype(ml_dtypes.bfloat16)
    P = np.zeros((128, 128), np.float32)
    for h in range(2):
        for e in range(8):
            P[64 * h + e + 8, 64 * h + e] = -1.0
            P[64 * h + e, 64 * h + e + 8] = 1.0
    c['ropeP'] = P
    cos, sin = rope_tables()
    c['cos_d'] = cos
    c['sin_d'] = sin
    k = np.arange(128)[:, None]
    q = np.arange(128)[None, :]
    m = np.concatenate([(k <= q), (k >= q)], axis=1).astype(np.float32)
    c['mask4'] = np.concatenate([m, m], axis=1).astype(ml_dtypes.bfloat16)
    c['ident_bf'] = np.eye(128, dtype=np.float32).astype(ml_dtypes.bfloat16)
    c['U_f32'] = (k <= q).astype(np.float32)
    c['ones_f32'] = np.ones((128, 128), np.float32)
    c['neg_bf'] = np.where(q < k, -32768.0, 0.0).astype(np.float32).astype(ml_dtypes.bfloat16)
    c['half'] = np.full((128, 1), 0.5, np.float32)
    c['neghalf'] = np.full((128, 1), -0.5, np.float32)
    c['one'] = np.ones((128, 1), np.float32)
    return c


CONST_SPECS = [('ones_bf', [128, 128], BF16), ('eps', [128, 1], F32), ('ident_bf', [128, 128], BF16),
               ('half', [128, 1], F32), ('neghalf', [128, 1], F32), ('one', [128, 1], F32)]
CONST_PHASE = [('blk64', [128, 128], BF16), ('ropeP', [128, 128], F32), ('mask4', [128, 512], BF16),
               ('U_f32', [128, 128], F32), ('ones_f32', [128, 128], F32), ('neg_bf', [128, 128], BF16)]
CONST_DRAM_ONLY = [('cos_d', [128, L], F32), ('sin_d', [128, L], F32)]


def build_program(phases, nvec):
    nc = bass.Bass("TRN2", target_bir_lowering=False)
    K = KB(nc)
    K.dram_in = Tile(None, "dram_in", ro=True)
    xin = nc.dram_tensor("xT", [D, L], F32, kind="ExternalInput").ap()
    yout = nc.dram_tensor("yT", [D, L], F32, kind="ExternalOutput").ap()
    vecs_d = nc.dram_tensor("vecs", [128, nvec], F32, kind="ExternalInput").ap()
    bv_d = nc.dram_tensor("bvecs", [128, 2 * 1072], F32, kind="ExternalInput").ap()
    W = {}
    for name, shp in WEIGHT_SHAPES.items():
        W[name] = nc.dram_tensor(name, shp, F32, kind="ExternalInput").ap()

    C = {}
    for name, shp, dt in CONST_SPECS:
        d_ap = nc.dram_tensor(name, shp, dt, kind="ExternalInput").ap()
        C[name] = K.alloc(name, shp[1:], dt)
        K.dma('sp', C[name], C[name].ap, K.dram_in, d_ap)
    for name, shp, dt in CONST_DRAM_ONLY:
        C[name] = nc.dram_tensor(name, shp, dt, kind="ExternalInput").ap()
    C['_phase_d'] = {name: (nc.dram_tensor(name, shp, dt, kind="ExternalInput").ap(), shp, dt)
                     for name, shp, dt in CONST_PHASE}
    C['vecs'] = K.alloc("vecs", [nvec], F32)
    K.dma('sp', C['vecs'], C['vecs'].ap, K.dram_in, vecs_d)
    C['sb_base'] = K.sb_ptr

    S = {}
    qt_d = nc.dram_tensor("QT_s", [8, 128, L], BF16, kind="Internal").ap()
    kt_d = nc.dram_tensor("KT_s", [8, 128, L], BF16, kind="Internal").ap()
    vp_d = nc.dram_tensor("VP_s", [3, 8, 128, 32, 128], BF16, kind="Internal").ap()
    ot_d = nc.dram_tensor("OT_s", [8, 128, L], BF16, kind="Internal").ap()
    S['QT'] = [[Tile(qt_d[c], f"QT{c}_{i}") for i in range(NT)] for c in range(8)]
    S['KT'] = [[Tile(kt_d[c], f"KT{c}_{i}") for i in range(NT)] for c in range(8)]
    S['VP'] = [[[Tile(vp_d[p, hp], f"VP{p}_{hp}_{s}") for s in range(2)] for hp in range(8)] for p in range(3)]
    S['OT'] = [Tile(ot_d[hp], f"OT{hp}") for hp in range(8)]
    S['OT_d'] = ot_d
    ks = "ExternalOutput" if os.environ.get('DBG_SCR') else "Internal"
    S['HT_d'] = nc.dram_tensor("HT_s", [8, 128, L], BF16, kind=ks).ap()
    S['XBC_d'] = nc.dram_tensor("XBC_s", [12, 128, L], BF16, kind=ks).ap()
    S['MA_d'] = nc.dram_tensor("MA_s", [8, 128, L], BF16, kind=ks).ap()
    S['HT'] = [Tile(S['HT_d'], f"HT{i}") for i in range(NT)]
    S['XBC'] = [[Tile(S['XBC_d'][c], f"XBC{c}_{i}") for i in range(NT)] for c in range(12)]
    S['MA'] = [[Tile(S['MA_d'][c], f"MA{c}_{i}") for i in range(NT)] for c in range(8)]

    XT = [Tile(yout, f"XT{i}") for i in range(NT)]
    for i in range(NT):
        K.dma('sp', XT[i], yout[:, i * TT:(i + 1) * TT], K.dram_in, xin[:, i * TT:(i + 1) * TT])

    for ph in phases:
        if ph[0] == 'ffn':
            layer = ph[1]
            ffn_phase(K, C, XT, layer, W['ffn_w_gate_up'][layer], W['ffn_w_down'][layer], C['vecs'],
                      VEC_OFF['ffn_norm'] + 8 * layer)
        elif ph[0] == 'att':
            li = ph[1]
            att_qkv_phase(K, C, XT, li, W['att_w_qkv'][li], S, VEC_OFF['att_norm'] + 8 * li,
                          VEC_OFF['att_q_norm'] + li, VEC_OFF['att_k_norm'] + li)
            att_core_phase(K, C, li, S)
            att_out_phase(K, C, XT, li, W['att_w_out'][li], S)
        elif ph[0] in ('rec', 'rec_a', 'rec_b'):
            li = ph[1]
            vo = rec_vo(li)
            if ph[0] != 'rec_b':
                rec_a_phase(K, C, XT, li, W['rec_w_in'][li], W['lru_w_r'][li], W['lru_w_i'][li], S, vo)
            if ph[0] != 'rec_a':
                rec_b_phase(K, C, XT, li, W['rec_w_in'][li], W['rec_w_out'][li], S, vo, bv_d, 1072 * li)
    K.op('sp', None, reads=XT + K.dbg_tiles, writes=[])
    K.emit()
    return nc, K


VEC_OFF = {'ffn_norm': 0, 'att_norm': 32, 'att_q_norm': 48, 'att_k_norm': 50}
REC_BASE = 64
REC_STRIDE = 160
REC_FIELDS = {'rec_norm': 0, 'lru_conv_w': 8, 'lru_conv_b': 40, 'lru_b_r': 48, 'lru_b_i': 56, 'lru_lambda': 64,
              'ssd_conv_w': 72, 'ssd_conv_b': 120}
NVEC = REC_BASE + 2 * REC_STRIDE


def rec_vo(li):
    return {k: REC_BASE + REC_STRIDE * li + v for k, v in REC_FIELDS.items()}


def pack_vecs(inputs):
    v = np.zeros((128, NVEC), dtype=np.float32)
    f = lambda k: np.asarray(inputs[k], dtype=np.float32)
    fn = f('ffn_norm')
    for l in range(4):
        v[:, VEC_OFF['ffn_norm'] + 8 * l: VEC_OFF['ffn_norm'] + 8 * l + 8] = fn[l].reshape(8, 128).T
    an = f('att_norm')
    for l in range(2):
        v[:, VEC_OFF['att_norm'] + 8 * l: VEC_OFF['att_norm'] + 8 * l + 8] = an[l].reshape(8, 128).T
        v[:, VEC_OFF['att_q_norm'] + l] = np.tile(f('att_q_norm')[l], 2)
        v[:, VEC_OFF['att_k_norm'] + l] = np.tile(f('att_k_norm')[l], 2)
    for l in range(2):
        vo = rec_vo(l)
        v[:, vo['rec_norm']:vo['rec_norm'] + 8] = f('rec_norm')[l].reshape(8, 128).T
        for j in range(4):
            v[:, vo['lru_conv_w'] + 8 * j:vo['lru_conv_w'] + 8 * j + 8] = f('lru_conv_w')[l, j].reshape(8, 128).T
            v[:, vo['ssd_conv_w'] + 12 * j:vo['ssd_conv_w'] + 12 * j + 12] = f('ssd_conv_w')[l, j].reshape(12, 128).T
        for k in ('lru_conv_b', 'lru_b_r', 'lru_b_i', 'lru_lambda'):
            v[:, vo[k]:vo[k] + 8] = f(k)[l].reshape(8, 128).T
        v[:, vo['ssd_conv_b']:vo['ssd_conv_b'] + 12] = f('ssd_conv_b')[l].reshape(12, 128).T
    return v


def pack_bvecs(inputs):
    f = lambda k: np.asarray(inputs[k], dtype=np.float32)
    b = np.zeros((128, 2 * 1072), np.float32)
    for l in range(2):
        row = np.concatenate([f('ssd_dt_bias')[l], f('ssd_a_log')[l], f('ssd_d')[l], f('ssd_norm')[l]])
        b[:, 1072 * l:1072 * (l + 1)] = row[None, :]
    return b


def run(inputs, phases, n_cores=8, trace=False):
    x = np.asarray(inputs['x'], dtype=np.float32)
    nc, K = build_program(phases, NVEC)
    consts = build_consts_host()
    vecs = pack_vecs(inputs)
    shared = {"vecs": vecs, "bvecs": pack_bvecs(inputs)}
    shared.update(consts)
    for name in WEIGHT_SHAPES:
        shared[name] = np.ascontiguousarray(np.asarray(inputs[name], dtype=np.float32))
    in_maps = []
    for c in range(n_cores):
        m = dict(shared)
        m["xT"] = np.ascontiguousarray(x[c].T)
        in_maps.append(m)
    res = run_bass_kernel_spmd(nc, in_maps, core_ids=list(range(n_cores)), trace=trace)
    out = np.stack([np.ascontiguousarray(res.results[c]["yT"].T) for c in range(n_cores)], axis=0)
    if trace:
        return out, res, K
    if os.environ.get('DBG_SCR'):
        return out, res
    return out


def kernel(**inputs):
    phases = [('rec', 0), ('ffn', 0), ('att', 0), ('ffn', 1), ('rec', 1), ('ffn', 2), ('att', 1), ('ffn', 3)]
    return run(inputs, phases)
```

```python
import os
import numpy as np
import ml_dtypes
import concourse.bass as bass
import concourse.mybir as mybir
from concourse.bass_utils import run_bass_kernel_spmd

F32 = mybir.dt.float32
BF16 = mybir.dt.bfloat16
AF = mybir.ActivationFunctionType
ALU = mybir.AluOpType

ENG_ATTR = {'pe': 'tensor', 'act': 'scalar', 'dve': 'vector', 'pool': 'gpsimd', 'sp': 'sync'}
SEM_LIMIT = 30000
NPOOL = 12

D = 1024
L = 4096
NT = 8
TT = 512
FH = 2816
NJ = FH // 128
EPS = 1e-6


def fsize(ap):
    n = 1
    for d in ap.shape[1:]:
        n *= d
    return n


def nbytes(ap):
    n = 1
    for d in ap.shape:
        n *= d
    return n * (2 if ap.dtype == BF16 else 4)


class Op:
    __slots__ = ('eng', 'fn', 'deps', 'idx', 'dma', 'sem', 'cnt', 'snap', 'sig', 'sigval',
                 'waits', 'key', 'cost', 'pidx', 'nrem', 'succ', 'ready', 'fin')


class Tile:
    __slots__ = ('ap', 'w', 'r', 'name', 'ro', 'span')

    def __init__(self, ap, name='', ro=False):
        self.ap = ap
        self.w = None
        self.r = []
        self.name = name
        self.ro = ro
        self.span = None


class KB:
    def __init__(self, nc):
        self.nc = nc
        self.ops = []
        self.nidx = {e: 0 for e in ENG_ATTR}
        self.arena = nc.alloc_sbuf_tensor("arena", [128, 212000 // 4], F32)
        self.sb_ptr = 0
        self.dbg_tiles = []
        self.regions = []
        self.psum = [Tile(nc.alloc_psum_tensor(f"psb{i}", [128, 512], F32), f"ps{i}")
                     for i in range(8)]

    def sb_set(self, ptr):
        self.sb_ptr = ptr

    def alloc(self, name, free_shape, dtype):
        esz = 2 if dtype == BF16 else 4
        n = 1
        for s in free_shape:
            n *= s
        nbytes = (n * esz + 31) // 32 * 32
        start = self.sb_ptr
        end = start + nbytes
        assert end <= 212000, f"SBUF overflow allocating {name}: {end}"
        self.sb_ptr = end
        ap = self.arena[:, start // 4:end // 4]
        if dtype == BF16:
            ap = ap.bitcast(BF16)
        ap = ap[:, 0:n]
        if len(free_shape) == 2:
            ap = ap.rearrange("p (a b) -> p a b", a=free_shape[0])
        elif len(free_shape) == 3:
            ap = ap.rearrange("p (a b c) -> p a b c", a=free_shape[0], b=free_shape[1])
        t = Tile(ap, name)
        t.span = (start, end)
        keep = []
        for (s, e, old) in self.regions:
            if s < end and start < e:
                if old.w is not None:
                    t.r.append(old.w)
                t.r.extend(old.r)
                if s < start:
                    keep.append((s, start, old))
                if end < e:
                    keep.append((end, e, old))
            else:
                keep.append((s, e, old))
        keep.append((start, end, t))
        self.regions = keep
        return t

    def alias(self, big, name):
        t = Tile(big.ap, name)
        t.span = big.span
        if big.w is not None:
            t.r.append(big.w)
        t.r.extend(big.r)
        self.regions.append((big.span[0], big.span[1], t))
        return t

    def op(self, eng, fn, reads=(), writes=(), dma=False, cost=500.0):
        o = Op()
        o.cost = cost
        o.eng = eng
        o.fn = fn
        o.dma = dma
        o.sig = False
        o.idx = 0
        deps = {}
        reads = [t for t in reads if not t.ro]
        for t in reads:
            if t.w is not None:
                deps[id(t.w)] = t.w
        for t in writes:
            if t.w is not None:
                deps[id(t.w)] = t.w
            for x in t.r:
                deps[id(x)] = x
        o.deps = list(deps.values())
        if not dma:
            self.nidx[eng] += 1
            o.idx = self.nidx[eng]
        wset = set(id(t) for t in writes)
        for t in writes:
            t.w = o
            t.r = []
        for t in reads:
            if id(t) in wset:
                continue
            t.r.append(o)
        self.ops.append(o)
        return o

    def mm(self, ps, out_ap, lt, lhsT_ap, rt, rhs_ap, start, stop, extra=()):
        n = fsize(rhs_ap)
        c = 30.0 + 0.5 * max(n, 64) * (4.0 if rhs_ap.dtype == F32 else 1.0)
        return self.op('pe', lambda e: e.matmul(out_ap, lhsT=lhsT_ap, rhs=rhs_ap, start=start, stop=stop),
                       reads=[lt, rt] + list(extra), writes=[ps], cost=c)

    def transpose(self, ps, out_ap, it, in_ap, idt, ident_ap):
        return self.op('pe', lambda e: e.transpose(out_ap, in_ap, ident_ap), reads=[it, idt], writes=[ps],
                       cost=150.0)

    def act(self, ot, out_ap, it, in_ap, func, bias=None, scale=None, reads=(), accum=None, eng='act',
            extra_w=()):
        kw = {}
        if bias is not None:
            kw['bias'] = bias
        if scale is not None:
            kw['scale'] = scale
        if accum is not None:
            kw['accum_out'] = accum
        return self.op(eng, lambda e: e.activation(out_ap, in_ap, func, **kw),
                       reads=[it] + list(reads), writes=[ot] + list(extra_w), cost=220.0 + 0.85 * fsize(in_ap))

    def tt(self, eng, ot, out_ap, at, a_ap, bt, b_ap, op):
        c = (150.0 + 1.2 * fsize(out_ap)) if eng == 'dve' else (300.0 + 1.6 * fsize(out_ap))
        return self.op(eng, lambda e: e.tensor_tensor(out_ap, a_ap, b_ap, op), reads=[at, bt], writes=[ot], cost=c)

    def ts(self, eng, ot, out_ap, at, a_ap, s1, s2, op0, op1=None, reads=()):
        c = (120.0 + 0.7 * fsize(out_ap)) if eng == 'dve' else 2000.0
        if op1 is None:
            return self.op(eng, lambda e: e.tensor_scalar(out_ap, a_ap, s1, None, op0),
                           reads=[at] + list(reads), writes=[ot], cost=c)
        return self.op(eng, lambda e: e.tensor_scalar(out_ap, a_ap, s1, s2, op0, op1),
                       reads=[at] + list(reads), writes=[ot], cost=c)

    def stt(self, ot, out_ap, at, a_ap, scalar, bt, b_ap, op0, op1, reads=()):
        return self.op('dve', lambda e: e.scalar_tensor_tensor(out_ap, a_ap, scalar, b_ap, op0, op1),
                       reads=[at, bt] + list(reads), writes=[ot], cost=150.0 + 1.2 * fsize(out_ap))

    def copy(self, eng, ot, out_ap, it, in_ap):
        n = fsize(out_ap)
        if eng == 'act':
            return self.op(eng, lambda e: e.copy(out_ap, in_ap), reads=[it], writes=[ot], cost=220.0 + 0.85 * n)
        c = (120.0 + 0.8 * n) if eng == 'dve' else (300.0 + 1.0 * n)
        return self.op(eng, lambda e: e.tensor_copy(out_ap, in_ap), reads=[it], writes=[ot], cost=c)

    def memset(self, eng, ot, out_ap, val):
        return self.op(eng, lambda e: e.memset(out_ap, val), reads=[], writes=[ot], cost=100.0 + 0.5 * fsize(out_ap))

    def dma(self, eng, ot, out_ap, it, in_ap, **kw):
        return self.op(eng, lambda e: e.dma_start(out=out_ap, in_=in_ap, **kw), reads=[it], writes=[ot],
                       dma=True, cost=float(nbytes(out_ap)))

    def dbg(self, name, tile, ap, dtype=F32):
        if not os.environ.get('DBG_SCR'):
            return
        d = self.nc.dram_tensor("dbg_" + name, list(ap.shape), dtype, kind="ExternalOutput").ap()
        t = Tile(d, "dbg_" + name)
        self.dma('sp', t, d, tile, ap)
        self.dbg_tiles.append(t)

    def schedule(self):
        import heapq
        if os.environ.get('NO_SCHED'):
            return self.ops
        ops = self.ops
        for i, o in enumerate(ops):
            o.pidx = i
            o.succ = []
            o.ready = 0.0
            o.fin = None
        for o in ops:
            o.nrem = len(o.deps)
            for d in o.deps:
                d.succ.append(o)
        waiting = {e: [] for e in ENG_ATTR}
        avail = {e: [] for e in ENG_ATTR}
        free_at = {e: 0.0 for e in ENG_ATTR}
        dma_bw_free = [0.0]
        for o in ops:
            if o.nrem == 0:
                heapq.heappush(waiting[o.eng], (0.0, o.pidx, o))
        out = []
        n = len(ops)
        WIN = int(os.environ.get('SCHED_WIN', '4000'))
        done_upto = [0]
        while len(out) < n:
            best = None
            for e in ENG_ATTR:
                w = waiting[e]
                a = avail[e]
                while w and w[0][0] <= free_at[e]:
                    r, p, o = heapq.heappop(w)
                    heapq.heappush(a, (p, o))
                if a:
                    cand = (free_at[e], a[0][0], e, 0)
                elif w:
                    cand = (w[0][0], w[0][1], e, 1)
                else:
                    continue
                if best is None or cand < best:
                    best = cand
            start, p, e, src = best
            if src == 0:
                p, o = heapq.heappop(avail[e])
            else:
                r, p, o = heapq.heappop(waiting[e])
            if o.dma:
                issue = 1000.0 if e == 'pool' else 100.0
                t0 = max(start, dma_bw_free[0])
                dma_bw_free[0] = t0 + o.cost / 180.0
                o.fin = t0 + o.cost / 180.0 + 2000.0
                free_at[e] = start + issue
            else:
                o.fin = start + o.cost
                free_at[e] = o.fin
            out.append(o)
            for sx in o.succ:
                sx.nrem -= 1
                if sx.ready < o.fin:
                    sx.ready = o.fin
                if sx.nrem == 0:
                    heapq.heappush(waiting[sx.eng], (sx.ready, sx.pidx, sx))
        self.sim_time = max(o.fin for o in out)
        return out

    def emit(self):
        nc = self.nc
        clock = {e: {} for e in ENG_ATTR}
        pools = {}
        rr = {}
        nid = [0]

        def newslot(e):
            nid[0] += 1
            return {'sem': nc.alloc_semaphore(f"d_{e}_{nid[0]}"), 'cnt': 0, 'last': None, 'id': nid[0]}

        self.ops = self.schedule()
        for e in ENG_ATTR:
            k = 0
            for o in self.ops:
                if o.eng == e and not o.dma:
                    k += 1
                    o.idx = k
        for o in self.ops:
            e = o.eng
            deps = list(o.deps)
            if o.dma:
                pool = pools.setdefault(e, [])
                if len(pool) < NPOOL:
                    pool.append(newslot(e))
                    i = len(pool) - 1
                    rr[e] = 0
                else:
                    i = rr[e]
                    rr[e] = (i + 1) % NPOOL
                    if pool[i]['cnt'] + 16 > SEM_LIMIT:
                        last = pool[i]['last']
                        pool[i] = newslot(e)
                        pool[i]['carry'] = last
                slot = pool[i]
                if slot['last'] is not None:
                    deps.append(slot['last'])
                slot['cnt'] += 16
                slot['last'] = o
                o.sem = slot['sem']
                o.cnt = slot['cnt']
                o.key = ('d', slot['id'])
            ck = clock[e]
            waits = []
            deps.sort(key=lambda d: -(d.cnt if d.dma else d.idx))
            for d in deps:
                if d.dma:
                    key = d.key
                    val = d.cnt
                else:
                    if d.eng == 'pe' and e == 'pe' and not o.dma:
                        continue
                    key = d.eng
                    val = d.idx
                if ck.get(key, 0) >= val:
                    continue
                waits.append(d)
                d.sig = True
                nk = dict(ck)
                for k, v in d.snap.items():
                    if nk.get(k, 0) < v:
                        nk[k] = v
                if nk.get(key, 0) < val:
                    nk[key] = val
                ck = nk
            clock[e] = ck
            o.snap = ck
            o.waits = waits

        cnt = {e: 0 for e in ENG_ATTR}
        esems = {e: [] for e in ENG_ATTR}
        for o in self.ops:
            if o.dma or not o.sig:
                continue
            c = cnt[o.eng]
            cnt[o.eng] = c + 1
            ep = c // SEM_LIMIT
            if ep >= len(esems[o.eng]):
                esems[o.eng].append(nc.alloc_semaphore(f"c_{o.eng}_{ep}"))
            o.sem = esems[o.eng][ep]
            o.sigval = c - ep * SEM_LIMIT + 1

        streams = {e: [] for e in ENG_ATTR}
        for o in self.ops:
            streams[o.eng].append(o)
        self.stats = {e: (len(streams[e]), sum(len(o.waits) for o in streams[e])) for e in ENG_ATTR}
        with nc.Block() as block:
            for e, attr in ENG_ATTR.items():
                ops = streams[e]

                def body(eng, ops=ops):
                    for o in ops:
                        for d in o.waits:
                            eng.wait_ge(d.sem, d.cnt if d.dma else d.sigval)
                        if o.fn is None:
                            continue
                        ins = o.fn(eng)
                        if o.dma:
                            ins.then_inc(o.sem, 16)
                        elif o.sig:
                            ins.then_inc(o.sem, 1)
                getattr(block, attr)(body)


def phase_consts(K, C, names):
    for n in names:
        d_ap, shp, dt = C['_phase_d'][n]
        C[n] = K.alloc("pc_" + n, shp[1:], dt)
        K.dma('sp', C[n], C[n].ap, K.dram_in, d_ap)


def xt_tile_ap(XT_ap, i):
    return XT_ap.rearrange("(c p) n -> p c n", p=128)[:, :, i * TT:(i + 1) * TT]


def load_weight_cast(K, W_ap, kchunks, ncols, name, seg=2048):
    big = K.alloc(name, [kchunks, ncols], BF16)
    pieces = {}
    Wv = W_ap.rearrange("(c p) n -> p c n", p=128)
    for kc in range(kchunks):
        for s0 in range(0, ncols, seg):
            s1 = min(ncols, s0 + seg)
            t = K.alias(big, f"{name}_{kc}_{s0}")
            K.dma('pool', t, big.ap[:, kc, s0:s1], K.dram_in, Wv[:, kc, s0:s1])
            pieces[(kc, s0 // seg)] = t
    return big.ap, pieces, seg


def rms_stats(K, C, xt, ps_stats, sq, lnv, rstd):
    for c in range(8):
        s = sq[c % 2]
        K.act(s, s.ap[:, :], xt, xt.ap[:, c, :], AF.Square)
        K.mm(ps_stats, ps_stats.ap[:, :], C['ones_bf'], C['ones_bf'].ap[:, :], s, s.ap[:, :],
             start=(c == 0), stop=(c == 7))
    K.act(lnv, lnv.ap[:, :], ps_stats, ps_stats.ap[:, :], AF.Ln, bias=C['eps'].ap[:, 0:1], scale=1.0 / D,
          reads=[C['eps']])
    K.act(rstd, rstd.ap[:, :], lnv, lnv.ap[:, :], AF.Exp, scale=-0.5)


def ffn_phase(K, C, XT, layer, Wgu_ap, Wd_ap, gvec, gcol):
    K.sb_set(C['sb_base'])
    wgu, pgu, seg = load_weight_cast(K, Wgu_ap, 8, 2 * FH, f"wgu{layer}")
    wd, pd, segd = load_weight_cast(K, Wd_ap, NJ, D, f"wd{layer}", seg=1024)
    xts = [K.alloc(f"f_xt{b}", [8, TT], F32) for b in range(2)]
    hT = K.alloc("f_hT", [8, TT], BF16)
    sq = [K.alloc(f"f_sq{b}", [TT], BF16) for b in range(2)]
    lnv = K.alloc("f_lnv", [TT], F32)
    rstd = K.alloc("f_rstd", [TT], F32)
    sg = [K.alloc("f_sg", [TT], F32)] * 2
    aT = K.alloc("f_aT", [NJ, TT], BF16)
    psS = K.psum[0]
    psG = [K.psum[1], K.psum[2]]
    psU = [K.psum[3], K.psum[4]]
    psO = [K.psum[5], K.psum[6]]

    def load(i):
        K.dma('sp', xts[i % 2], xts[i % 2].ap, XT[i], xt_tile_ap(XT[i].ap, i))

    load(0)
    for i in range(NT):
        xt = xts[i % 2]
        if i + 1 < NT:
            load(i + 1)
        rstd_pow(K, C, xt, psS, sq, rstd, lnv)
        for c in range(8):
            K.stt(hT, hT.ap[:, c, :], xt, xt.ap[:, c, :], gvec.ap[:, gcol + c:gcol + c + 1], rstd, rstd.ap[:, :],
                  ALU.mult, ALU.mult, reads=[gvec])
        for j in range(NJ):
            pg = psG[j % 2]
            pu = psU[j % 2]
            for kc in range(8):
                c0 = j * 128
                K.mm(pg, pg.ap[:, :], pgu[(kc, c0 // seg)], wgu[:, kc, c0:c0 + 128], hT, hT.ap[:, kc, :],
                     start=(kc == 0), stop=(kc == 7))
            for kc in range(8):
                c0 = FH + j * 128
                K.mm(pu, pu.ap[:, :], pgu[(kc, c0 // seg)], wgu[:, kc, c0:c0 + 128], hT, hT.ap[:, kc, :],
                     start=(kc == 0), stop=(kc == 7))
            s = sg[j % 2]
            K.act(s, s.ap[:, :], pg, pg.ap[:, :], AF.Silu)
            K.tt('dve', aT, aT.ap[:, j, :], s, s.ap[:, :], pu, pu.ap[:, :], ALU.mult)
        for m in range(8):
            po = psO[m % 2]
            for j in range(NJ):
                K.mm(po, po.ap[:, :], pd[(j, 0)], wd[:, j, m * 128:(m + 1) * 128], aT, aT.ap[:, j, :],
                     start=(j == 0), stop=(j == NJ - 1))
            K.tt('dve', xt, xt.ap[:, m, :], po, po.ap[:, :], xt, xt.ap[:, m, :], ALU.add)
        K.dma('sp', XT[i], xt_tile_ap(XT[i].ap, i), xt, xt.ap)


PATS = (1, 4, 16)


def att_qkv_phase(K, C, XT, li, Wqkv_ap, S, gcol, qcol, kcol):
    K.sb_set(C['sb_base'])
    phase_consts(K, C, ['blk64', 'ropeP'])
    w, pw, seg = load_weight_cast(K, Wqkv_ap, 8, 3 * D, f"wqkv{li}", seg=1024)
    xts = [K.alloc(f"a_xt{b}", [8, TT], F32) for b in range(2)]
    hTs = K.alloc("a_hTs", [8, 2048], BF16)
    sq = [K.alloc(f"a_sq{b}", [TT], BF16) for b in range(2)]
    lnv = K.alloc("a_lnv", [TT], F32)
    rstd = K.alloc("a_rstd", [TT], F32)
    cs = [(K.alloc(f"a_cos{b}", [TT], F32), K.alloc(f"a_sin{b}", [TT], F32)) for b in range(2)]
    sets = []
    for b in range(2):
        sets.append(dict(qs=K.alloc(f"a_qs{b}", [TT], F32), hsq=K.alloc(f"a_hsq{b}", [TT], BF16),
                         lnh=K.alloc(f"a_lnh{b}", [TT], F32), rsh=K.alloc(f"a_rsh{b}", [TT], F32),
                         qn=K.alloc(f"a_qn{b}", [TT], F32), t1=K.alloc(f"a_t1{b}", [TT], F32),
                         t2=K.alloc(f"a_t2{b}", [TT], F32), qo=K.alloc(f"a_qo{b}", [TT], BF16)))
    vbuf = K.alloc("a_vbuf", [16, D], BF16)
    psS = K.psum[0]
    psQ = [K.psum[1], K.psum[2]]
    psH = [K.psum[3], K.psum[7]]
    psP = K.psum[4]
    psV = [K.psum[5], K.psum[6]]
    vecs = C['vecs']

    def load(i):
        K.dma('sp', xts[i % 2], xts[i % 2].ap, XT[i], xt_tile_ap(XT[i].ap, i))
        co, si = cs[i % 2]
        K.dma('sp', co, co.ap, K.dram_in, C['cos_d'][:, i * TT:(i + 1) * TT])
        K.dma('sp', si, si.ap, K.dram_in, C['sin_d'][:, i * TT:(i + 1) * TT])

    load(0)
    vcnt = 0
    for i in range(NT):
        xt = xts[i % 2]
        co, si = cs[i % 2]
        if i + 1 < NT:
            load(i + 1)
        rms_stats(K, C, xt, psS, sq, lnv, rstd)
        lc = (i % 4) * TT
        for c in range(8):
            K.stt(hTs, hTs.ap[:, c, lc:lc + TT], xt, xt.ap[:, c, :], vecs.ap[:, gcol + c:gcol + c + 1], rstd,
                  rstd.ap[:, :], ALU.mult, ALU.mult, reads=[vecs])
        for c in range(16):
            st = sets[c % 2]
            ps = psQ[c % 2]
            ph = psH[c % 2]
            for kc in range(8):
                c0 = c * 128
                K.mm(ps, ps.ap[:, :], pw[(kc, c0 // seg)], w[:, kc, c0:c0 + 128], hTs, hTs.ap[:, kc, lc:lc + TT],
                     start=(kc == 0), stop=(kc == 7))
            K.copy('act', st['qs'], st['qs'].ap[:, :], ps, ps.ap[:, :])
            K.act(st['hsq'], st['hsq'].ap[:, :], ps, ps.ap[:, :], AF.Square)
            K.mm(ph, ph.ap[:, :], C['blk64'], C['blk64'].ap[:, :], st['hsq'], st['hsq'].ap[:, :], True, True)
            K.act(st['lnh'], st['lnh'].ap[:, :], ph, ph.ap[:, :], AF.Ln, bias=C['eps'].ap[:, 0:1], scale=1.0 / 64,
                  reads=[C['eps']])
            K.act(st['rsh'], st['rsh'].ap[:, :], st['lnh'], st['lnh'].ap[:, :], AF.Exp, scale=-0.5)
            gc = (qcol if c < 8 else kcol)
            K.stt(st['qn'], st['qn'].ap[:, :], st['qs'], st['qs'].ap[:, :], vecs.ap[:, gc:gc + 1], st['rsh'],
                  st['rsh'].ap[:, :], ALU.mult, ALU.mult, reads=[vecs])
            K.mm(psP, psP.ap[:, :], C['ropeP'], C['ropeP'].ap[:, :], st['qn'], st['qn'].ap[:, :], True, True)
            K.tt('pool', st['t1'], st['t1'].ap[:, :], st['qn'], st['qn'].ap[:, :], co, co.ap[:, :], ALU.mult)
            K.tt('dve', st['t2'], st['t2'].ap[:, :], psP, psP.ap[:, :], si, si.ap[:, :], ALU.mult)
            K.tt('dve', st['qo'], st['qo'].ap[:, :], st['t1'], st['t1'].ap[:, :], st['t2'], st['t2'].ap[:, :], ALU.add)
            dst = S['QT'][c][i] if c < 8 else S['KT'][c - 8][i]
            K.dma('sp', dst, dst.ap[:, i * TT:(i + 1) * TT], st['qo'], st['qo'].ap[:, :])
        if i % 4 == 3:
            s = i // 4
            for pi, d in enumerate(PATS):
                for lb in range(16):
                    nl = lb // d
                    r = lb % d
                    start = nl * 128 * d + r
                    for half in range(2):
                        pv = psV[vcnt % 2]
                        vcnt += 1
                        for kc in range(8):
                            c0 = 2 * D + half * 512
                            K.mm(pv, pv.ap[:, :], hTs, hTs.ap[:, kc, start:start + 127 * d + 1:d],
                                 pw[(kc, c0 // seg)], w[:, kc, c0:c0 + 512], start=(kc == 0), stop=(kc == 7))
                        eng = 'act' if half == 0 else 'dve'
                        K.copy(eng, vbuf, vbuf.ap[:, lb, half * 512:(half + 1) * 512], pv, pv.ap[:, :])
                for hp in range(8):
                    dst = S['VP'][pi][hp][s]
                    K.dma('sp', dst, dst.ap[:, 16 * s:16 * s + 16, :], vbuf, vbuf.ap[:, :, hp * 128:(hp + 1) * 128])


def att_core_phase(K, C, li, S):
    K.sb_set(C['sb_base'])
    phase_consts(K, C, ['mask4'])
    qk = [(K.alloc(f"c_qt{b}", [L], BF16), K.alloc(f"c_kt{b}", [L], BF16)) for b in range(2)]
    vps = [[K.alloc(f"c_vp{b}_{p}", [32, 128], BF16) for p in range(3)] for b in range(2)]
    accN = K.alloc("c_accN", [L], F32)
    accD = K.alloc("c_accD", [L], F32)
    pts = [[K.alloc(f"c_pt{h}_{k}", [TT], BF16) for k in range(3)] for h in range(2)]
    ot = K.alloc("c_ot", [L], BF16)
    rec = [K.alloc(f"c_rec{b}", [TT], F32) for b in range(2)]
    psSb = [[K.psum[0], K.psum[1]], [K.psum[2], K.psum[3]]]
    psOb = [K.psum[4], K.psum[5]]
    psDb = [K.psum[6], K.psum[7]]
    mask = C['mask4']
    ones64 = C['ones_bf']

    def load(hp):
        qt, kt = qk[hp % 2]
        K.dma('sp', qt, qt.ap, S['QT'][hp][0], S['QT'][hp][0].ap)
        for t in S['QT'][hp][1:]:
            qt.r
        K.dma('sp', kt, kt.ap, S['KT'][hp][0], S['KT'][hp][0].ap)
        for p in range(3):
            v = vps[hp % 2][p]
            K.dma('sp', v, v.ap, S['VP'][p][hp][0], S['VP'][p][hp][0].ap)

    def load_multi(dst, dst_ap, tiles, src_ap):
        K.op('sp', lambda e: e.dma_start(out=dst_ap, in_=src_ap), reads=list(tiles), writes=[dst], dma=True)

    def load2(hp):
        qt, kt = qk[hp % 2]
        load_multi(qt, qt.ap, S['QT'][hp], S['QT'][hp][0].ap)
        load_multi(kt, kt.ap, S['KT'][hp], S['KT'][hp][0].ap)
        for p in range(3):
            v = vps[hp % 2][p]
            load_multi(v, v.ap, S['VP'][p][hp], S['VP'][p][hp][0].ap)

    load2(0)
    octr = 0
    for hp in range(8):
        if hp + 1 < 8:
            load2(hp + 1)
        qt, kt = qk[hp % 2]
        groups = []
        for pi, d in enumerate(PATS):
            nb = 32 // d
            for r in range(d):
                for g in range((nb + 1) // 2):
                    units = [m for m in (2 * g, 2 * g + 1) if m < nb]
                    groups.append((pi, d, r, nb, units))

        def emit_S(gi):
            pi, d, r, nb, units = groups[gi]
            qv = qt.ap.rearrange("p (m d) -> p d m", d=d)
            kv = kt.ap.rearrange("p (m d) -> p d m", d=d)
            for h in range(2):
                ps = psSb[h][gi % 2]
                rows = slice(64 * h, 64 * h + 64)
                for ui, m in enumerate(units):
                    ncols = 256 if m < nb - 1 else 128
                    K.mm(ps, ps.ap[:, ui * 256:ui * 256 + ncols], kt, kv[rows, r, 128 * m:128 * m + 128],
                         qt, qv[rows, r, 128 * m:128 * m + ncols], True, True)

        def emit_E(gi):
            pi, d, r, nb, units = groups[gi]
            valid = 0
            for ui, m in enumerate(units):
                valid = ui * 256 + (256 if m < nb - 1 else 128)
            for h in range(2):
                ps = psSb[h][gi % 2]
                pt = pts[h][gi % 3]
                K.act(pt, pt.ap[:, 0:valid], ps, ps.ap[:, 0:valid], AF.Exp, scale=0.125)
                meng = 'pool' if (2 * gi + h) % 3 == 0 else 'dve'
                K.tt(meng, pt, pt.ap[:, 0:valid], pt, pt.ap[:, 0:valid], mask, mask.ap[:, 0:valid], ALU.mult)

        state = {'ob': 0, 'obank': 0}

        def emit_PV(gi):
            nonlocal octr
            pi, d, r, nb, units = groups[gi]
            vp = vps[hp % 2][pi]
            for ui, m in enumerate(units):
                blk = m * d + r
                pso = psOb[octr % 2]
                psd = psDb[octr % 2]
                oc = state['ob'] * 128
                for h in range(2):
                    rows = slice(64 * h, 64 * h + 64)
                    pt = pts[h][gi % 3]
                    cur = pt.ap[:, ui * 256:ui * 256 + 128]
                    if m > 0:
                        if ui == 1:
                            ppt = pt
                            prev = pt.ap[:, 128:256]
                        else:
                            ppt = pts[h][(gi - 1) % 3]
                            prev = ppt.ap[:, 256 + 128:512]
                    for (pp, lcur, lprev) in ((pso, vp.ap[:, blk, 64 * h:64 * h + 64],
                                               vp.ap[:, blk - d, 64 * h:64 * h + 64] if m > 0 else None),
                                              (psd, ones64.ap[:, 0:64], ones64.ap[:, 0:64])):
                        lt = vp if pp is pso else ones64
                        K.mm(pp, pp.ap[rows, oc:oc + 128], lt, lcur, pt, cur, True, m == 0)
                        if m > 0:
                            K.mm(pp, pp.ap[rows, oc:oc + 128], lt, lprev, ppt, prev, False, True)
                state['ob'] += 1
                last_of_class = (m == nb - 1)
                flush = state['ob'] == 4 or (last_of_class and d != 16) or (last_of_class and d == 16 and r % 2 == 1)
                if flush:
                    nblk = state['ob']
                    accNv = accN.ap.rearrange("p (m d) -> p d m", d=d)
                    accDv = accD.ap.rearrange("p (m d) -> p d m", d=d)
                    if d == 16:
                        av = lambda a: a[:, r - 1:r + 1, :]
                        pv_ = lambda p: p.ap.rearrange("p (a b) -> p a b", a=2)
                    else:
                        n0 = m - nblk + 1
                        av = lambda a: a[:, r, 128 * n0:128 * (m + 1)]
                        pv_ = lambda p: p.ap[:, 0:128 * nblk]
                    if d == 1:
                        K.copy('act', accN, av(accNv), pso, pv_(pso))
                        K.copy('act', accD, av(accDv), psd, pv_(psd))
                    else:
                        K.tt('dve', accN, av(accNv), pso, pv_(pso), accN, av(accNv), ALU.add)
                        K.tt('dve', accD, av(accDv), psd, pv_(psd), accD, av(accDv), ALU.add)
                    state['ob'] = 0
                    octr += 1

        ng = len(groups)
        emit_S(0)
        for gi in range(ng):
            if gi + 1 < ng:
                emit_S(gi + 1)
            emit_E(gi)
            emit_PV(gi)
        for i in range(NT):
            rc = rec[i % 2]
            cols = slice(i * TT, (i + 1) * TT)
            K.act(rc, rc.ap[:, :], accD, accD.ap[:, cols], AF.Ln)
            K.act(rc, rc.ap[:, :], rc, rc.ap[:, :], AF.Exp, scale=-1.0)
            K.tt('pool', ot, ot.ap[:, cols], accN, accN.ap[:, cols], rc, rc.ap[:, :], ALU.mult)
        K.dma('sp', S['OT'][hp], S['OT'][hp].ap, ot, ot.ap)


def att_out_phase(K, C, XT, li, Wo_ap, S):
    K.sb_set(C['sb_base'])
    w, pw, seg = load_weight_cast(K, Wo_ap, 8, D, f"wo{li}", seg=1024)
    xts = [K.alloc(f"o_xt{b}", [8, TT], F32) for b in range(2)]
    ots = [K.alloc(f"o_ot{b}", [8, TT], BF16) for b in range(2)]
    psO = [K.psum[0], K.psum[1], K.psum[2], K.psum[3]]

    def load(i):
        K.dma('sp', xts[i % 2], xts[i % 2].ap, XT[i], xt_tile_ap(XT[i].ap, i))
        K.op('sp', lambda e, i=i: e.dma_start(out=ots[i % 2].ap,
                                             in_=S['OT_d'].rearrange("c p n -> p c n")[:, :, i * TT:(i + 1) * TT]),
             reads=S['OT'], writes=[ots[i % 2]], dma=True)

    load(0)
    for i in range(NT):
        if i + 1 < NT:
            load(i + 1)
        xt = xts[i % 2]
        o = ots[i % 2]
        for m in range(8):
            po = psO[m % 4]
            for kc in range(8):
                K.mm(po, po.ap[:, :], pw[(kc, 0)], w[:, kc, m * 128:(m + 1) * 128], o, o.ap[:, kc, :],
                     start=(kc == 0), stop=(kc == 7))
            K.tt('dve', xt, xt.ap[:, m, :], po, po.ap[:, :], xt, xt.ap[:, m, :], ALU.add)
        K.dma('sp', XT[i], xt_tile_ap(XT[i].ap, i), xt, xt.ap)


def rstd_pow(K, C, xt, ps_stats, sq, rstd, tmp):
    for c in range(8):
        s = sq[c % 2]
        K.act(s, s.ap[:, :], xt, xt.ap[:, c, :], AF.Square)
        K.mm(ps_stats, ps_stats.ap[:, :], C['ones_bf'], C['ones_bf'].ap[:, :], s, s.ap[:, :],
             start=(c == 0), stop=(c == 7))
    K.act(tmp, tmp.ap[:, :], ps_stats, ps_stats.ap[:, :], AF.Ln, bias=C['eps'].ap[:, 0:1], scale=1.0 / D,
          reads=[C['eps']])
    K.act(rstd, rstd.ap[:, :], tmp, tmp.ap[:, :], AF.Exp, scale=-0.5)


def rec_a_phase(K, C, XT, li, Win_ap, Wr_ap, Wi_ap, S, vo):
    K.sb_set(C['sb_base'])
    vecs = C['vecs']
    NW = 3584
    wbig = K.alloc(f"ra_w{li}", [8, NW], BF16)
    w = wbig.ap
    Wv = Win_ap.rearrange("(c p) n -> p c n", p=128)
    pw = {}
    for kc in range(8):
        for (d0, s0, n) in ((0, 0, 1024), (1024, 1024, 1024), (2048, 3072, 1536)):
            t = K.alias(wbig, f"ra_w{li}_{kc}_{d0}")
            K.dma('pool', t, w[:, kc, d0:d0 + n], K.dram_in, Wv[:, kc, s0:s0 + n])
            pw[(kc, d0)] = t

    def wpiece(kc, col):
        return pw[(kc, 0 if col < 1024 else (1024 if col < 2048 else 2048))]

    wr = K.alloc("ra_wr", [8, 128], BF16)
    wi = K.alloc("ra_wi", [8, 128], BF16)
    K.memset('pool', wr, wr.ap, 0.0)
    K.memset('pool', wi, wi.ap, 0.0)
    for (wt, src) in ((wr, Wr_ap), (wi, Wi_ap)):
        sv = src.rearrange("(c two) i j -> two i c j", two=2)
        for h in range(2):
            K.dma('pool', wt, wt.ap[64 * h:64 * h + 64, :, 64 * h:64 * h + 64], K.dram_in, sv[h])
    diag = K.alloc("ra_diag", [20 * 4, 128], BF16)
    for c in range(20):
        for j in range(4):
            col = (vo['lru_conv_w'] + j * 8 + c) if c < 8 else (vo['ssd_conv_w'] + j * 12 + (c - 8))
            K.ts('pool', diag, diag.ap[:, c * 4 + j, :], C['ident_bf'], C['ident_bf'].ap[:, :],
                 vecs.ap[:, col:col + 1], None, ALU.mult, reads=[vecs])
    dv = K.alloc("ra_dv", [32], F32)
    sp = [K.alloc(f"ra_sp{k}", [8], F32) for k in range(4)]
    K.ts('dve', dv, dv.ap[:, 0:8], vecs, vecs.ap[:, vo['lru_b_r']:vo['lru_b_r'] + 8], 0.5, None, ALU.mult)
    K.ts('dve', dv, dv.ap[:, 8:16], vecs, vecs.ap[:, vo['lru_b_i']:vo['lru_b_i'] + 8], 0.5, None, ALU.mult)
    lam = vecs.ap[:, vo['lru_lambda']:vo['lru_lambda'] + 8]
    K.act(sp[0], sp[0].ap[:, :], vecs, lam, AF.Exp, scale=-1.0)
    K.ts('dve', sp[1], sp[1].ap[:, :], sp[0], sp[0].ap[:, :], 1.0, None, ALU.add)
    K.ts('dve', sp[2], sp[2].ap[:, :], sp[1], sp[1].ap[:, :], -1.0, 1e-30, ALU.add, ALU.max)
    K.op('dve', lambda e: e.reciprocal(sp[2].ap[:, :], sp[2].ap[:, :]), reads=[sp[2]], writes=[sp[2]])
    K.tt('dve', sp[2], sp[2].ap[:, :], sp[2], sp[2].ap[:, :], sp[0], sp[0].ap[:, :], ALU.mult)
    K.act(sp[3], sp[3].ap[:, :], sp[1], sp[1].ap[:, :], AF.Ln)
    K.tt('dve', sp[3], sp[3].ap[:, :], sp[3], sp[3].ap[:, :], sp[2], sp[2].ap[:, :], ALU.mult)
    K.ts('dve', dv, dv.ap[:, 16:24], sp[3], sp[3].ap[:, :], -8.0, None, ALU.mult)
    K.ts('dve', dv, dv.ap[:, 24:32], sp[3], sp[3].ap[:, :], -4.0, None, ALU.mult)

    rawb = [K.alloc(f"ra_raw{c}", [516], BF16) for c in range(20)]
    for c in range(20):
        K.memset('pool', rawb[c], rawb[c].ap[:, 0:4], 0.0)
    hcar = K.alloc("ra_hcar", [8], F32)
    xt = K.alloc("ra_xt", [8, TT], F32)
    hT = K.alloc("ra_hT", [8, TT], BF16)
    sq = [K.alloc(f"ra_sq{b}", [TT], BF16) for b in range(2)]
    rtmp = K.alloc("ra_rtmp", [TT], F32)
    rstd = K.alloc("ra_rstd", [TT], F32)
    gl = K.alloc("ra_gl", [8, TT], BF16)
    xo = [K.alloc(f"ra_xo{b}", [TT], BF16) for b in range(2)]
    names = ['xc', 'thr', 'thi', 'a', 'a2', 'om', 'sr', 't2', 'u', 'hl']
    sets = [{n: K.alloc(f"ra_{n}{b}", [TT], F32) for n in names} for b in range(2)]
    for b in range(2):
        sets[b]['xcb'] = K.alloc(f"ra_xcb{b}", [TT], BF16)
        sets[b]['oa'] = K.alloc(f"ra_oa{b}", [TT], BF16)
    psS = K.psum[0]
    psP = [K.psum[1], K.psum[2]]
    psC = [K.psum[3], K.psum[4]]
    psR = K.psum[5]
    psI = K.psum[6]

    def proj(ps, col0, ncol=128):
        for kc in range(8):
            K.mm(ps, ps.ap[0:ncol, :], wpiece(kc, col0), w[:, kc, col0:col0 + ncol], hT, hT.ap[:, kc, :],
                 start=(kc == 0), stop=(kc == 7))

    def conv(pc, c, ps, first):
        rb = rawb[c]
        if not first:
            K.copy('pool', rb, rb.ap[:, 0:4], rb, rb.ap[:, 512:516])
        K.copy('act', rb, rb.ap[:, 4:516], ps, ps.ap[:, :])
        for j in range(4):
            K.mm(pc, pc.ap[:, :], diag, diag.ap[:, c * 4 + j, :], rb, rb.ap[:, 1 + j:1 + j + 512],
                 start=(j == 0), stop=(j == 3))

    for i in range(NT):
        K.dma('sp', xt, xt.ap, XT[i], xt_tile_ap(XT[i].ap, i))
        rstd_pow(K, C, xt, psS, sq, rstd, rtmp)
        for c in range(8):
            K.stt(hT, hT.ap[:, c, :], xt, xt.ap[:, c, :], vecs.ap[:, vo['rec_norm'] + c:vo['rec_norm'] + c + 1],
                  rstd, rstd.ap[:, :], ALU.mult, ALU.mult, reads=[vecs])
        K.dma('sp', S['HT'][i], S['HT_d'].rearrange("c p n -> p c n")[:, :, i * TT:(i + 1) * TT], hT, hT.ap)
        for c in range(8):
            ps = psP[c % 2]
            proj(ps, 1024 + c * 128)
            K.act(gl, gl.ap[:, c, :], ps, ps.ap[:, :], AF.Gelu_apprx_tanh)
        for cc in range(12):
            ps = psP[cc % 2]
            pc = psC[cc % 2]
            proj(ps, 2048 + cc * 128)
            conv(pc, 8 + cc, ps, i == 0)
            o = xo[cc % 2]
            bcol = vo['ssd_conv_b'] + cc
            K.act(o, o.ap[:, :], pc, pc.ap[:, :], AF.Silu, bias=vecs.ap[:, bcol:bcol + 1], reads=[vecs])
            K.dma('sp', S['XBC'][cc][i], S['XBC'][cc][i].ap[:, i * TT:(i + 1) * TT], o, o.ap[:, :])
        for c in range(8):
            st = sets[c % 2]
            ps = psP[c % 2]
            pc = psC[c % 2]
            proj(ps, c * 128)
            conv(pc, c, ps, i == 0)
            bcol = vo['lru_conv_b'] + c
            K.act(st['xc'], st['xc'].ap[:, :], pc, pc.ap[:, :], AF.Identity, bias=vecs.ap[:, bcol:bcol + 1],
                  reads=[vecs])
            K.copy('dve', st['xcb'], st['xcb'].ap[:, :], st['xc'], st['xc'].ap[:, :])
            K.mm(psR, psR.ap[:, :], wr, wr.ap[:, c, :], st['xcb'], st['xcb'].ap[:, :], True, True)
            K.mm(psI, psI.ap[:, :], wi, wi.ap[:, c, :], st['xcb'], st['xcb'].ap[:, :], True, True)
            K.act(st['thr'], st['thr'].ap[:, :], psR, psR.ap[:, :], AF.Tanh, bias=dv.ap[:, c:c + 1], scale=0.5,
                  reads=[dv])
            K.act(st['thi'], st['thi'].ap[:, :], psI, psI.ap[:, :], AF.Tanh, bias=dv.ap[:, 8 + c:9 + c], scale=0.5,
                  reads=[dv])
            K.act(st['a'], st['a'].ap[:, :], st['thr'], st['thr'].ap[:, :], AF.Exp, bias=dv.ap[:, 24 + c:25 + c],
                  scale=dv.ap[:, 24 + c:25 + c], reads=[dv])
            K.act(st['a2'], st['a2'].ap[:, :], st['thr'], st['thr'].ap[:, :], AF.Exp, bias=dv.ap[:, 16 + c:17 + c],
                  scale=dv.ap[:, 16 + c:17 + c], reads=[dv])
            K.ts('dve', st['om'], st['om'].ap[:, :], st['a2'], st['a2'].ap[:, :], -1.0, 1.0, ALU.mult, ALU.add)
            K.act(st['sr'], st['sr'].ap[:, :], st['om'], st['om'].ap[:, :], AF.Ln)
            K.act(st['sr'], st['sr'].ap[:, :], st['sr'], st['sr'].ap[:, :], AF.Exp, scale=0.5)
            K.stt(st['t2'], st['t2'].ap[:, :], st['thi'], st['thi'].ap[:, :], 1.0, st['xc'], st['xc'].ap[:, :],
                  ALU.add, ALU.mult)
            K.stt(st['u'], st['u'].ap[:, :], st['t2'], st['t2'].ap[:, :], 0.5, st['sr'], st['sr'].ap[:, :],
                  ALU.mult, ALU.mult)
            init = 0.0 if i == 0 else hcar.ap[:, c:c + 1]
            K.op('dve', lambda e, st=st, init=init: e.tensor_tensor_scan(st['hl'].ap[:, :], st['a'].ap[:, :],
                                                                       st['u'].ap[:, :], init, ALU.mult, ALU.add),
                 reads=[st['a'], st['u'], hcar], writes=[st['hl']])
            K.copy('pool', hcar, hcar.ap[:, c:c + 1], st['hl'], st['hl'].ap[:, 511:512])
            K.tt('dve', st['oa'], st['oa'].ap[:, :], st['hl'], st['hl'].ap[:, :], gl, gl.ap[:, c, :], ALU.mult)
            K.dma('sp', S['MA'][c][i], S['MA'][c][i].ap[:, i * TT:(i + 1) * TT], st['oa'], st['oa'].ap[:, :])


def rec_b_phase(K, C, XT, li, Win_ap, Wout_ap, S, vo, bv_d, bo):
    K.sb_set(C['sb_base'])
    phase_consts(K, C, ['U_f32', 'ones_f32', 'neg_bf'])
    wo, pwo, sego = load_weight_cast(K, Wout_ap, 16, D, f"rb_wo{li}", seg=1024)
    wzbig = K.alloc(f"rb_wz{li}", [8, 1040], BF16)
    wz = wzbig.ap
    Wv = Win_ap.rearrange("(c p) n -> p c n", p=128)
    pwz = {}
    for kc in range(8):
        t = K.alias(wzbig, f"rb_wz{li}_{kc}")
        K.dma('pool', t, wz[:, kc, 0:1024], K.dram_in, Wv[:, kc, 2048:3072])
        K.dma('pool', t, wz[:, kc, 1024:1040], K.dram_in, Wv[:, kc, 4608:4624])
        pwz[kc] = t
    bv = K.alloc("rb_bv", [1072], F32)
    K.dma('sp', bv, bv.ap, K.dram_in, bv_d[:, bo:bo + 1072])
    Aneg = K.alloc("rb_A", [16], F32)
    K.act(Aneg, Aneg.ap[:, :], bv, bv.ap[:, 16:32], AF.Exp)
    K.ts('dve', Aneg, Aneg.ap[:, :], Aneg, Aneg.ap[:, :], -1.0, None, ALU.mult)
    DI = K.alloc("rb_DI", [16, 128], BF16)
    for h in range(16):
        K.ts('pool', DI, DI.ap[:, h, :], C['ident_bf'], C['ident_bf'].ap[:, :], bv.ap[:, 32 + h:33 + h], None,
             ALU.mult, reads=[bv])
    xts = [K.alloc("rb_xt", [8, TT], F32)] * 2
    hTs = [K.alloc(f"rb_hT{b}", [8, TT], BF16) for b in range(2)]
    xbcs = [K.alloc(f"rb_xbc{b}", [12, TT], BF16) for b in range(2)]
    mAs = [K.alloc("rb_mA", [8, TT], BF16)] * 2
    mB = K.alloc("rb_mB", [8, TT], BF16)
    ytok = K.alloc("rb_ytok", [4, D], F32)
    Sst = K.alloc("rb_S", [D], F32)
    prevb = K.alloc("rb_prev", [D], BF16)
    rhsR = K.alloc("rb_rhsR", [16 * 128], F32)
    Eexp = K.alloc("rb_E", [16 * 128], F32)
    MT = K.alloc("rb_MT", [16, 128], BF16)
    xsb = K.alloc("rb_xsb", [D], BF16)
    xsw = K.alloc("rb_xsw", [D], BF16)
    Btok = K.alloc("rb_Btok", [256], BF16)
    tmpF = K.alloc("rb_tmpF", [512], F32)
    bcw = K.alloc("rb_bcw", [D], F32)
    bce = K.alloc("rb_bce", [D], F32)
    bcc = K.alloc("rb_bcc", [D], F32)
    small = {n: K.alloc(f"rb_{n}", [16], F32) for n in ['v', 'e', 'dt', 'lndt', 'adt', 'cs', 'bE', 'wx', 'w', 'cd',
                                                        'ecs']}
    sz = K.alloc("rb_sz", [512], F32)
    yz = K.alloc("rb_yz", [512], F32)
    ysq = K.alloc("rb_ysq", [512], BF16)
    ss = K.alloc("rb_ss", [8], F32)
    ss2 = K.alloc("rb_ss2", [8], F32)
    yb = K.alloc("rb_yb", [512], BF16)
    ps_small = K.psum[0]
    ps_R = K.psum[1]
    ps_X = K.psum[2]
    ps_G = K.psum[3]
    ps_Y = K.psum[4]
    ps_F = K.psum[5]
    ps_St = K.psum[6]
    ps_Z = K.psum[7]
    identb = C['ident_bf']

    def load(i):
        b = i % 2
        K.dma('sp', hTs[b], hTs[b].ap, S['HT'][i], S['HT_d'].rearrange("c p n -> p c n")[:, :, i * TT:(i + 1) * TT])
        K.op('sp', lambda e, b=b, i=i: e.dma_start(
            out=xbcs[b].ap, in_=S['XBC_d'].rearrange("c p n -> p c n")[:, :, i * TT:(i + 1) * TT]),
            reads=[S['XBC'][cc][i] for cc in range(12)], writes=[xbcs[b]], dma=True)

    def load_single(i):
        b = i % 2
        K.dma('sp', xts[b], xts[b].ap, XT[i], xt_tile_ap(XT[i].ap, i))
        K.op('sp', lambda e, b=b, i=i: e.dma_start(
            out=mAs[b].ap, in_=S['MA_d'].rearrange("c p n -> p c n")[:, :, i * TT:(i + 1) * TT]),
            reads=[S['MA'][c][i] for c in range(8)], writes=[mAs[b]], dma=True)

    load(0)
    for i in range(NT):
        b = i % 2
        xt, hT, xbc, mA = xts[b], hTs[b], xbcs[b], mAs[b]
        load_single(i)
        if i + 1 < NT:
            load(i + 1)
        RB = float(os.environ.get('RB_STOP', '99'))
        for q in range(4):
            cg = 4 * i + q
            tc = slice(128 * q, 128 * q + 128)
            sm = small
            if RB < 2:
                continue
            for kc in range(8):
                K.mm(ps_small, ps_small.ap[:, 0:16], hT, hT.ap[:, kc, tc], pwz[kc], wz[:, kc, 1024:1040],
                     start=(kc == 0), stop=(kc == 7))
            K.tt('dve', sm['v'], sm['v'].ap[:, :], ps_small, ps_small.ap[:, 0:16], bv, bv.ap[:, 0:16], ALU.add)
            K.act(sm['e'], sm['e'].ap[:, :], sm['v'], sm['v'].ap[:, :], AF.Exp)
            K.act(sm['dt'], sm['dt'].ap[:, :], sm['e'], sm['e'].ap[:, :], AF.Ln, bias=C['one'].ap[:, 0:1],
                  reads=[C['one']])
            K.act(sm['lndt'], sm['lndt'].ap[:, :], sm['dt'], sm['dt'].ap[:, :], AF.Ln)
            K.tt('dve', sm['adt'], sm['adt'].ap[:, :], sm['dt'], sm['dt'].ap[:, :], Aneg, Aneg.ap[:, :], ALU.mult)
            K.mm(ps_small, ps_small.ap[:, 16:32], C['U_f32'], C['U_f32'].ap[:, :], sm['adt'], sm['adt'].ap[:, :],
                 True, True)
            K.mm(ps_small, ps_small.ap[:, 32:48], C['ones_f32'], C['ones_f32'].ap[:, :], sm['adt'],
                 sm['adt'].ap[:, :], True, True)
            K.copy('dve', sm['cs'], sm['cs'].ap[:, :], ps_small, ps_small.ap[:, 16:32])
            K.tt('dve', sm['bE'], sm['bE'].ap[:, :], sm['lndt'], sm['lndt'].ap[:, :], sm['cs'], sm['cs'].ap[:, :],
                 ALU.subtract)
            K.tt('dve', sm['wx'], sm['wx'].ap[:, :], ps_small, ps_small.ap[:, 32:48], sm['bE'], sm['bE'].ap[:, :],
                 ALU.add)
            K.act(sm['w'], sm['w'].ap[:, :], sm['wx'], sm['wx'].ap[:, :], AF.Exp)
            K.act(sm['cd'], sm['cd'].ap[:, :], ps_small, ps_small.ap[:, 32:48], AF.Exp)
            K.act(sm['ecs'], sm['ecs'].ap[:, :], sm['cs'], sm['cs'].ap[:, :], AF.Exp)
            if RB < 3:
                continue
            if cg <= 1:
                K.dbg(f"dt{cg}", sm['dt'], sm['dt'].ap[:, :])
                K.dbg(f"cs{cg}", sm['cs'], sm['cs'].ap[:, :])
                K.dbg(f"w{cg}", sm['w'], sm['w'].ap[:, :])
                K.dbg(f"cd{cg}", sm['cd'], sm['cd'].ap[:, :])
            K.tt('pool', rhsR, rhsR.ap.rearrange("p (h l) -> p h l", h=16),
                 C['U_f32'], C['U_f32'].ap.unsqueeze(1).to_broadcast([128, 16, 128]),
                 sm['adt'], sm['adt'].ap.unsqueeze(2).to_broadcast([128, 16, 128]), ALU.mult)
            for bb in range(4):
                K.mm(ps_R, ps_R.ap[:, :], C['ones_f32'], C['ones_f32'].ap[:, :], rhsR,
                     rhsR.ap[:, 512 * bb:512 * bb + 512], True, False)
                K.mm(ps_R, ps_R.ap.rearrange("p (h l) -> p h l", h=4), identb, identb.ap[:, :], C['neg_bf'],
                     C['neg_bf'].ap.unsqueeze(1).to_broadcast([128, 4, 128]), False, True)
                for hh in range(4):
                    h = 4 * bb + hh
                    K.act(Eexp, Eexp.ap[:, 128 * h:128 * h + 128], ps_R, ps_R.ap[:, 128 * hh:128 * hh + 128], AF.Exp,
                          bias=sm['bE'].ap[:, h:h + 1], reads=[sm['bE']])
            if RB < 4:
                continue
            pxb = ps_X.ap.bitcast(BF16)
            for c in range(8):
                K.transpose(ps_X, pxb[:, 128 * c:128 * c + 128], xbc, xbc.ap[:, c, tc], identb, identb.ap[:, :])
            if RB < 4.2:
                continue
            K.copy('act', xsb, xsb.ap[:, :], ps_X, pxb[:, :])
            if RB < 4.4:
                continue
            K.copy('pool', bcw, bcw.ap.rearrange("p (h e) -> p h e", h=16), sm['w'],
                   sm['w'].ap.unsqueeze(2).to_broadcast([128, 16, 64]))
            if RB < 4.47:
                continue
            K.tt('dve', xsw, xsw.ap[:, :], xsb, xsb.ap[:, :], bcw, bcw.ap[:, :], ALU.mult)
            if RB < 4.6:
                continue
            pgb = ps_G.ap.bitcast(BF16)
            for g in range(2):
                K.transpose(ps_G, pgb[:, 512 + 128 * g:512 + 128 * g + 128], xbc, xbc.ap[:, 8 + g, tc], identb,
                            identb.ap[:, :])
            K.copy('act', Btok, Btok.ap[:, :], ps_G, pgb[:, 512:768])
            if RB < 5:
                continue
            for g in range(2):
                K.mm(ps_G, ps_G.ap[:, 128 * g:128 * g + 128], xbc, xbc.ap[:, 8 + g, tc], xbc, xbc.ap[:, 10 + g, tc],
                     True, True)
            for g in range(2):
                K.tt('dve', MT, MT.ap[:, 8 * g:8 * g + 8, :], Eexp,
                     Eexp.ap[:, 1024 * g:1024 * g + 1024].rearrange("p (h l) -> p h l", h=8), ps_G,
                     ps_G.ap[:, 128 * g:128 * g + 128].unsqueeze(1).to_broadcast([128, 8, 128]), ALU.mult)
            if RB < 6:
                continue
            if cg <= 1:
                K.dbg(f"E{cg}", Eexp, Eexp.ap[:, :])
                K.dbg(f"xsb{cg}", xsb, xsb.ap[:, :], BF16)
                K.dbg(f"xsw{cg}", xsw, xsw.ap[:, :], BF16)
                K.dbg(f"Btok{cg}", Btok, Btok.ap[:, :], BF16)
                K.dbg(f"MT{cg}", MT, MT.ap.rearrange("p h l -> p (h l)"), BF16)
            for g in range(2):
                for hh in range(8):
                    h = 8 * g + hh
                    K.mm(ps_Y, ps_Y.ap[:, 64 * hh:64 * hh + 64], MT, MT.ap[:, h, :], xsb, xsb.ap[:, 64 * h:64 * h + 64],
                         True, False)
                    K.mm(ps_Y, ps_Y.ap[:, 64 * hh:64 * hh + 64], DI, DI.ap[:, h, :], xsb, xsb.ap[:, 64 * h:64 * h + 64],
                         False, True)
                yslot = ytok.ap[:, q, 512 * g:512 * g + 512]
                if cg > 0:
                    K.mm(ps_F, ps_F.ap[:, :], xbc, xbc.ap[:, 10 + g, tc], prevb, prevb.ap[:, 512 * g:512 * g + 512],
                         True, True)
                    if g == 0:
                        K.copy('pool', bce, bce.ap.rearrange("p (h e) -> p h e", h=16), sm['ecs'],
                               sm['ecs'].ap.unsqueeze(2).to_broadcast([128, 16, 64]))
                    K.tt('dve', tmpF, tmpF.ap[:, :], ps_F, ps_F.ap[:, :], bce, bce.ap[:, 512 * g:512 * g + 512],
                         ALU.mult)
                    K.tt('dve', ytok, yslot, ps_Y, ps_Y.ap[:, :], tmpF, tmpF.ap[:, :], ALU.add)
                else:
                    K.copy('dve', ytok, yslot, ps_Y, ps_Y.ap[:, :])
            if RB < 7:
                continue
            if cg < 31:
                for g in range(2):
                    K.mm(ps_St, ps_St.ap[:, :], Btok, Btok.ap[:, 128 * g:128 * g + 128], xsw,
                         xsw.ap[:, 512 * g:512 * g + 512], True, True)
                    sv = Sst.ap[:, 512 * g:512 * g + 512]
                    if cg > 0:
                        if g == 0:
                            K.copy('pool', bcc, bcc.ap.rearrange("p (h e) -> p h e", h=16), sm['cd'],
                                   sm['cd'].ap.unsqueeze(2).to_broadcast([128, 16, 64]))
                        K.tt('pool', Sst, sv, Sst, sv, bcc, bcc.ap[:, 512 * g:512 * g + 512], ALU.mult)
                        K.tt('dve', Sst, sv, ps_St, ps_St.ap[:, :], Sst, sv, ALU.add)
                    else:
                        K.copy('dve', Sst, sv, ps_St, ps_St.ap[:, :])
                K.copy('pool', prevb, prevb.ap[:, :], Sst, Sst.ap[:, :])
                if cg <= 1:
                    K.dbg(f"S{cg}", Sst, Sst.ap[:, :])
        if RB >= 8:
            for q in range(4):
                tc = slice(128 * q, 128 * q + 128)
                for g in range(2):
                    for kc in range(8):
                        K.mm(ps_Z, ps_Z.ap[:, :], hT, hT.ap[:, kc, tc], pwz[kc], wz[:, kc, 512 * g:512 * g + 512],
                             start=(kc == 0), stop=(kc == 7))
                    K.act(sz, sz.ap[:, :], ps_Z, ps_Z.ap[:, :], AF.Silu)
                    ysl = ytok.ap[:, q, 512 * g:512 * g + 512]
                    K.tt('dve', ytok, ysl, ytok, ysl, sz, sz.ap[:, :], ALU.mult)
                    K.act(ysq, ysq.ap[:, :], ytok, ysl, AF.Square, accum=ss.ap[:, 2 * q + g:2 * q + g + 1],
                          extra_w=[ss])
            K.act(ss2, ss2.ap[:, :], ss, ss.ap[:, :], AF.Ln, bias=C['eps'].ap[:, 0:1], scale=1.0 / 512,
                  reads=[C['eps']])
            K.act(ss2, ss2.ap[:, :], ss2, ss2.ap[:, :], AF.Exp, scale=-0.5)
            for q in range(4):
                tc = slice(128 * q, 128 * q + 128)
                for g in range(2):
                    ysl = ytok.ap[:, q, 512 * g:512 * g + 512]
                    K.stt(yb, yb.ap[:, :], ytok, ysl, ss2.ap[:, 2 * q + g:2 * q + g + 1], bv,
                          bv.ap[:, 48 + 512 * g:48 + 512 * g + 512], ALU.mult, ALU.mult, reads=[ss2])
                    ptr = ps_X.ap.bitcast(BF16)
                    for k4 in range(4):
                        K.transpose(ps_X, ptr[:, 128 * k4:128 * k4 + 128], yb, yb.ap[:, 128 * k4:128 * k4 + 128],
                                    identb, identb.ap[:, :])
                    K.copy('act', mB, mB.ap[:, 4 * g:4 * g + 4, tc], ps_X,
                           ptr[:, 0:512].rearrange("p (k t) -> p k t", k=4))
        if i == 0:
            K.dbg("ytok", ytok, ytok.ap.rearrange("p q d -> p (q d)"))
            K.dbg("mB", mB, mB.ap.rearrange("p c t -> p (c t)"), BF16)
        for m in range(8):
            po = [ps_Y, ps_F, ps_St, ps_Z][m % 4]
            for k in range(16):
                src, sap = (mA, mA.ap[:, k, :]) if k < 8 else (mB, mB.ap[:, k - 8, :])
                K.mm(po, po.ap[:, :], pwo[(k, 0)], wo[:, k, m * 128:(m + 1) * 128], src, sap,
                     start=(k == 0), stop=(k == 15))
            K.tt('dve', xt, xt.ap[:, m, :], po, po.ap[:, :], xt, xt.ap[:, m, :], ALU.add)
        K.dma('sp', XT[i], xt_tile_ap(XT[i].ap, i), xt, xt.ap)


WEIGHT_SHAPES = {
    'rec_w_in': [2, D, 4624], 'rec_w_out': [2, 2048, D],
    'lru_w_r': [2, 16, 64, 64], 'lru_w_i': [2, 16, 64, 64],
    'att_w_qkv': [2, D, 3 * D], 'att_w_out': [2, D, D],
    'ffn_w_gate_up': [4, D, 2 * FH], 'ffn_w_down': [4, FH, D],
}


def rope_tables():
    half = 8
    inv = (np.float32(500000.0) ** (-2.0 * np.arange(half, dtype=np.float32) / np.float32(16))).astype(np.float32)
    pos = np.arange(L, dtype=np.float32)
    ang = (pos[:, None] * inv[None, :]).astype(np.float32)
    cos = np.cos(ang).astype(np.float32).T
    sin = np.sin(ang).astype(np.float32).T
    COS = np.ones((128, L), np.float32)
    SIN = np.zeros((128, L), np.float32)
    for h in range(2):
        COS[64 * h:64 * h + 8] = cos
        COS[64 * h + 8:64 * h + 16] = cos
        SIN[64 * h:64 * h + 8] = sin
        SIN[64 * h + 8:64 * h + 16] = sin
    return COS, SIN


def build_consts_host():
    c = {}
    c['ones_bf'] = np.ones((128, 128), dtype=ml_dtypes.bfloat16)
    c['eps'] = np.full((128, 1), EPS, dtype=np.float32)
    blk = np.zeros((128, 128), np.float32)
    blk[:64, :64] = 1
    blk[64:, 64:] = 1
    c['blk64'] = blk.astype(ml_dtypes.bfloat16)
    P = np.zeros((128, 128), np.float32)
    for h in range(2):
        for e in range(8):
            P[64 * h + e + 8, 64 * h + e] = -1.0
            P[64 * h + e, 64 * h + e + 8] = 1.0
    c['ropeP'] = P
    cos, sin = rope_tables()
    c['cos_d'] = cos
    c['sin_d'] = sin
    k = np.arange(128)[:, None]
    q = np.arange(128)[None, :]
    m = np.concatenate([(k <= q), (k >= q)], axis=1).astype(np.float32)
    c['mask4'] = np.concatenate([m, m], axis=1).astype(ml_dtypes.bfloat16)
    c['ident_bf'] = np.eye(128, dtype=np.float32).astype(ml_dtypes.bfloat16)
    c['U_f32'] = (k <= q).astype(np.float32)
    c['ones_f32'] = np.ones((128, 128), np.float32)
    c['neg_bf'] = np.where(q < k, -32768.0, 0.0).astype(np.float32).astype(ml_dtypes.bfloat16)
    c['half'] = np.full((128, 1), 0.5, np.float32)
    c['neghalf'] = np.full((128, 1), -0.5, np.float32)
    c['one'] = np.ones((128, 1), np.float32)
    return c


CONST_SPECS = [('ones_bf', [128, 128], BF16), ('eps', [128, 1], F32), ('ident_bf', [128, 128], BF16),
               ('half', [128, 1], F32), ('neghalf', [128, 1], F32), ('one', [128, 1], F32)]
CONST_PHASE = [('blk64', [128, 128], BF16), ('ropeP', [128, 128], F32), ('mask4', [128, 512], BF16),
               ('U_f32', [128, 128], F32), ('ones_f32', [128, 128], F32), ('neg_bf', [128, 128], BF16)]
CONST_DRAM_ONLY = [('cos_d', [128, L], F32), ('sin_d', [128, L], F32)]


def build_program(phases, nvec):
    nc = bass.Bass("TRN2", target_bir_lowering=False)
    K = KB(nc)
    K.dram_in = Tile(None, "dram_in", ro=True)
    xin = nc.dram_tensor("xT", [D, L], F32, kind="ExternalInput").ap()
    yout = nc.dram_tensor("yT", [D, L], F32, kind="ExternalOutput").ap()
    vecs_d = nc.dram_tensor("vecs", [128, nvec], F32, kind="ExternalInput").ap()
    bv_d = nc.dram_tensor("bvecs", [128, 2 * 1072], F32, kind="ExternalInput").ap()
    W = {}
    for name, shp in WEIGHT_SHAPES.items():
        W[name] = nc.dram_tensor(name, shp, F32, kind="ExternalInput").ap()

    C = {}
    for name, shp, dt in CONST_SPECS:
        d_ap = nc.dram_tensor(name, shp, dt, kind="ExternalInput").ap()
        C[name] = K.alloc(name, shp[1:], dt)
        K.dma('sp', C[name], C[name].ap, K.dram_in, d_ap)
    for name, shp, dt in CONST_DRAM_ONLY:
        C[name] = nc.dram_tensor(name, shp, dt, kind="ExternalInput").ap()
    C['_phase_d'] = {name: (nc.dram_tensor(name, shp, dt, kind="ExternalInput").ap(), shp, dt)
                     for name, shp, dt in CONST_PHASE}
    C['vecs'] = K.alloc("vecs", [nvec], F32)
    K.dma('sp', C['vecs'], C['vecs'].ap, K.dram_in, vecs_d)
    C['sb_base'] = K.sb_ptr

    S = {}
    qt_d = nc.dram_tensor("QT_s", [8, 128, L], BF16, kind="Internal").ap()
    kt_d = nc.dram_tensor("KT_s", [8, 128, L], BF16, kind="Internal").ap()
    vp_d = nc.dram_tensor("VP_s", [3, 8, 128, 32, 128], BF16, kind="Internal").ap()
    ot_d = nc.dram_tensor("OT_s", [8, 128, L], BF16, kind="Internal").ap()
    S['QT'] = [[Tile(qt_d[c], f"QT{c}_{i}") for i in range(NT)] for c in range(8)]
    S['KT'] = [[Tile(kt_d[c], f"KT{c}_{i}") for i in range(NT)] for c in range(8)]
    S['VP'] = [[[Tile(vp_d[p, hp], f"VP{p}_{hp}_{s}") for s in range(2)] for hp in range(8)] for p in range(3)]
    S['OT'] = [Tile(ot_d[hp], f"OT{hp}") for hp in range(8)]
    S['OT_d'] = ot_d
    ks = "ExternalOutput" if os.environ.get('DBG_SCR') else "Internal"
    S['HT_d'] = nc.dram_tensor("HT_s", [8, 128, L], BF16, kind=ks).ap()
    S['XBC_d'] = nc.dram_tensor("XBC_s", [12, 128, L], BF16, kind=ks).ap()
    S['MA_d'] = nc.dram_tensor("MA_s", [8, 128, L], BF16, kind=ks).ap()
    S['HT'] = [Tile(S['HT_d'], f"HT{i}") for i in range(NT)]
    S['XBC'] = [[Tile(S['XBC_d'][c], f"XBC{c}_{i}") for i in range(NT)] for c in range(12)]
    S['MA'] = [[Tile(S['MA_d'][c], f"MA{c}_{i}") for i in range(NT)] for c in range(8)]

    XT = [Tile(yout, f"XT{i}") for i in range(NT)]
    for i in range(NT):
        K.dma('sp', XT[i], yout[:, i * TT:(i + 1) * TT], K.dram_in, xin[:, i * TT:(i + 1) * TT])

    for ph in phases:
        if ph[0] == 'ffn':
            layer = ph[1]
            ffn_phase(K, C, XT, layer, W['ffn_w_gate_up'][layer], W['ffn_w_down'][layer], C['vecs'],
                      VEC_OFF['ffn_norm'] + 8 * layer)
        elif ph[0] == 'att':
            li = ph[1]
            att_qkv_phase(K, C, XT, li, W['att_w_qkv'][li], S, VEC_OFF['att_norm'] + 8 * li,
                          VEC_OFF['att_q_norm'] + li, VEC_OFF['att_k_norm'] + li)
            att_core_phase(K, C, li, S)
            att_out_phase(K, C, XT, li, W['att_w_out'][li], S)
        elif ph[0] in ('rec', 'rec_a', 'rec_b'):
            li = ph[1]
            vo = rec_vo(li)
            if ph[0] != 'rec_b':
                rec_a_phase(K, C, XT, li, W['rec_w_in'][li], W['lru_w_r'][li], W['lru_w_i'][li], S, vo)
            if ph[0] != 'rec_a':
                rec_b_phase(K, C, XT, li, W['rec_w_in'][li], W['rec_w_out'][li], S, vo, bv_d, 1072 * li)
    K.op('sp', None, reads=XT + K.dbg_tiles, writes=[])
    K.emit()
    return nc, K


VEC_OFF = {'ffn_norm': 0, 'att_norm': 32, 'att_q_norm': 48, 'att_k_norm': 50}
REC_BASE = 64
REC_STRIDE = 160
REC_FIELDS = {'rec_norm': 0, 'lru_conv_w': 8, 'lru_conv_b': 40, 'lru_b_r': 48, 'lru_b_i': 56, 'lru_lambda': 64,
              'ssd_conv_w': 72, 'ssd_conv_b': 120}
NVEC = REC_BASE + 2 * REC_STRIDE


def rec_vo(li):
    return {k: REC_BASE + REC_STRIDE * li + v for k, v in REC_FIELDS.items()}


def pack_vecs(inputs):
    v = np.zeros((128, NVEC), dtype=np.float32)
    f = lambda k: np.asarray(inputs[k], dtype=np.float32)
    fn = f('ffn_norm')
    for l in range(4):
        v[:, VEC_OFF['ffn_norm'] + 8 * l: VEC_OFF['ffn_norm'] + 8 * l + 8] = fn[l].reshape(8, 128).T
    an = f('att_norm')
    for l in range(2):
        v[:, VEC_OFF['att_norm'] + 8 * l: VEC_OFF['att_norm'] + 8 * l + 8] = an[l].reshape(8, 128).T
        v[:, VEC_OFF['att_q_norm'] + l] = np.tile(f('att_q_norm')[l], 2)
        v[:, VEC_OFF['att_k_norm'] + l] = np.tile(f('att_k_norm')[l], 2)
    for l in range(2):
        vo = rec_vo(l)
        v[:, vo['rec_norm']:vo['rec_norm'] + 8] = f('rec_norm')[l].reshape(8, 128).T
        for j in range(4):
            v[:, vo['lru_conv_w'] + 8 * j:vo['lru_conv_w'] + 8 * j + 8] = f('lru_conv_w')[l, j].reshape(8, 128).T
            v[:, vo['ssd_conv_w'] + 12 * j:vo['ssd_conv_w'] + 12 * j + 12] = f('ssd_conv_w')[l, j].reshape(12, 128).T
        for k in ('lru_conv_b', 'lru_b_r', 'lru_b_i', 'lru_lambda'):
            v[:, vo[k]:vo[k] + 8] = f(k)[l].reshape(8, 128).T
        v[:, vo['ssd_conv_b']:vo['ssd_conv_b'] + 12] = f('ssd_conv_b')[l].reshape(12, 128).T
    return v


def pack_bvecs(inputs):
    f = lambda k: np.asarray(inputs[k], dtype=np.float32)
    b = np.zeros((128, 2 * 1072), np.float32)
    for l in range(2):
        row = np.concatenate([f('ssd_dt_bias')[l], f('ssd_a_log')[l], f('ssd_d')[l], f('ssd_norm')[l]])
        b[:, 1072 * l:1072 * (l + 1)] = row[None, :]
    return b


def run(inputs, phases, n_cores=8, trace=False):
    x = np.asarray(inputs['x'], dtype=np.float32)
    nc, K = build_program(phases, NVEC)
    consts = build_consts_host()
    vecs = pack_vecs(inputs)
    shared = {"vecs": vecs, "bvecs": pack_bvecs(inputs)}
    shared.update(consts)
    for name in WEIGHT_SHAPES:
        shared[name] = np.ascontiguousarray(np.asarray(inputs[name], dtype=np.float32))
    in_maps = []
    for c in range(n_cores):
        m = dict(shared)
        m["xT"] = np.ascontiguousarray(x[c].T)
        in_maps.append(m)
    res = run_bass_kernel_spmd(nc, in_maps, core_ids=list(range(n_cores)), trace=trace)
    out = np.stack([np.ascontiguousarray(res.results[c]["yT"].T) for c in range(n_cores)], axis=0)
    if trace:
        return out, res, K
    if os.environ.get('DBG_SCR'):
        return out, res
    return out


def kernel(**inputs):
    phases = [('rec', 0), ('ffn', 0), ('att', 0), ('ffn', 1), ('rec', 1), ('ffn', 2), ('att', 1), ('ffn', 3)]
    return run(inputs, phases)
```

```python
import os
import numpy as np
import ml_dtypes
import concourse.bass as bass
import concourse.mybir as mybir
from concourse.bass_utils import run_bass_kernel_spmd

F32 = mybir.dt.float32
BF16 = mybir.dt.bfloat16
AF = mybir.ActivationFunctionType
ALU = mybir.AluOpType

ENG_ATTR = {'pe': 'tensor', 'act': 'scalar', 'dve': 'vector', 'pool': 'gpsimd', 'sp': 'sync'}
SEM_LIMIT = 30000
NPOOL = 12

D = 1024
L = 4096
NT = 8
TT = 512
FH = 2816
NJ = FH // 128
EPS = 1e-6


def fsize(ap):
    n = 1
    for d in ap.shape[1:]:
        n *= d
    return n


def nbytes(ap):
    n = 1
    for d in ap.shape:
        n *= d
    return n * (2 if ap.dtype == BF16 else 4)


class Op:
    __slots__ = ('eng', 'fn', 'deps', 'idx', 'dma', 'sem', 'cnt', 'snap', 'sig', 'sigval',
                 'waits', 'key', 'cost', 'pidx', 'nrem', 'succ', 'ready', 'fin')


class Tile:
    __slots__ = ('ap', 'w', 'r', 'name', 'ro', 'span')

    def __init__(self, ap, name='', ro=False):
        self.ap = ap
        self.w = None
        self.r = []
        self.name = name
        self.ro = ro
        self.span = None


class KB:
    def __init__(self, nc):
        self.nc = nc
        self.ops = []
        self.nidx = {e: 0 for e in ENG_ATTR}
        self.arena = nc.alloc_sbuf_tensor("arena", [128, 212000 // 4], F32)
        self.sb_ptr = 0
        self.dbg_tiles = []
        self.regions = []
        self.psum = [Tile(nc.alloc_psum_tensor(f"psb{i}", [128, 512], F32), f"ps{i}")
                     for i in range(8)]

    def sb_set(self, ptr):
        self.sb_ptr = ptr

    def alloc(self, name, free_shape, dtype):
        esz = 2 if dtype == BF16 else 4
        n = 1
        for s in free_shape:
            n *= s
        nbytes = (n * esz + 31) // 32 * 32
        start = self.sb_ptr
        end = start + nbytes
        assert end <= 212000, f"SBUF overflow allocating {name}: {end}"
        self.sb_ptr = end
        ap = self.arena[:, start // 4:end // 4]
        if dtype == BF16:
            ap = ap.bitcast(BF16)
        ap = ap[:, 0:n]
        if len(free_shape) == 2:
            ap = ap.rearrange("p (a b) -> p a b", a=free_shape[0])
        elif len(free_shape) == 3:
            ap = ap.rearrange("p (a b c) -> p a b c", a=free_shape[0], b=free_shape[1])
        t = Tile(ap, name)
        t.span = (start, end)
        keep = []
        for (s, e, old) in self.regions:
            if s < end and start < e:
                if old.w is not None:
                    t.r.append(old.w)
                t.r.extend(old.r)
                if s < start:
                    keep.append((s, start, old))
                if end < e:
                    keep.append((end, e, old))
            else:
                keep.append((s, e, old))
        keep.append((start, end, t))
        self.regions = keep
        return t

    def alias(self, big, name):
        t = Tile(big.ap, name)
        t.span = big.span
        if big.w is not None:
            t.r.append(big.w)
        t.r.extend(big.r)
        self.regions.append((big.span[0], big.span[1], t))
        return t

    def op(self, eng, fn, reads=(), writes=(), dma=False, cost=500.0):
        o = Op()
        o.cost = cost
        o.eng = eng
        o.fn = fn
        o.dma = dma
        o.sig = False
        o.idx = 0
        deps = {}
        reads = [t for t in reads if not t.ro]
        for t in reads:
            if t.w is not None:
                deps[id(t.w)] = t.w
        for t in writes:
            if t.w is not None:
                deps[id(t.w)] = t.w
            for x in t.r:
                deps[id(x)] = x
        o.deps = list(deps.values())
        if not dma:
            self.nidx[eng] += 1
            o.idx = self.nidx[eng]
        wset = set(id(t) for t in writes)
        for t in writes:
            t.w = o
            t.r = []
        for t in reads:
            if id(t) in wset:
                continue
            t.r.append(o)
        self.ops.append(o)
        return o

    def mm(self, ps, out_ap, lt, lhsT_ap, rt, rhs_ap, start, stop, extra=()):
        n = fsize(rhs_ap)
        c = 30.0 + 0.5 * max(n, 64) * (4.0 if rhs_ap.dtype == F32 else 1.0)
        return self.op('pe', lambda e: e.matmul(out_ap, lhsT=lhsT_ap, rhs=rhs_ap, start=start, stop=stop),
                       reads=[lt, rt] + list(extra), writes=[ps], cost=c)

    def transpose(self, ps, out_ap, it, in_ap, idt, ident_ap):
        return self.op('pe', lambda e: e.transpose(out_ap, in_ap, ident_ap), reads=[it, idt], writes=[ps],
                       cost=150.0)

    def act(self, ot, out_ap, it, in_ap, func, bias=None, scale=None, reads=(), accum=None, eng='act',
            extra_w=()):
        kw = {}
        if bias is not None:
            kw['bias'] = bias
        if scale is not None:
            kw['scale'] = scale
        if accum is not None:
            kw['accum_out'] = accum
        return self.op(eng, lambda e: e.activation(out_ap, in_ap, func, **kw),
                       reads=[it] + list(reads), writes=[ot] + list(extra_w), cost=220.0 + 0.85 * fsize(in_ap))

    def tt(self, eng, ot, out_ap, at, a_ap, bt, b_ap, op):
        c = (150.0 + 1.2 * fsize(out_ap)) if eng == 'dve' else (300.0 + 1.6 * fsize(out_ap))
        return self.op(eng, lambda e: e.tensor_tensor(out_ap, a_ap, b_ap, op), reads=[at, bt], writes=[ot], cost=c)

    def ts(self, eng, ot, out_ap, at, a_ap, s1, s2, op0, op1=None, reads=()):
        c = (120.0 + 0.7 * fsize(out_ap)) if eng == 'dve' else 2000.0
        if op1 is None:
            return self.op(eng, lambda e: e.tensor_scalar(out_ap, a_ap, s1, None, op0),
                           reads=[at] + list(reads), writes=[ot], cost=c)
        return self.op(eng, lambda e: e.tensor_scalar(out_ap, a_ap, s1, s2, op0, op1),
                       reads=[at] + list(reads), writes=[ot], cost=c)

    def stt(self, ot, out_ap, at, a_ap, scalar, bt, b_ap, op0, op1, reads=()):
        return self.op('dve', lambda e: e.scalar_tensor_tensor(out_ap, a_ap, scalar, b_ap, op0, op1),
                       reads=[at, bt] + list(reads), writes=[ot], cost=150.0 + 1.2 * fsize(out_ap))

    def copy(self, eng, ot, out_ap, it, in_ap):
        n = fsize(out_ap)
        if eng == 'act':
            return self.op(eng, lambda e: e.copy(out_ap, in_ap), reads=[it], writes=[ot], cost=220.0 + 0.85 * n)
        c = (120.0 + 0.8 * n) if eng == 'dve' else (300.0 + 1.0 * n)
        return self.op(eng, lambda e: e.tensor_copy(out_ap, in_ap), reads=[it], writes=[ot], cost=c)

    def memset(self, eng, ot, out_ap, val):
        return self.op(eng, lambda e: e.memset(out_ap, val), reads=[], writes=[ot], cost=100.0 + 0.5 * fsize(out_ap))

    def dma(self, eng, ot, out_ap, it, in_ap, **kw):
        return self.op(eng, lambda e: e.dma_start(out=out_ap, in_=in_ap, **kw), reads=[it], writes=[ot],
                       dma=True, cost=float(nbytes(out_ap)))

    def dbg(self, name, tile, ap, dtype=F32):
        if not os.environ.get('DBG_SCR'):
            return
        d = self.nc.dram_tensor("dbg_" + name, list(ap.shape), dtype, kind="ExternalOutput").ap()
        t = Tile(d, "dbg_" + name)
        self.dma('sp', t, d, tile, ap)
        self.dbg_tiles.append(t)

    def schedule(self):
        import heapq
        if os.environ.get('NO_SCHED'):
            return self.ops
        ops = self.ops
        for i, o in enumerate(ops):
            o.pidx = i
            o.succ = []
            o.ready = 0.0
            o.fin = None
        for o in ops:
            o.nrem = len(o.deps)
            for d in o.deps:
                d.succ.append(o)
        use_cp = os.environ.get('SCHED_CP', '1') == '1'
        if use_cp:
            for o in reversed(ops):
                c = (o.cost / 180.0 + 2000.0) if o.dma else o.cost
                b = 0.0
                for sx in o.succ:
                    if sx.ready > b:
                        b = sx.ready
                o.ready = b + c
            bl = [o.ready for o in ops]
            mx = max(bl) + 1.0
            W = float(os.environ.get('SCHED_MIX', '4.0'))
            prio = [i - W * (bl[i] / mx) * len(ops) for i in range(len(ops))]
            for i, o in enumerate(ops):
                o.pidx = prio[i]
                o.ready = 0.0
        waiting = {e: [] for e in ENG_ATTR}
        avail = {e: [] for e in ENG_ATTR}
        free_at = {e: 0.0 for e in ENG_ATTR}
        dma_bw_free = [0.0]
        for o in ops:
            if o.nrem == 0:
                heapq.heappush(waiting[o.eng], (0.0, o.pidx, id(o), o))
        out = []
        n = len(ops)
        WIN = int(os.environ.get('SCHED_WIN', '4000'))
        done_upto = [0]
        while len(out) < n:
            best = None
            for e in ENG_ATTR:
                w = waiting[e]
                a = avail[e]
                while w and w[0][0] <= free_at[e]:
                    r, p, _i, o = heapq.heappop(w)
                    heapq.heappush(a, (p, _i, o))
                if a:
                    cand = (free_at[e], a[0][0], e, 0)
                elif w:
                    cand = (w[0][0], w[0][1], e, 1)
                else:
                    continue
                if best is None or cand < best:
                    best = cand
            start, p, e, src = best
            if src == 0:
                p, _i, o = heapq.heappop(avail[e])
            else:
                r, p, _i, o = heapq.heappop(waiting[e])
            if o.dma:
                issue = 1000.0 if e == 'pool' else 100.0
                t0 = max(start, dma_bw_free[0])
                dma_bw_free[0] = t0 + o.cost / 180.0
                o.fin = t0 + o.cost / 180.0 + 2000.0
                free_at[e] = start + issue
            else:
                o.fin = start + o.cost
                free_at[e] = o.fin
            out.append(o)
            for sx in o.succ:
                sx.nrem -= 1
                if sx.ready < o.fin:
                    sx.ready = o.fin
                if sx.nrem == 0:
                    heapq.heappush(waiting[sx.eng], (sx.ready, sx.pidx, id(sx), sx))
        self.sim_time = max(o.fin for o in out)
        return out

    def emit(self):
        nc = self.nc
        clock = {e: {} for e in ENG_ATTR}
        pools = {}
        rr = {}
        nid = [0]

        def newslot(e):
            nid[0] += 1
            return {'sem': nc.alloc_semaphore(f"d_{e}_{nid[0]}"), 'cnt': 0, 'last': None, 'id': nid[0]}

        self.ops = self.schedule()
        for e in ENG_ATTR:
            k = 0
            for o in self.ops:
                if o.eng == e and not o.dma:
                    k += 1
                    o.idx = k
        for o in self.ops:
            e = o.eng
            deps = list(o.deps)
            if o.dma:
                pool = pools.setdefault(e, [])
                if len(pool) < NPOOL:
                    pool.append(newslot(e))
                    i = len(pool) - 1
                    rr[e] = 0
                else:
                    i = rr[e]
                    rr[e] = (i + 1) % NPOOL
                    if pool[i]['cnt'] + 16 > SEM_LIMIT:
                        last = pool[i]['last']
                        pool[i] = newslot(e)
                        pool[i]['carry'] = last
                slot = pool[i]
                if slot['last'] is not None:
                    deps.append(slot['last'])
                slot['cnt'] += 16
                slot['last'] = o
                o.sem = slot['sem']
                o.cnt = slot['cnt']
                o.key = ('d', slot['id'])
            ck = clock[e]
            waits = []
            deps.sort(key=lambda d: -(d.cnt if d.dma else d.idx))
            for d in deps:
                if d.dma:
                    key = d.key
                    val = d.cnt
                else:
                    if d.eng == 'pe' and e == 'pe' and not o.dma:
                        continue
                    key = d.eng
                    val = d.idx
                if ck.get(key, 0) >= val:
                    continue
                waits.append(d)
                d.sig = True
                nk = dict(ck)
                for k, v in d.snap.items():
                    if nk.get(k, 0) < v:
                        nk[k] = v
                if nk.get(key, 0) < val:
                    nk[key] = val
                ck = nk
            clock[e] = ck
            o.snap = ck
            o.waits = waits

        cnt = {e: 0 for e in ENG_ATTR}
        esems = {e: [] for e in ENG_ATTR}
        for o in self.ops:
            if o.dma or not o.sig:
                continue
            c = cnt[o.eng]
            cnt[o.eng] = c + 1
            ep = c // SEM_LIMIT
            if ep >= len(esems[o.eng]):
                esems[o.eng].append(nc.alloc_semaphore(f"c_{o.eng}_{ep}"))
            o.sem = esems[o.eng][ep]
            o.sigval = c - ep * SEM_LIMIT + 1

        streams = {e: [] for e in ENG_ATTR}
        for o in self.ops:
            streams[o.eng].append(o)
        self.stats = {e: (len(streams[e]), sum(len(o.waits) for o in streams[e])) for e in ENG_ATTR}
        with nc.Block() as block:
            for e, attr in ENG_ATTR.items():
                ops = streams[e]

                def body(eng, ops=ops):
                    for o in ops:
                        for d in o.waits:
                            eng.wait_ge(d.sem, d.cnt if d.dma else d.sigval)
                        if o.fn is None:
                            continue
                        ins = o.fn(eng)
                        if o.dma:
                            ins.then_inc(o.sem, 16)
                        elif o.sig:
                            ins.then_inc(o.sem, 1)
                getattr(block, attr)(body)


def phase_consts(K, C, names):
    for n in names:
        d_ap, shp, dt = C['_phase_d'][n]
        C[n] = K.alloc("pc_" + n, shp[1:], dt)
        K.dma('sp', C[n], C[n].ap, K.dram_in, d_ap)


def xt_tile_ap(XT_ap, i):
    return XT_ap.rearrange("(c p) n -> p c n", p=128)[:, :, i * TT:(i + 1) * TT]


def load_weight_cast(K, W_ap, kchunks, ncols, name, seg=2048):
    big = K.alloc(name, [kchunks, ncols], BF16)
    pieces = {}
    Wv = W_ap.rearrange("(c p) n -> p c n", p=128)
    for kc in range(kchunks):
        for s0 in range(0, ncols, seg):
            s1 = min(ncols, s0 + seg)
            t = K.alias(big, f"{name}_{kc}_{s0}")
            K.dma('pool', t, big.ap[:, kc, s0:s1], K.dram_in, Wv[:, kc, s0:s1])
            pieces[(kc, s0 // seg)] = t
    return big.ap, pieces, seg


def rms_stats(K, C, xt, ps_stats, sq, lnv, rstd):
    for c in range(8):
        s = sq[c % 2]
        K.act(s, s.ap[:, :], xt, xt.ap[:, c, :], AF.Square)
        K.mm(ps_stats, ps_stats.ap[:, :], C['ones_bf'], C['ones_bf'].ap[:, :], s, s.ap[:, :],
             start=(c == 0), stop=(c == 7))
    K.act(lnv, lnv.ap[:, :], ps_stats, ps_stats.ap[:, :], AF.Ln, bias=C['eps'].ap[:, 0:1], scale=1.0 / D,
          reads=[C['eps']])
    K.act(rstd, rstd.ap[:, :], lnv, lnv.ap[:, :], AF.Exp, scale=-0.5)


def ffn_phase(K, C, XT, layer, Wgu_ap, Wd_ap, gvec, gcol):
    K.sb_set(C['sb_base'])
    wgu, pgu, seg = load_weight_cast(K, Wgu_ap, 8, 2 * FH, f"wgu{layer}")
    wd, pd, segd = load_weight_cast(K, Wd_ap, NJ, D, f"wd{layer}", seg=1024)
    xts = [K.alloc(f"f_xt{b}", [8, TT], F32) for b in range(2)]
    hT = K.alloc("f_hT", [8, TT], BF16)
    sq = [K.alloc(f"f_sq{b}", [TT], BF16) for b in range(2)]
    lnv = K.alloc("f_lnv", [TT], F32)
    rstd = K.alloc("f_rstd", [TT], F32)
    sg = [K.alloc("f_sg", [TT], F32)] * 2
    aT = K.alloc("f_aT", [NJ, TT], BF16)
    psS = K.psum[0]
    psG = [K.psum[1], K.psum[2]]
    psU = [K.psum[3], K.psum[4]]
    psO = [K.psum[5], K.psum[6]]

    def load(i):
        K.dma('sp', xts[i % 2], xts[i % 2].ap, XT[i], xt_tile_ap(XT[i].ap, i))

    load(0)
    for i in range(NT):
        xt = xts[i % 2]
        if i + 1 < NT:
            load(i + 1)
        rstd_pow(K, C, xt, psS, sq, rstd, lnv)
        for c in range(8):
            K.stt(hT, hT.ap[:, c, :], xt, xt.ap[:, c, :], gvec.ap[:, gcol + c:gcol + c + 1], rstd, rstd.ap[:, :],
                  ALU.mult, ALU.mult, reads=[gvec])
        for j in range(NJ):
            pg = psG[j % 2]
            pu = psU[j % 2]
            for kc in range(8):
                c0 = j * 128
                K.mm(pg, pg.ap[:, :], pgu[(kc, c0 // seg)], wgu[:, kc, c0:c0 + 128], hT, hT.ap[:, kc, :],
                     start=(kc == 0), stop=(kc == 7))
            for kc in range(8):
                c0 = FH + j * 128
                K.mm(pu, pu.ap[:, :], pgu[(kc, c0 // seg)], wgu[:, kc, c0:c0 + 128], hT, hT.ap[:, kc, :],
                     start=(kc == 0), stop=(kc == 7))
            s = sg[j % 2]
            K.act(s, s.ap[:, :], pg, pg.ap[:, :], AF.Silu)
            K.tt('dve', aT, aT.ap[:, j, :], s, s.ap[:, :], pu, pu.ap[:, :], ALU.mult)
        for m in range(8):
            po = psO[m % 2]
            for j in range(NJ):
                K.mm(po, po.ap[:, :], pd[(j, 0)], wd[:, j, m * 128:(m + 1) * 128], aT, aT.ap[:, j, :],
                     start=(j == 0), stop=(j == NJ - 1))
            K.tt('dve', xt, xt.ap[:, m, :], po, po.ap[:, :], xt, xt.ap[:, m, :], ALU.add)
        K.dma('sp', XT[i], xt_tile_ap(XT[i].ap, i), xt, xt.ap)


PATS = (1, 4, 16)


def att_qkv_phase(K, C, XT, li, Wqkv_ap, S, gcol, qcol, kcol):
    K.sb_set(C['sb_base'])
    phase_consts(K, C, ['blk64', 'ropeP'])
    w, pw, seg = load_weight_cast(K, Wqkv_ap, 8, 3 * D, f"wqkv{li}", seg=1024)
    xts = [K.alloc(f"a_xt{b}", [8, TT], F32) for b in range(2)]
    hTs = K.alloc("a_hTs", [8, 2048], BF16)
    sq = [K.alloc(f"a_sq{b}", [TT], BF16) for b in range(2)]
    lnv = K.alloc("a_lnv", [TT], F32)
    rstd = K.alloc("a_rstd", [TT], F32)
    cs = [(K.alloc(f"a_cos{b}", [TT], F32), K.alloc(f"a_sin{b}", [TT], F32)) for b in range(2)]
    sets = []
    for b in range(2):
        sets.append(dict(qs=K.alloc(f"a_qs{b}", [TT], F32), hsq=K.alloc(f"a_hsq{b}", [TT], BF16),
                         lnh=K.alloc(f"a_lnh{b}", [TT], F32), rsh=K.alloc(f"a_rsh{b}", [TT], F32),
                         qn=K.alloc(f"a_qn{b}", [TT], F32), t1=K.alloc(f"a_t1{b}", [TT], F32),
                         t2=K.alloc(f"a_t2{b}", [TT], F32), qo=K.alloc(f"a_qo{b}", [TT], BF16)))
    vbuf = K.alloc("a_vbuf", [16, D], BF16)
    psS = K.psum[0]
    psQ = [K.psum[1], K.psum[2]]
    psH = [K.psum[3], K.psum[7]]
    psP = K.psum[4]
    psV = [K.psum[5], K.psum[6]]
    vecs = C['vecs']

    def load(i):
        K.dma('sp', xts[i % 2], xts[i % 2].ap, XT[i], xt_tile_ap(XT[i].ap, i))
        co, si = cs[i % 2]
        K.dma('sp', co, co.ap, K.dram_in, C['cos_d'][:, i * TT:(i + 1) * TT])
        K.dma('sp', si, si.ap, K.dram_in, C['sin_d'][:, i * TT:(i + 1) * TT])

    load(0)
    vcnt = 0
    for i in range(NT):
        xt = xts[i % 2]
        co, si = cs[i % 2]
        if i + 1 < NT:
            load(i + 1)
        rms_stats(K, C, xt, psS, sq, lnv, rstd)
        lc = (i % 4) * TT
        for c in range(8):
            K.stt(hTs, hTs.ap[:, c, lc:lc + TT], xt, xt.ap[:, c, :], vecs.ap[:, gcol + c:gcol + c + 1], rstd,
                  rstd.ap[:, :], ALU.mult, ALU.mult, reads=[vecs])
        for c in range(16):
            st = sets[c % 2]
            ps = psQ[c % 2]
            ph = psH[c % 2]
            for kc in range(8):
                c0 = c * 128
                K.mm(ps, ps.ap[:, :], pw[(kc, c0 // seg)], w[:, kc, c0:c0 + 128], hTs, hTs.ap[:, kc, lc:lc + TT],
                     start=(kc == 0), stop=(kc == 7))
            K.copy('act', st['qs'], st['qs'].ap[:, :], ps, ps.ap[:, :])
            K.act(st['hsq'], st['hsq'].ap[:, :], ps, ps.ap[:, :], AF.Square)
            K.mm(ph, ph.ap[:, :], C['blk64'], C['blk64'].ap[:, :], st['hsq'], st['hsq'].ap[:, :], True, True)
            K.act(st['lnh'], st['lnh'].ap[:, :], ph, ph.ap[:, :], AF.Ln, bias=C['eps'].ap[:, 0:1], scale=1.0 / 64,
                  reads=[C['eps']])
            K.act(st['rsh'], st['rsh'].ap[:, :], st['lnh'], st['lnh'].ap[:, :], AF.Exp, scale=-0.5)
            gc = (qcol if c < 8 else kcol)
            K.stt(st['qn'], st['qn'].ap[:, :], st['qs'], st['qs'].ap[:, :], vecs.ap[:, gc:gc + 1], st['rsh'],
                  st['rsh'].ap[:, :], ALU.mult, ALU.mult, reads=[vecs])
            K.mm(psP, psP.ap[:, :], C['ropeP'], C['ropeP'].ap[:, :], st['qn'], st['qn'].ap[:, :], True, True)
            K.tt('pool', st['t1'], st['t1'].ap[:, :], st['qn'], st['qn'].ap[:, :], co, co.ap[:, :], ALU.mult)
            K.tt('dve', st['t2'], st['t2'].ap[:, :], psP, psP.ap[:, :], si, si.ap[:, :], ALU.mult)
            K.tt('dve', st['qo'], st['qo'].ap[:, :], st['t1'], st['t1'].ap[:, :], st['t2'], st['t2'].ap[:, :], ALU.add)
            dst = S['QT'][c][i] if c < 8 else S['KT'][c - 8][i]
            K.dma('sp', dst, dst.ap[:, i * TT:(i + 1) * TT], st['qo'], st['qo'].ap[:, :])
        if i % 4 == 3:
            s = i // 4
            for pi, d in enumerate(PATS):
                for lb in range(16):
                    nl = lb // d
                    r = lb % d
                    start = nl * 128 * d + r
                    for half in range(2):
                        pv = psV[vcnt % 2]
                        vcnt += 1
                        for kc in range(8):
                            c0 = 2 * D + half * 512
                            K.mm(pv, pv.ap[:, :], hTs, hTs.ap[:, kc, start:start + 127 * d + 1:d],
                                 pw[(kc, c0 // seg)], w[:, kc, c0:c0 + 512], start=(kc == 0), stop=(kc == 7))
                        eng = 'act' if half == 0 else 'dve'
                        K.copy(eng, vbuf, vbuf.ap[:, lb, half * 512:(half + 1) * 512], pv, pv.ap[:, :])
                for hp in range(8):
                    dst = S['VP'][pi][hp][s]
                    K.dma('sp', dst, dst.ap[:, 16 * s:16 * s + 16, :], vbuf, vbuf.ap[:, :, hp * 128:(hp + 1) * 128])


def att_core_phase(K, C, li, S):
    K.sb_set(C['sb_base'])
    phase_consts(K, C, ['mask4'])
    qk = [(K.alloc(f"c_qt{b}", [L], BF16), K.alloc(f"c_kt{b}", [L], BF16)) for b in range(2)]
    vps = [[K.alloc(f"c_vp{b}_{p}", [32, 128], BF16) for p in range(3)] for b in range(2)]
    accN = K.alloc("c_accN", [L], F32)
    accD = K.alloc("c_accD", [L], F32)
    pts = [[K.alloc(f"c_pt{h}_{k}", [TT], BF16) for k in range(3)] for h in range(2)]
    ot = K.alloc("c_ot", [L], BF16)
    rec = [K.alloc(f"c_rec{b}", [TT], F32) for b in range(2)]
    psSb = [[K.psum[0], K.psum[1]], [K.psum[2], K.psum[3]]]
    psOb = [K.psum[4], K.psum[5]]
    psDb = [K.psum[6], K.psum[7]]
    mask = C['mask4']
    ones64 = C['ones_bf']

    def load(hp):
        qt, kt = qk[hp % 2]
        K.dma('sp', qt, qt.ap, S['QT'][hp][0], S['QT'][hp][0].ap)
        for t in S['QT'][hp][1:]:
            qt.r
        K.dma('sp', kt, kt.ap, S['KT'][hp][0], S['KT'][hp][0].ap)
        for p in range(3):
            v = vps[hp % 2][p]
            K.dma('sp', v, v.ap, S['VP'][p][hp][0], S['VP'][p][hp][0].ap)

    def load_multi(dst, dst_ap, tiles, src_ap):
        K.op('sp', lambda e: e.dma_start(out=dst_ap, in_=src_ap), reads=list(tiles), writes=[dst], dma=True)

    def load2(hp):
        qt, kt = qk[hp % 2]
        load_multi(qt, qt.ap, S['QT'][hp], S['QT'][hp][0].ap)
        load_multi(kt, kt.ap, S['KT'][hp], S['KT'][hp][0].ap)
        for p in range(3):
            v = vps[hp % 2][p]
            load_multi(v, v.ap, S['VP'][p][hp], S['VP'][p][hp][0].ap)

    load2(0)
    octr = 0
    for hp in range(8):
        if hp + 1 < 8:
            load2(hp + 1)
        qt, kt = qk[hp % 2]
        groups = []
        for pi, d in enumerate(PATS):
            nb = 32 // d
            for r in range(d):
                for g in range((nb + 1) // 2):
                    units = [m for m in (2 * g, 2 * g + 1) if m < nb]
                    groups.append((pi, d, r, nb, units))

        def emit_S(gi):
            pi, d, r, nb, units = groups[gi]
            qv = qt.ap.rearrange("p (m d) -> p d m", d=d)
            kv = kt.ap.rearrange("p (m d) -> p d m", d=d)
            for h in range(2):
                ps = psSb[h][gi % 2]
                rows = slice(64 * h, 64 * h + 64)
                for ui, m in enumerate(units):
                    ncols = 256 if m < nb - 1 else 128
                    K.mm(ps, ps.ap[:, ui * 256:ui * 256 + ncols], kt, kv[rows, r, 128 * m:128 * m + 128],
                         qt, qv[rows, r, 128 * m:128 * m + ncols], True, True)

        def emit_E(gi):
            pi, d, r, nb, units = groups[gi]
            valid = 0
            for ui, m in enumerate(units):
                valid = ui * 256 + (256 if m < nb - 1 else 128)
            for h in range(2):
                ps = psSb[h][gi % 2]
                pt = pts[h][gi % 3]
                K.act(pt, pt.ap[:, 0:valid], ps, ps.ap[:, 0:valid], AF.Exp, scale=0.125)
                meng = 'pool' if (2 * gi + h) % 3 == 0 else 'dve'
                K.tt(meng, pt, pt.ap[:, 0:valid], pt, pt.ap[:, 0:valid], mask, mask.ap[:, 0:valid], ALU.mult)

        state = {'ob': 0, 'obank': 0}

        def emit_PV(gi):
            nonlocal octr
            pi, d, r, nb, units = groups[gi]
            vp = vps[hp % 2][pi]
            for ui, m in enumerate(units):
                blk = m * d + r
                pso = psOb[octr % 2]
                psd = psDb[octr % 2]
                oc = state['ob'] * 128
                for h in range(2):
                    rows = slice(64 * h, 64 * h + 64)
                    pt = pts[h][gi % 3]
                    cur = pt.ap[:, ui * 256:ui * 256 + 128]
                    if m > 0:
                        if ui == 1:
                            ppt = pt
                            prev = pt.ap[:, 128:256]
                        else:
                            ppt = pts[h][(gi - 1) % 3]
                            prev = ppt.ap[:, 256 + 128:512]
                    for (pp, lcur, lprev) in ((pso, vp.ap[:, blk, 64 * h:64 * h + 64],
                                               vp.ap[:, blk - d, 64 * h:64 * h + 64] if m > 0 else None),
                                              (psd, ones64.ap[:, 0:64], ones64.ap[:, 0:64])):
                        lt = vp if pp is pso else ones64
                        K.mm(pp, pp.ap[rows, oc:oc + 128], lt, lcur, pt, cur, True, m == 0)
                        if m > 0:
                            K.mm(pp, pp.ap[rows, oc:oc + 128], lt, lprev, ppt, prev, False, True)
                state['ob'] += 1
                last_of_class = (m == nb - 1)
                flush = state['ob'] == 4 or (last_of_class and d != 16) or (last_of_class and d == 16 and r % 2 == 1)
                if flush:
                    nblk = state['ob']
                    accNv = accN.ap.rearrange("p (m d) -> p d m", d=d)
                    accDv = accD.ap.rearrange("p (m d) -> p d m", d=d)
                    if d == 16:
                        av = lambda a: a[:, r - 1:r + 1, :]
                        pv_ = lambda p: p.ap.rearrange("p (a b) -> p a b", a=2)
                    else:
                        n0 = m - nblk + 1
                        av = lambda a: a[:, r, 128 * n0:128 * (m + 1)]
                        pv_ = lambda p: p.ap[:, 0:128 * nblk]
                    if d == 1:
                        K.copy('act', accN, av(accNv), pso, pv_(pso))
                        K.copy('act', accD, av(accDv), psd, pv_(psd))
                    else:
                        K.tt('dve', accN, av(accNv), pso, pv_(pso), accN, av(accNv), ALU.add)
                        K.tt('dve', accD, av(accDv), psd, pv_(psd), accD, av(accDv), ALU.add)
                    state['ob'] = 0
                    octr += 1

        ng = len(groups)
        emit_S(0)
        for gi in range(ng):
            if gi + 1 < ng:
                emit_S(gi + 1)
            emit_E(gi)
            emit_PV(gi)
        for i in range(NT):
            rc = rec[i % 2]
            cols = slice(i * TT, (i + 1) * TT)
            K.act(rc, rc.ap[:, :], accD, accD.ap[:, cols], AF.Ln)
            K.act(rc, rc.ap[:, :], rc, rc.ap[:, :], AF.Exp, scale=-1.0)
            K.tt('pool', ot, ot.ap[:, cols], accN, accN.ap[:, cols], rc, rc.ap[:, :], ALU.mult)
        K.dma('sp', S['OT'][hp], S['OT'][hp].ap, ot, ot.ap)


def att_out_phase(K, C, XT, li, Wo_ap, S):
    K.sb_set(C['sb_base'])
    w, pw, seg = load_weight_cast(K, Wo_ap, 8, D, f"wo{li}", seg=1024)
    xts = [K.alloc(f"o_xt{b}", [8, TT], F32) for b in range(2)]
    ots = [K.alloc(f"o_ot{b}", [8, TT], BF16) for b in range(2)]
    psO = [K.psum[0], K.psum[1], K.psum[2], K.psum[3]]

    def load(i):
        K.dma('sp', xts[i % 2], xts[i % 2].ap, XT[i], xt_tile_ap(XT[i].ap, i))
        K.op('sp', lambda e, i=i: e.dma_start(out=ots[i % 2].ap,
                                             in_=S['OT_d'].rearrange("c p n -> p c n")[:, :, i * TT:(i + 1) * TT]),
             reads=S['OT'], writes=[ots[i % 2]], dma=True)

    load(0)
    for i in range(NT):
        if i + 1 < NT:
            load(i + 1)
        xt = xts[i % 2]
        o = ots[i % 2]
        for m in range(8):
            po = psO[m % 4]
            for kc in range(8):
                K.mm(po, po.ap[:, :], pw[(kc, 0)], w[:, kc, m * 128:(m + 1) * 128], o, o.ap[:, kc, :],
                     start=(kc == 0), stop=(kc == 7))
            K.tt('dve', xt, xt.ap[:, m, :], po, po.ap[:, :], xt, xt.ap[:, m, :], ALU.add)
        K.dma('sp', XT[i], xt_tile_ap(XT[i].ap, i), xt, xt.ap)


def rstd_pow(K, C, xt, ps_stats, sq, rstd, tmp):
    for c in range(8):
        s = sq[c % 2]
        K.act(s, s.ap[:, :], xt, xt.ap[:, c, :], AF.Square)
        K.mm(ps_stats, ps_stats.ap[:, :], C['ones_bf'], C['ones_bf'].ap[:, :], s, s.ap[:, :],
             start=(c == 0), stop=(c == 7))
    K.act(tmp, tmp.ap[:, :], ps_stats, ps_stats.ap[:, :], AF.Ln, bias=C['eps'].ap[:, 0:1], scale=1.0 / D,
          reads=[C['eps']])
    K.act(rstd, rstd.ap[:, :], tmp, tmp.ap[:, :], AF.Exp, scale=-0.5)


def rec_a_phase(K, C, XT, li, Win_ap, Wr_ap, Wi_ap, S, vo, xsrc=None):
    K.sb_set(C['sb_base'])
    vecs = C['vecs']
    NW = 3584
    wbig = K.alloc(f"ra_w{li}", [8, NW], BF16)
    w = wbig.ap
    Wv = Win_ap.rearrange("(c p) n -> p c n", p=128)
    pw = {}
    for kc in range(8):
        for (d0, s0, n) in ((0, 0, 1024), (1024, 1024, 1024), (2048, 3072, 1536)):
            t = K.alias(wbig, f"ra_w{li}_{kc}_{d0}")
            K.dma('pool', t, w[:, kc, d0:d0 + n], K.dram_in, Wv[:, kc, s0:s0 + n])
            pw[(kc, d0)] = t

    def wpiece(kc, col):
        return pw[(kc, 0 if col < 1024 else (1024 if col < 2048 else 2048))]

    wr = K.alloc("ra_wr", [8, 128], BF16)
    wi = K.alloc("ra_wi", [8, 128], BF16)
    K.memset('pool', wr, wr.ap, 0.0)
    K.memset('pool', wi, wi.ap, 0.0)
    for (wt, src) in ((wr, Wr_ap), (wi, Wi_ap)):
        sv = src.rearrange("(c two) i j -> two i c j", two=2)
        for h in range(2):
            K.dma('pool', wt, wt.ap[64 * h:64 * h + 64, :, 64 * h:64 * h + 64], K.dram_in, sv[h])
    diag = K.alloc("ra_diag", [20 * 4, 128], BF16)
    for c in range(20):
        for j in range(4):
            col = (vo['lru_conv_w'] + j * 8 + c) if c < 8 else (vo['ssd_conv_w'] + j * 12 + (c - 8))
            K.ts('dve', diag, diag.ap[:, c * 4 + j, :], C['ident_bf'], C['ident_bf'].ap[:, :],
                 vecs.ap[:, col:col + 1], None, ALU.mult, reads=[vecs])
    dv = K.alloc("ra_dv", [32], F32)
    sp = [K.alloc(f"ra_sp{k}", [8], F32) for k in range(4)]
    K.ts('dve', dv, dv.ap[:, 0:8], vecs, vecs.ap[:, vo['lru_b_r']:vo['lru_b_r'] + 8], 0.5, None, ALU.mult)
    K.ts('dve', dv, dv.ap[:, 8:16], vecs, vecs.ap[:, vo['lru_b_i']:vo['lru_b_i'] + 8], 0.5, None, ALU.mult)
    lam = vecs.ap[:, vo['lru_lambda']:vo['lru_lambda'] + 8]
    K.act(sp[0], sp[0].ap[:, :], vecs, lam, AF.Exp, scale=-1.0)
    K.ts('dve', sp[1], sp[1].ap[:, :], sp[0], sp[0].ap[:, :], 1.0, None, ALU.add)
    K.ts('dve', sp[2], sp[2].ap[:, :], sp[1], sp[1].ap[:, :], -1.0, 1e-30, ALU.add, ALU.max)
    K.op('dve', lambda e: e.reciprocal(sp[2].ap[:, :], sp[2].ap[:, :]), reads=[sp[2]], writes=[sp[2]])
    K.tt('dve', sp[2], sp[2].ap[:, :], sp[2], sp[2].ap[:, :], sp[0], sp[0].ap[:, :], ALU.mult)
    K.act(sp[3], sp[3].ap[:, :], sp[1], sp[1].ap[:, :], AF.Ln)
    K.tt('dve', sp[3], sp[3].ap[:, :], sp[3], sp[3].ap[:, :], sp[2], sp[2].ap[:, :], ALU.mult)
    K.ts('dve', dv, dv.ap[:, 16:24], sp[3], sp[3].ap[:, :], -8.0, None, ALU.mult)
    K.ts('dve', dv, dv.ap[:, 24:32], sp[3], sp[3].ap[:, :], -4.0, None, ALU.mult)

    rawb = [K.alloc(f"ra_raw{c}", [516], BF16) for c in range(20)]
    for c in range(20):
        K.memset('pool', rawb[c], rawb[c].ap[:, 0:4], 0.0)
    hcar = K.alloc("ra_hcar", [8], F32)
    xt = K.alloc("ra_xt", [8, TT], F32)
    hTd = [K.alloc(f"ra_hT{b}", [8, TT], BF16) for b in range(2)]
    sq = [K.alloc(f"ra_sq{b}", [TT], BF16) for b in range(2)]
    rtmp = K.alloc("ra_rtmp", [TT], F32)
    rstd = K.alloc("ra_rstd", [TT], F32)
    gl = K.alloc("ra_gl", [8, TT], BF16)
    xo = [K.alloc(f"ra_xo{b}", [TT], BF16) for b in range(2)]
    names = ['xc', 'thr', 'thi', 'a', 'a2', 'om', 'sr', 't2', 'u', 'hl']
    sets = [{n: K.alloc(f"ra_{n}{b}", [TT], F32) for n in names} for b in range(2)]
    for b in range(2):
        sets[b]['xcb'] = K.alloc(f"ra_xcb{b}", [TT], BF16)
        sets[b]['oa'] = K.alloc(f"ra_oa{b}", [TT], BF16)
    psS = K.psum[0]
    psP = [K.psum[1], K.psum[2]]
    psC = [K.psum[3], K.psum[4]]
    psR = K.psum[5]
    psI = K.psum[6]

    def proj(ps, col0, ncol=128):
        hT = hTd[cur[0] % 2]
        for kc in range(8):
            K.mm(ps, ps.ap[0:ncol, :], wpiece(kc, col0), w[:, kc, col0:col0 + ncol], hT, hT.ap[:, kc, :],
                 start=(kc == 0), stop=(kc == 7))

    def conv(pc, c, ps, first):
        rb = rawb[c]
        if not first:
            K.copy('pool', rb, rb.ap[:, 0:4], rb, rb.ap[:, 512:516])
        K.copy('dve', rb, rb.ap[:, 4:516], ps, ps.ap[:, :])
        for j in range(4):
            K.mm(pc, pc.ap[:, :], diag, diag.ap[:, c * 4 + j, :], rb, rb.ap[:, 1 + j:1 + j + 512],
                 start=(j == 0), stop=(j == 3))

    cur = [0]
    for i in range(NT):
        cur[0] = i
        hT = hTd[i % 2]
        if xsrc is not None:
            K.dma('sp', xt, xt.ap, K.dram_in, xt_tile_ap(xsrc, i))
        else:
            K.dma('sp', xt, xt.ap, XT[i], xt_tile_ap(XT[i].ap, i))
        rstd_pow(K, C, xt, psS, sq, rstd, rtmp)
        for c in range(8):
            K.stt(hT, hT.ap[:, c, :], xt, xt.ap[:, c, :], vecs.ap[:, vo['rec_norm'] + c:vo['rec_norm'] + c + 1],
                  rstd, rstd.ap[:, :], ALU.mult, ALU.mult, reads=[vecs])
        K.dma('sp', S['HT'][i], S['HT_d'].rearrange("c p n -> p c n")[:, :, i * TT:(i + 1) * TT], hT, hT.ap)
        for c in range(8):
            ps = psP[c % 2]
            proj(ps, 1024 + c * 128)
            K.act(gl, gl.ap[:, c, :], ps, ps.ap[:, :], AF.Gelu_apprx_tanh)
        for cc in range(12):
            ps = psP[cc % 2]
            pc = psC[cc % 2]
            proj(ps, 2048 + cc * 128)
            conv(pc, 8 + cc, ps, i == 0)
            o = xo[cc % 2]
            bcol = vo['ssd_conv_b'] + cc
            K.act(o, o.ap[:, :], pc, pc.ap[:, :], AF.Silu, bias=vecs.ap[:, bcol:bcol + 1], reads=[vecs])
            K.dma('sp', S['XBC'][cc][i], S['XBC'][cc][i].ap[:, i * TT:(i + 1) * TT], o, o.ap[:, :])
        for c in range(8):
            st = sets[c % 2]
            ps = psP[c % 2]
            pc = psC[c % 2]
            proj(ps, c * 128)
            conv(pc, c, ps, i == 0)
            bcol = vo['lru_conv_b'] + c
            K.ts('dve', st['xc'], st['xc'].ap[:, :], pc, pc.ap[:, :], vecs.ap[:, bcol:bcol + 1], None, ALU.add,
                 reads=[vecs])
            K.copy('dve', st['xcb'], st['xcb'].ap[:, :], st['xc'], st['xc'].ap[:, :])
            K.mm(psR, psR.ap[:, :], wr, wr.ap[:, c, :], st['xcb'], st['xcb'].ap[:, :], True, True)
            K.mm(psI, psI.ap[:, :], wi, wi.ap[:, c, :], st['xcb'], st['xcb'].ap[:, :], True, True)
            K.act(st['thr'], st['thr'].ap[:, :], psR, psR.ap[:, :], AF.Tanh, bias=dv.ap[:, c:c + 1], scale=0.5,
                  reads=[dv])
            K.act(st['thi'], st['thi'].ap[:, :], psI, psI.ap[:, :], AF.Tanh, bias=dv.ap[:, 8 + c:9 + c], scale=0.5,
                  reads=[dv])
            K.act(st['a'], st['a'].ap[:, :], st['thr'], st['thr'].ap[:, :], AF.Exp, bias=dv.ap[:, 24 + c:25 + c],
                  scale=dv.ap[:, 24 + c:25 + c], reads=[dv])
            K.act(st['a2'], st['a2'].ap[:, :], st['thr'], st['thr'].ap[:, :], AF.Exp, bias=dv.ap[:, 16 + c:17 + c],
                  scale=dv.ap[:, 16 + c:17 + c], reads=[dv])
            K.ts('dve', st['om'], st['om'].ap[:, :], st['a2'], st['a2'].ap[:, :], -1.0, 1.0, ALU.mult, ALU.add)
            K.act(st['sr'], st['sr'].ap[:, :], st['om'], st['om'].ap[:, :], AF.Ln)
            K.act(st['sr'], st['sr'].ap[:, :], st['sr'], st['sr'].ap[:, :], AF.Exp, scale=0.5)
            K.stt(st['t2'], st['t2'].ap[:, :], st['thi'], st['thi'].ap[:, :], 1.0, st['xc'], st['xc'].ap[:, :],
                  ALU.add, ALU.mult)
            K.stt(st['u'], st['u'].ap[:, :], st['t2'], st['t2'].ap[:, :], 0.5, st['sr'], st['sr'].ap[:, :],
                  ALU.mult, ALU.mult)
            init = 0.0 if i == 0 else hcar.ap[:, c:c + 1]
            K.op('dve', lambda e, st=st, init=init: e.tensor_tensor_scan(st['hl'].ap[:, :], st['a'].ap[:, :],
                                                                       st['u'].ap[:, :], init, ALU.mult, ALU.add),
                 reads=[st['a'], st['u'], hcar], writes=[st['hl']])
            K.copy('pool', hcar, hcar.ap[:, c:c + 1], st['hl'], st['hl'].ap[:, 511:512])
            K.tt('dve', st['oa'], st['oa'].ap[:, :], st['hl'], st['hl'].ap[:, :], gl, gl.ap[:, c, :], ALU.mult)
            K.dma('sp', S['MA'][c][i], S['MA'][c][i].ap[:, i * TT:(i + 1) * TT], st['oa'], st['oa'].ap[:, :])


def rec_b_phase(K, C, XT, li, Win_ap, Wout_ap, S, vo, bv_d, bo, xsrc=None):
    K.sb_set(C['sb_base'])
    phase_consts(K, C, ['U_f32', 'ones_f32', 'neg_bf'])
    wo, pwo, sego = load_weight_cast(K, Wout_ap, 16, D, f"rb_wo{li}", seg=1024)
    wzbig = K.alloc(f"rb_wz{li}", [8, 1040], BF16)
    wz = wzbig.ap
    Wv = Win_ap.rearrange("(c p) n -> p c n", p=128)
    pwz = {}
    for kc in range(8):
        t = K.alias(wzbig, f"rb_wz{li}_{kc}")
        K.dma('pool', t, wz[:, kc, 0:1024], K.dram_in, Wv[:, kc, 2048:3072])
        K.dma('pool', t, wz[:, kc, 1024:1040], K.dram_in, Wv[:, kc, 4608:4624])
        pwz[kc] = t
    bv = K.alloc("rb_bv", [1072], F32)
    K.dma('sp', bv, bv.ap, K.dram_in, bv_d[:, bo:bo + 1072])
    Aneg = K.alloc("rb_A", [16], F32)
    K.act(Aneg, Aneg.ap[:, :], bv, bv.ap[:, 16:32], AF.Exp)
    K.ts('dve', Aneg, Aneg.ap[:, :], Aneg, Aneg.ap[:, :], -1.0, None, ALU.mult)
    DI = K.alloc("rb_DI", [16, 128], BF16)
    for h in range(16):
        K.ts('dve', DI, DI.ap[:, h, :], C['ident_bf'], C['ident_bf'].ap[:, :], bv.ap[:, 32 + h:33 + h], None,
             ALU.mult, reads=[bv])
    xts = [K.alloc("rb_xt", [8, TT], F32)] * 2
    hTs = [K.alloc(f"rb_hT{b}", [8, TT], BF16) for b in range(2)]
    xbcs = [K.alloc(f"rb_xbc{b}", [12, TT], BF16) for b in range(2)]
    mAs = [K.alloc("rb_mA", [8, TT], BF16)] * 2
    mB = K.alloc("rb_mB", [8, TT], BF16)
    ytok = K.alloc("rb_ytok", [4, D], F32)
    Sst = K.alloc("rb_S", [D], F32)
    prevb = K.alloc("rb_prev", [D], BF16)
    rhsR = K.alloc("rb_rhsR", [16 * 128], F32)
    Eexp = K.alloc("rb_E", [16 * 128], F32)
    MTs = [K.alloc(f"rb_MT{b}", [16, 128], BF16) for b in range(2)]
    xsbs = [K.alloc(f"rb_xsb{b}", [D], BF16) for b in range(2)]
    xsws = [K.alloc(f"rb_xsw{b}", [D], BF16) for b in range(2)]
    Btoks = [K.alloc(f"rb_Btok{b}", [256], BF16) for b in range(2)]
    tmpF = K.alloc("rb_tmpF", [512], F32)
    bcw = K.alloc("rb_bcw", [D], F32)
    bce = K.alloc("rb_bce", [D], F32)
    bcc = K.alloc("rb_bcc", [D], F32)
    smalls = [{n: K.alloc(f"rb_{n}{b}", [16], F32) for n in ['v', 'e', 'dt', 'lndt', 'adt', 'cs', 'bE', 'wx', 'w',
                                                              'cd', 'ecs']} for b in range(2)]
    sz = K.alloc("rb_sz", [512], F32)
    ysq = K.alloc("rb_ysq", [512], BF16)
    ss = K.alloc("rb_ss", [8], F32)
    ss2 = K.alloc("rb_ss2", [8], F32)
    yb = K.alloc("rb_yb", [512], BF16)
    ps_small = K.psum[0]
    ps_R = K.psum[1]
    ps_X = K.psum[2]
    ps_G = K.psum[3]
    ps_Y = K.psum[4]
    ps_F = K.psum[5]
    ps_St = K.psum[6]
    ps_Z = K.psum[7]
    identb = C['ident_bf']

    def load(i):
        b = i % 2
        K.dma('sp', hTs[b], hTs[b].ap, S['HT'][i], S['HT_d'].rearrange("c p n -> p c n")[:, :, i * TT:(i + 1) * TT])
        K.op('sp', lambda e, b=b, i=i: e.dma_start(
            out=xbcs[b].ap, in_=S['XBC_d'].rearrange("c p n -> p c n")[:, :, i * TT:(i + 1) * TT]),
            reads=[S['XBC'][cc][i] for cc in range(12)], writes=[xbcs[b]], dma=True)

    def load_single(i):
        b = i % 2
        if xsrc is not None:
            K.dma('sp', xts[b], xts[b].ap, K.dram_in, xt_tile_ap(xsrc, i))
        else:
            K.dma('sp', xts[b], xts[b].ap, XT[i], xt_tile_ap(XT[i].ap, i))
        K.op('sp', lambda e, b=b, i=i: e.dma_start(
            out=mAs[b].ap, in_=S['MA_d'].rearrange("c p n -> p c n")[:, :, i * TT:(i + 1) * TT]),
            reads=[S['MA'][c][i] for c in range(8)], writes=[mAs[b]], dma=True)

    load(0)
    for i in range(NT):
        b = i % 2
        xt, hT, xbc, mA = xts[b], hTs[b], xbcs[b], mAs[b]
        load_single(i)
        if i + 1 < NT:
            load(i + 1)
        RB = float(os.environ.get('RB_STOP', '99'))
        for q in range(4):
            cg = 4 * i + q
            tc = slice(128 * q, 128 * q + 128)
            sm = smalls[cg % 2]
            MT, xsb, xsw, Btok = MTs[cg % 2], xsbs[cg % 2], xsws[cg % 2], Btoks[cg % 2]
            if RB < 2:
                continue
            for kc in range(8):
                K.mm(ps_small, ps_small.ap[:, 0:16], hT, hT.ap[:, kc, tc], pwz[kc], wz[:, kc, 1024:1040],
                     start=(kc == 0), stop=(kc == 7))
            K.tt('dve', sm['v'], sm['v'].ap[:, :], ps_small, ps_small.ap[:, 0:16], bv, bv.ap[:, 0:16], ALU.add)
            K.act(sm['e'], sm['e'].ap[:, :], sm['v'], sm['v'].ap[:, :], AF.Exp)
            K.act(sm['dt'], sm['dt'].ap[:, :], sm['e'], sm['e'].ap[:, :], AF.Ln, bias=C['one'].ap[:, 0:1],
                  reads=[C['one']])
            K.act(sm['lndt'], sm['lndt'].ap[:, :], sm['dt'], sm['dt'].ap[:, :], AF.Ln)
            K.tt('dve', sm['adt'], sm['adt'].ap[:, :], sm['dt'], sm['dt'].ap[:, :], Aneg, Aneg.ap[:, :], ALU.mult)
            K.mm(ps_small, ps_small.ap[:, 16:32], C['U_f32'], C['U_f32'].ap[:, :], sm['adt'], sm['adt'].ap[:, :],
                 True, True)
            K.mm(ps_small, ps_small.ap[:, 32:48], C['ones_f32'], C['ones_f32'].ap[:, :], sm['adt'],
                 sm['adt'].ap[:, :], True, True)
            K.copy('dve', sm['cs'], sm['cs'].ap[:, :], ps_small, ps_small.ap[:, 16:32])
            K.tt('dve', sm['bE'], sm['bE'].ap[:, :], sm['lndt'], sm['lndt'].ap[:, :], sm['cs'], sm['cs'].ap[:, :],
                 ALU.subtract)
            K.tt('dve', sm['wx'], sm['wx'].ap[:, :], ps_small, ps_small.ap[:, 32:48], sm['bE'], sm['bE'].ap[:, :],
                 ALU.add)
            K.act(sm['w'], sm['w'].ap[:, :], sm['wx'], sm['wx'].ap[:, :], AF.Exp)
            K.act(sm['cd'], sm['cd'].ap[:, :], ps_small, ps_small.ap[:, 32:48], AF.Exp)
            K.act(sm['ecs'], sm['ecs'].ap[:, :], sm['cs'], sm['cs'].ap[:, :], AF.Exp)
            if RB < 3:
                continue
            if cg <= 1:
                K.dbg(f"dt{cg}", sm['dt'], sm['dt'].ap[:, :])
                K.dbg(f"cs{cg}", sm['cs'], sm['cs'].ap[:, :])
                K.dbg(f"w{cg}", sm['w'], sm['w'].ap[:, :])
                K.dbg(f"cd{cg}", sm['cd'], sm['cd'].ap[:, :])
            K.tt('pool', rhsR, rhsR.ap.rearrange("p (h l) -> p h l", h=16),
                 C['U_f32'], C['U_f32'].ap.unsqueeze(1).to_broadcast([128, 16, 128]),
                 sm['adt'], sm['adt'].ap.unsqueeze(2).to_broadcast([128, 16, 128]), ALU.mult)
            for bb in range(4):
                K.mm(ps_R, ps_R.ap[:, :], C['ones_f32'], C['ones_f32'].ap[:, :], rhsR,
                     rhsR.ap[:, 512 * bb:512 * bb + 512], True, False)
                K.mm(ps_R, ps_R.ap.rearrange("p (h l) -> p h l", h=4), identb, identb.ap[:, :], C['neg_bf'],
                     C['neg_bf'].ap.unsqueeze(1).to_broadcast([128, 4, 128]), False, True)
                for hh in range(4):
                    h = 4 * bb + hh
                    K.act(Eexp, Eexp.ap[:, 128 * h:128 * h + 128], ps_R, ps_R.ap[:, 128 * hh:128 * hh + 128], AF.Exp,
                          bias=sm['bE'].ap[:, h:h + 1], reads=[sm['bE']])
            if RB < 4:
                continue
            pxb = ps_X.ap.bitcast(BF16)
            for c in range(8):
                K.transpose(ps_X, pxb[:, 128 * c:128 * c + 128], xbc, xbc.ap[:, c, tc], identb, identb.ap[:, :])
            K.copy('act', xsb, xsb.ap[:, :], ps_X, pxb[:, :])
            K.copy('pool', bcw, bcw.ap.rearrange("p (h e) -> p h e", h=16), sm['w'],
                   sm['w'].ap.unsqueeze(2).to_broadcast([128, 16, 64]))
            if RB < 4.47:
                continue
            K.tt('dve', xsw, xsw.ap[:, :], xsb, xsb.ap[:, :], bcw, bcw.ap[:, :], ALU.mult)
            if RB < 4.6:
                continue
            pgb = ps_G.ap.bitcast(BF16)
            for g in range(2):
                K.transpose(ps_G, pgb[:, 512 + 128 * g:512 + 128 * g + 128], xbc, xbc.ap[:, 8 + g, tc], identb,
                            identb.ap[:, :])
            K.copy('act', Btok, Btok.ap[:, :], ps_G, pgb[:, 512:768])
            if RB < 5:
                continue
            for g in range(2):
                K.mm(ps_G, ps_G.ap[:, 128 * g:128 * g + 128], xbc, xbc.ap[:, 8 + g, tc], xbc, xbc.ap[:, 10 + g, tc],
                     True, True)
            for g in range(2):
                K.tt('dve', MT, MT.ap[:, 8 * g:8 * g + 8, :], Eexp,
                     Eexp.ap[:, 1024 * g:1024 * g + 1024].rearrange("p (h l) -> p h l", h=8), ps_G,
                     ps_G.ap[:, 128 * g:128 * g + 128].unsqueeze(1).to_broadcast([128, 8, 128]), ALU.mult)
            if RB < 6:
                continue
            if cg <= 1:
                K.dbg(f"E{cg}", Eexp, Eexp.ap[:, :])
                K.dbg(f"xsb{cg}", xsb, xsb.ap[:, :], BF16)
                K.dbg(f"xsw{cg}", xsw, xsw.ap[:, :], BF16)
                K.dbg(f"Btok{cg}", Btok, Btok.ap[:, :], BF16)
                K.dbg(f"MT{cg}", MT, MT.ap.rearrange("p h l -> p (h l)"), BF16)
            for g in range(2):
                for hh in range(8):
                    h = 8 * g + hh
                    K.mm(ps_Y, ps_Y.ap[:, 64 * hh:64 * hh + 64], MT, MT.ap[:, h, :], xsb, xsb.ap[:, 64 * h:64 * h + 64],
                         True, False)
                    K.mm(ps_Y, ps_Y.ap[:, 64 * hh:64 * hh + 64], DI, DI.ap[:, h, :], xsb, xsb.ap[:, 64 * h:64 * h + 64],
                         False, True)
                yslot = ytok.ap[:, q, 512 * g:512 * g + 512]
                if cg > 0:
                    K.mm(ps_F, ps_F.ap[:, :], xbc, xbc.ap[:, 10 + g, tc], prevb, prevb.ap[:, 512 * g:512 * g + 512],
                         True, True)
                    if g == 0:
                        K.copy('pool', bce, bce.ap.rearrange("p (h e) -> p h e", h=16), sm['ecs'],
                               sm['ecs'].ap.unsqueeze(2).to_broadcast([128, 16, 64]))
                    K.tt('dve', tmpF, tmpF.ap[:, :], ps_F, ps_F.ap[:, :], bce, bce.ap[:, 512 * g:512 * g + 512],
                         ALU.mult)
                    K.tt('dve', ytok, yslot, ps_Y, ps_Y.ap[:, :], tmpF, tmpF.ap[:, :], ALU.add)
                else:
                    K.copy('dve', ytok, yslot, ps_Y, ps_Y.ap[:, :])
            if RB < 7:
                continue
            if cg < 31:
                for g in range(2):
                    K.mm(ps_St, ps_St.ap[:, :], Btok, Btok.ap[:, 128 * g:128 * g + 128], xsw,
                         xsw.ap[:, 512 * g:512 * g + 512], True, True)
                    sv = Sst.ap[:, 512 * g:512 * g + 512]
                    if cg > 0:
                        if g == 0:
                            K.copy('pool', bcc, bcc.ap.rearrange("p (h e) -> p h e", h=16), sm['cd'],
                                   sm['cd'].ap.unsqueeze(2).to_broadcast([128, 16, 64]))
                        K.tt('pool', Sst, sv, Sst, sv, bcc, bcc.ap[:, 512 * g:512 * g + 512], ALU.mult)
                        K.tt('dve', Sst, sv, ps_St, ps_St.ap[:, :], Sst, sv, ALU.add)
                    else:
                        K.copy('dve', Sst, sv, ps_St, ps_St.ap[:, :])
                K.copy('pool', prevb, prevb.ap[:, :], Sst, Sst.ap[:, :])
                if cg <= 1:
                    K.dbg(f"S{cg}", Sst, Sst.ap[:, :])
        if RB >= 8:
            for q in range(4):
                tc = slice(128 * q, 128 * q + 128)
                for g in range(2):
                    for kc in range(8):
                        K.mm(ps_Z, ps_Z.ap[:, :], hT, hT.ap[:, kc, tc], pwz[kc], wz[:, kc, 512 * g:512 * g + 512],
                             start=(kc == 0), stop=(kc == 7))
                    K.act(sz, sz.ap[:, :], ps_Z, ps_Z.ap[:, :], AF.Silu)
                    ysl = ytok.ap[:, q, 512 * g:512 * g + 512]
                    K.tt('dve', ytok, ysl, ytok, ysl, sz, sz.ap[:, :], ALU.mult)
                    K.act(ysq, ysq.ap[:, :], ytok, ysl, AF.Square, accum=ss.ap[:, 2 * q + g:2 * q + g + 1],
                          extra_w=[ss])
            K.act(ss2, ss2.ap[:, :], ss, ss.ap[:, :], AF.Ln, bias=C['eps'].ap[:, 0:1], scale=1.0 / 512,
                  reads=[C['eps']])
            K.act(ss2, ss2.ap[:, :], ss2, ss2.ap[:, :], AF.Exp, scale=-0.5)
            for q in range(4):
                tc = slice(128 * q, 128 * q + 128)
                for g in range(2):
                    ysl = ytok.ap[:, q, 512 * g:512 * g + 512]
                    K.stt(yb, yb.ap[:, :], ytok, ysl, ss2.ap[:, 2 * q + g:2 * q + g + 1], bv,
                          bv.ap[:, 48 + 512 * g:48 + 512 * g + 512], ALU.mult, ALU.mult, reads=[ss2])
                    ptr = ps_X.ap.bitcast(BF16)
                    for k4 in range(4):
                        K.transpose(ps_X, ptr[:, 128 * k4:128 * k4 + 128], yb, yb.ap[:, 128 * k4:128 * k4 + 128],
                                    identb, identb.ap[:, :])
                    K.copy('act', mB, mB.ap[:, 4 * g:4 * g + 4, tc], ps_X,
                           ptr[:, 0:512].rearrange("p (k t) -> p k t", k=4))
        if i == 0:
            K.dbg("ytok", ytok, ytok.ap.rearrange("p q d -> p (q d)"))
            K.dbg("mB", mB, mB.ap.rearrange("p c t -> p (c t)"), BF16)
        for m in range(8):
            po = [ps_Y, ps_F, ps_St, ps_Z][m % 4]
            for k in range(16):
                src, sap = (mA, mA.ap[:, k, :]) if k < 8 else (mB, mB.ap[:, k - 8, :])
                K.mm(po, po.ap[:, :], pwo[(k, 0)], wo[:, k, m * 128:(m + 1) * 128], src, sap,
                     start=(k == 0), stop=(k == 15))
            K.tt('dve', xt, xt.ap[:, m, :], po, po.ap[:, :], xt, xt.ap[:, m, :], ALU.add)
        K.dma('sp', XT[i], xt_tile_ap(XT[i].ap, i), xt, xt.ap)


WEIGHT_SHAPES = {
    'rec_w_in': [2, D, 4624], 'rec_w_out': [2, 2048, D],
    'lru_w_r': [2, 16, 64, 64], 'lru_w_i': [2, 16, 64, 64],
    'att_w_qkv': [2, D, 3 * D], 'att_w_out': [2, D, D],
    'ffn_w_gate_up': [4, D, 2 * FH], 'ffn_w_down': [4, FH, D],
}


def rope_tables():
    half = 8
    inv = (np.float32(500000.0) ** (-2.0 * np.arange(half, dtype=np.float32) / np.float32(16))).astype(np.float32)
    pos = np.arange(L, dtype=np.float32)
    ang = (pos[:, None] * inv[None, :]).astype(np.float32)
    cos = np.cos(ang).astype(np.float32).T
    sin = np.sin(ang).astype(np.float32).T
    COS = np.ones((128, L), np.float32)
    SIN = np.zeros((128, L), np.float32)
    for h in range(2):
        COS[64 * h:64 * h + 8] = cos
        COS[64 * h + 8:64 * h + 16] = cos
        SIN[64 * h:64 * h + 8] = sin
        SIN[64 * h + 8:64 * h + 16] = sin
    return COS, SIN


def build_consts_host():
    c = {}
    c['ones_bf'] = np.ones((128, 128), dtype=ml_dtypes.bfloat16)
    c['eps'] = np.full((128, 1), EPS, dtype=np.float32)
    blk = np.zeros((128, 128), np.float32)
    blk[:64, :64] = 1
    blk[64:, 64:] = 1
    c['blk64'] = blk.astype(ml_dtypes.bfloat16)
    P = np.zeros((128, 128), np.float32)
    for h in range(2):
        for e in range(8):
            P[64 * h + e + 8, 64 * h + e] = -1.0
            P[64 * h + e, 64 * h + e + 8] = 1.0
    c['ropeP'] = P
    cos, sin = rope_tables()
    c['cos_d'] = cos
    c['sin_d'] = sin
    k = np.arange(128)[:, None]
    q = np.arange(128)[None, :]
    m = np.concatenate([(k <= q), (k >= q)], axis=1).astype(np.float32)
    c['mask4'] = np.concatenate([m, m], axis=1).astype(ml_dtypes.bfloat16)
    c['ident_bf'] = np.eye(128, dtype=np.float32).astype(ml_dtypes.bfloat16)
    c['U_f32'] = (k <= q).astype(np.float32)
    c['ones_f32'] = np.ones((128, 128), np.float32)
    c['neg_bf'] = np.where(q < k, -32768.0, 0.0).astype(np.float32).astype(ml_dtypes.bfloat16)
    c['half'] = np.full((128, 1), 0.5, np.float32)
    c['neghalf'] = np.full((128, 1), -0.5, np.float32)
    c['one'] = np.ones((128, 1), np.float32)
    return c


CONST_SPECS = [('ones_bf', [128, 128], BF16), ('eps', [128, 1], F32), ('ident_bf', [128, 128], BF16),
               ('half', [128, 1], F32), ('neghalf', [128, 1], F32), ('one', [128, 1], F32)]
CONST_PHASE = [('blk64', [128, 128], BF16), ('ropeP', [128, 128], F32), ('mask4', [128, 512], BF16),
               ('U_f32', [128, 128], F32), ('ones_f32', [128, 128], F32), ('neg_bf', [128, 128], BF16)]
CONST_DRAM_ONLY = [('cos_d', [128, L], F32), ('sin_d', [128, L], F32)]


def build_program(phases, nvec):
    nc = bass.Bass("TRN2", target_bir_lowering=False)
    K = KB(nc)
    K.dram_in = Tile(None, "dram_in", ro=True)
    xin = nc.dram_tensor("xT", [D, L], F32, kind="ExternalInput").ap()
    yout = nc.dram_tensor("yT", [D, L], F32, kind="ExternalOutput").ap()
    vecs_d = nc.dram_tensor("vecs", [128, nvec], F32, kind="ExternalInput").ap()
    bv_d = nc.dram_tensor("bvecs", [128, 2 * 1072], F32, kind="ExternalInput").ap()
    W = {}
    for name, shp in WEIGHT_SHAPES.items():
        W[name] = nc.dram_tensor(name, shp, F32, kind="ExternalInput").ap()

    C = {}
    for name, shp, dt in CONST_SPECS:
        d_ap = nc.dram_tensor(name, shp, dt, kind="ExternalInput").ap()
        C[name] = K.alloc(name, shp[1:], dt)
        K.dma('sp', C[name], C[name].ap, K.dram_in, d_ap)
    for name, shp, dt in CONST_DRAM_ONLY:
        C[name] = nc.dram_tensor(name, shp, dt, kind="ExternalInput").ap()
    C['_phase_d'] = {name: (nc.dram_tensor(name, shp, dt, kind="ExternalInput").ap(), shp, dt)
                     for name, shp, dt in CONST_PHASE}
    C['vecs'] = K.alloc("vecs", [nvec], F32)
    K.dma('sp', C['vecs'], C['vecs'].ap, K.dram_in, vecs_d)
    C['sb_base'] = K.sb_ptr

    S = {}
    qt_d = nc.dram_tensor("QT_s", [8, 128, L], BF16, kind="Internal").ap()
    kt_d = nc.dram_tensor("KT_s", [8, 128, L], BF16, kind="Internal").ap()
    vp_d = nc.dram_tensor("VP_s", [3, 8, 128, 32, 128], BF16, kind="Internal").ap()
    ot_d = nc.dram_tensor("OT_s", [8, 128, L], BF16, kind="Internal").ap()
    S['QT'] = [[Tile(qt_d[c], f"QT{c}_{i}") for i in range(NT)] for c in range(8)]
    S['KT'] = [[Tile(kt_d[c], f"KT{c}_{i}") for i in range(NT)] for c in range(8)]
    S['VP'] = [[[Tile(vp_d[p, hp], f"VP{p}_{hp}_{s}") for s in range(2)] for hp in range(8)] for p in range(3)]
    S['OT'] = [Tile(ot_d[hp], f"OT{hp}") for hp in range(8)]
    S['OT_d'] = ot_d
    ks = "ExternalOutput" if os.environ.get('DBG_SCR') else "Internal"
    S['HT_d'] = nc.dram_tensor("HT_s", [8, 128, L], BF16, kind=ks).ap()
    S['XBC_d'] = nc.dram_tensor("XBC_s", [12, 128, L], BF16, kind=ks).ap()
    S['MA_d'] = nc.dram_tensor("MA_s", [8, 128, L], BF16, kind=ks).ap()
    S['HT'] = [Tile(S['HT_d'], f"HT{i}") for i in range(NT)]
    S['XBC'] = [[Tile(S['XBC_d'][c], f"XBC{c}_{i}") for i in range(NT)] for c in range(12)]
    S['MA'] = [[Tile(S['MA_d'][c], f"MA{c}_{i}") for i in range(NT)] for c in range(8)]

    XT = [Tile(yout, f"XT{i}") for i in range(NT)]
    C['xin'] = xin
    if phases[0][0] != 'rec':
        for i in range(NT):
            K.dma('sp', XT[i], yout[:, i * TT:(i + 1) * TT], K.dram_in, xin[:, i * TT:(i + 1) * TT])

    for ph in phases:
        if ph[0] == 'ffn':
            layer = ph[1]
            ffn_phase(K, C, XT, layer, W['ffn_w_gate_up'][layer], W['ffn_w_down'][layer], C['vecs'],
                      VEC_OFF['ffn_norm'] + 8 * layer)
        elif ph[0] == 'att':
            li = ph[1]
            att_qkv_phase(K, C, XT, li, W['att_w_qkv'][li], S, VEC_OFF['att_norm'] + 8 * li,
                          VEC_OFF['att_q_norm'] + li, VEC_OFF['att_k_norm'] + li)
            att_core_phase(K, C, li, S)
            att_out_phase(K, C, XT, li, W['att_w_out'][li], S)
        elif ph[0] in ('rec', 'rec_a', 'rec_b'):
            li = ph[1]
            vo = rec_vo(li)
            xsrc = xin if (ph is phases[0] and ph[0] == 'rec') else None
            if ph[0] != 'rec_b':
                rec_a_phase(K, C, XT, li, W['rec_w_in'][li], W['lru_w_r'][li], W['lru_w_i'][li], S, vo, xsrc)
            if ph[0] != 'rec_a':
                rec_b_phase(K, C, XT, li, W['rec_w_in'][li], W['rec_w_out'][li], S, vo, bv_d, 1072 * li, xsrc)
    K.op('sp', None, reads=XT + K.dbg_tiles, writes=[])
    K.emit()
    return nc, K


VEC_OFF = {'ffn_norm': 0, 'att_norm': 32, 'att_q_norm': 48, 'att_k_norm': 50}
REC_BASE = 64
REC_STRIDE = 160
REC_FIELDS = {'rec_norm': 0, 'lru_conv_w': 8, 'lru_conv_b': 40, 'lru_b_r': 48, 'lru_b_i': 56, 'lru_lambda': 64,
              'ssd_conv_w': 72, 'ssd_conv_b': 120}
NVEC = REC_BASE + 2 * REC_STRIDE


def rec_vo(li):
    return {k: REC_BASE + REC_STRIDE * li + v for k, v in REC_FIELDS.items()}


def pack_vecs(inputs):
    v = np.zeros((128, NVEC), dtype=np.float32)
    f = lambda k: np.asarray(inputs[k], dtype=np.float32)
    fn = f('ffn_norm')
    for l in range(4):
        v[:, VEC_OFF['ffn_norm'] + 8 * l: VEC_OFF['ffn_norm'] + 8 * l + 8] = fn[l].reshape(8, 128).T
    an = f('att_norm')
    for l in range(2):
        v[:, VEC_OFF['att_norm'] + 8 * l: VEC_OFF['att_norm'] + 8 * l + 8] = an[l].reshape(8, 128).T
        v[:, VEC_OFF['att_q_norm'] + l] = np.tile(f('att_q_norm')[l], 2)
        v[:, VEC_OFF['att_k_norm'] + l] = np.tile(f('att_k_norm')[l], 2)
    for l in range(2):
        vo = rec_vo(l)
        v[:, vo['rec_norm']:vo['rec_norm'] + 8] = f('rec_norm')[l].reshape(8, 128).T
        for j in range(4):
            v[:, vo['lru_conv_w'] + 8 * j:vo['lru_conv_w'] + 8 * j + 8] = f('lru_conv_w')[l, j].reshape(8, 128).T
            v[:, vo['ssd_conv_w'] + 12 * j:vo['ssd_conv_w'] + 12 * j + 12] = f('ssd_conv_w')[l, j].reshape(12, 128).T
        for k in ('lru_conv_b', 'lru_b_r', 'lru_b_i', 'lru_lambda'):
            v[:, vo[k]:vo[k] + 8] = f(k)[l].reshape(8, 128).T
        v[:, vo['ssd_conv_b']:vo['ssd_conv_b'] + 12] = f('ssd_conv_b')[l].reshape(12, 128).T
    return v


def pack_bvecs(inputs):
    f = lambda k: np.asarray(inputs[k], dtype=np.float32)
    b = np.zeros((128, 2 * 1072), np.float32)
    for l in range(2):
        row = np.concatenate([f('ssd_dt_bias')[l], f('ssd_a_log')[l], f('ssd_d')[l], f('ssd_norm')[l]])
        b[:, 1072 * l:1072 * (l + 1)] = row[None, :]
    return b


def run(inputs, phases, n_cores=8, trace=False):
    x = np.asarray(inputs['x'], dtype=np.float32)
    nc, K = build_program(phases, NVEC)
    consts = build_consts_host()
    vecs = pack_vecs(inputs)
    shared = {"vecs": vecs, "bvecs": pack_bvecs(inputs)}
    shared.update(consts)
    for name in WEIGHT_SHAPES:
        shared[name] = np.ascontiguousarray(np.asarray(inputs[name], dtype=np.float32))
    in_maps = []
    for c in range(n_cores):
        m = dict(shared)
        m["xT"] = np.ascontiguousarray(x[c].T)
        in_maps.append(m)
    res = run_bass_kernel_spmd(nc, in_maps, core_ids=list(range(n_cores)), trace=trace)
    out = np.stack([np.ascontiguousarray(res.results[c]["yT"].T) for c in range(n_cores)], axis=0)
    if trace:
        return out, res, K
    if os.environ.get('DBG_SCR'):
        return out, res
    return out


def kernel(**inputs):
    phases = [('rec', 0), ('ffn', 0), ('att', 0), ('ffn', 1), ('rec', 1), ('ffn', 2), ('att', 1), ('ffn', 3)]
    return run(inputs, phases)
```
